# Optimizing a Trainium2 kernel written in Bass

```python
import jax, jax.numpy as jnp
from jax import lax
import numpy as np

D_MODEL = 1024
BATCH = 32
SEQ = 2048
DEPTH = 1
DEC_BATCH = 2
DEC_SEQ = 8192
PAST_LEN = 128

N_META = 16
GRID_W = 64
MIX_W = D_MODEL
W_M = MIX_W // 2
H_M = 4
DH_M = W_M // H_M
W_A = MIX_W - W_M
H_A = 8
DH_A = W_A // H_A
WIN_H = 8
WIN_W = 16
CHUNK = 64
CONV_K = 5
IN_COLS = 5 * W_M + 4 * H_M + 4 * W_A
EPS = 1e-6
NEG_BIG = -1e30

kernel_name = 'hymba_mlstm_natten_encoder'


def rms_norm(x, g):
    xf = x.astype(jnp.float32)
    y = xf * lax.rsqrt(jnp.mean(xf * xf, axis=-1, keepdims=True) + EPS)
    return (y * g.astype(jnp.float32)).astype(x.dtype)


def centred_dwconv(x, w):
    return lax.conv_general_dilated(
        x, w[:, None, :].astype(x.dtype), window_strides=(1,),
        padding=[(CONV_K // 2, CONV_K // 2)],
        dimension_numbers=('NWC', 'WIO', 'NWC'), feature_group_count=x.shape[-1])


def _mlstm_scan(q, k, v, li, lf):
    B, H, Lp, d = q.shape
    nc = Lp // CHUNK
    tri = jnp.tril(jnp.ones((CHUNK, CHUNK), dtype=bool))

    def to_chunks(a):
        return jnp.moveaxis(a.reshape(a.shape[:2] + (nc, CHUNK) + a.shape[3:]), 2, 0)

    def step(carry, inp):
        C, n, m = carry
        qc, kc, vc, lic, lfc = inp
        b = jnp.cumsum(lfc, axis=-1)
        D = jnp.where(tri, b[..., :, None] - b[..., None, :] + lic[..., None, :], -jnp.inf)
        g = b + m[..., None]
        m_j = jnp.maximum(g, jnp.max(D, axis=-1))
        S = jnp.einsum('bhjd,bhsd->bhjs', qc, kc) * jnp.exp(D - m_j[..., None])
        w_inter = jnp.exp(g - m_j)
        num = jnp.einsum('bhjs,bhsd->bhjd', S, vc) + w_inter[..., None] * jnp.einsum('bhed,bhjd->bhje', C, qc)
        den = jnp.sum(S, axis=-1) + w_inter * jnp.einsum('bhd,bhjd->bhj', n, qc)
        h = num / jnp.maximum(jnp.abs(den), jnp.exp(-m_j))[..., None]
        bL = b[..., -1]
        w_s = bL[..., None] - b + lic
        m_new = jnp.maximum(bL + m, jnp.max(w_s, axis=-1))
        decay = jnp.exp(bL + m - m_new)
        ws = jnp.exp(w_s - m_new[..., None])
        C_new = decay[..., None, None] * C + jnp.einsum('bhs,bhse,bhsd->bhed', ws, vc, kc)
        n_new = decay[..., None] * n + jnp.einsum('bhs,bhsd->bhd', ws, kc)
        return (C_new, n_new, m_new), h

    init = (jnp.zeros((B, H, d, d), jnp.float32), jnp.zeros((B, H, d), jnp.float32),
            jnp.zeros((B, H), jnp.float32))
    _, h = lax.scan(step, init, (to_chunks(q), to_chunks(k), to_chunks(v), to_chunks(li), to_chunks(lf)))
    return jnp.moveaxis(h, 0, 2).reshape(B, H, Lp, d)


def mlstm_bidir(q, k, v, gates):
    B, L, H, d = q.shape
    pad = CHUNK - N_META

    def prep(a):
        return jnp.pad(jnp.swapaxes(a.astype(jnp.float32), 1, 2), ((0, 0), (0, 0), (pad, 0), (0, 0)))

    def pad_gate(a, value):
        return jnp.pad(a, ((0, 0), (0, 0), (pad, 0)), constant_values=value)

    qh, kh, vh = prep(q), prep(k * d ** -0.5), prep(v)
    g = jnp.moveaxis(gates.reshape(B, L, 4, H), 1, -1)
    li_f = pad_gate(g[:, 0], NEG_BIG)
    lf_f = pad_gate(jax.nn.log_sigmoid(g[:, 1]), 0.0)
    li_b = pad_gate(g[:, 2], NEG_BIG)
    lf_b = pad_gate(jax.nn.log_sigmoid(g[:, 3]), 0.0)
    flip = lambda a: jnp.flip(a, axis=2)
    h_f = _mlstm_scan(qh, kh, vh, li_f, lf_f)
    h_b = flip(_mlstm_scan(flip(qh), flip(kh), flip(vh), flip(li_b), flip(lf_b)))
    return jnp.swapaxes((h_f + h_b)[:, :, pad:], 1, 2)


def neighbourhood_attention(q, k, v, rpb):
    L = q.shape[0]
    T = L - N_META
    rows = T // GRID_W
    kh = min(WIN_H, rows)
    scale = DH_A ** -0.5
    qm, km, vm = q[:N_META], k[:N_META], v[:N_META]
    qr = q[N_META:].reshape(rows, GRID_W, H_A, DH_A)
    kr = k[N_META:].reshape(rows, GRID_W, H_A, DH_A)
    vr = v[N_META:].reshape(rows, GRID_W, H_A, DH_A)
    r = jnp.arange(rows)
    rs = jnp.clip(r - kh // 2, 0, rows - kh)
    key_rows = rs[:, None] + jnp.arange(kh)[None, :]
    kg, vg = kr[key_rows], vr[key_rows]
    c = jnp.arange(GRID_W)
    cs = jnp.clip(c - WIN_W // 2, 0, GRID_W - WIN_W)
    allowed = (c[None, :] >= cs[:, None]) & (c[None, :] < cs[:, None] + WIN_W)
    row_idx = key_rows - r[:, None] + (WIN_H - 1)
    col_idx = jnp.clip(c[None, :] - c[:, None] + (WIN_W - 1), 0, 2 * WIN_W - 2)
    bias = rpb[:, row_idx[:, None, :, None], col_idx[None, :, None, :]].astype(jnp.float32)
    s_win = jnp.einsum('rqhd,rkwhd->hrqkw', qr, kg).astype(jnp.float32) * scale + bias
    s_win = jnp.where(allowed[None, None, :, None, :], s_win, -jnp.inf)
    s_meta = jnp.einsum('rqhd,mhd->hrqm', qr, km).astype(jnp.float32) * scale
    p = jax.nn.softmax(jnp.concatenate([s_win.reshape(H_A, rows, GRID_W, kh * GRID_W), s_meta], axis=-1), axis=-1)
    p = p.astype(v.dtype)
    pw = p[..., :kh * GRID_W].reshape(H_A, rows, GRID_W, kh, GRID_W)
    pm = p[..., kh * GRID_W:]
    out_r = jnp.einsum('hrqkw,rkwhd->rqhd', pw, vg) + jnp.einsum('hrqm,mhd->rqhd', pm, vm)
    p_mm = jax.nn.softmax(jnp.einsum('qhd,mhd->hqm', qm, km).astype(jnp.float32) * scale, axis=-1).astype(v.dtype)
    out_m = jnp.einsum('hqm,mhd->qhd', p_mm, vm)
    return jnp.concatenate([out_m, out_r.reshape(T, H_A, DH_A)], axis=0)


def hybrid_layer(h, norm_g, w_in, b_gate, conv_w, mlstm_norm_g, q_norm_g, k_norm_g, rpb, w_out):
    B, L, _ = h.shape
    xn = rms_norm(h, norm_g)
    proj = xn @ w_in
    cuts = np.cumsum([W_M] * 5 + [4 * H_M] + [W_A] * 4)[:-1].tolist()
    q_m, k_m, v_m, o_m, z_m, gates, q_a, k_a, v_a, z_a = jnp.split(proj, cuts, axis=-1)
    qk_m = jax.nn.silu(centred_dwconv(jnp.concatenate([q_m, k_m], axis=-1), conv_w))
    q_m, k_m = jnp.split(qk_m, 2, axis=-1)
    heads_m = lambda a: a.reshape(B, L, H_M, DH_M)
    gates = gates.astype(jnp.float32) + b_gate.astype(jnp.float32)
    h_m = mlstm_bidir(heads_m(q_m), heads_m(k_m), heads_m(v_m), gates)
    h_m = jax.nn.sigmoid(heads_m(o_m).astype(jnp.float32)) * h_m
    h_m = rms_norm(h_m, mlstm_norm_g.reshape(H_M, DH_M))
    y_m = h_m.reshape(B, L, W_M).astype(h.dtype) * jax.nn.silu(z_m)
    heads_a = lambda a: a.reshape(B, L, H_A, DH_A)
    qa = rms_norm(heads_a(q_a), q_norm_g)
    ka = rms_norm(heads_a(k_a), k_norm_g)
    va = heads_a(v_a)
    a = lax.map(lambda t: neighbourhood_attention(t[0], t[1], t[2], rpb), (qa, ka, va))
    y_a = a.reshape(B, L, W_A).astype(h.dtype) * jax.nn.silu(z_a)
    return h + jnp.concatenate([y_m, y_a], axis=-1) @ w_out


def run_trunk(x, meta_tokens, norm_g, w_in, b_gate, conv_w, mlstm_norm_g, q_norm_g, k_norm_g, rpb, w_out):
    B = x.shape[0]
    meta = jnp.broadcast_to(meta_tokens[None].astype(x.dtype), (B, N_META, D_MODEL))
    h = jnp.concatenate([meta, x], axis=1)
    for l in range(DEPTH):
        h = hybrid_layer(h, norm_g[l], w_in[l], b_gate[l], conv_w[l], mlstm_norm_g[l],
                         q_norm_g[l], k_norm_g[l], rpb[l], w_out[l])
    return h[:, N_META:]


def setup_inputs(seed: int = 0) -> dict:
    key = jax.random.key(seed)
    ks = jax.random.split(key, 12)
    nrm = jax.random.normal
    f_bias = jnp.linspace(3.0, 6.0, H_M)
    gate_offset = jnp.stack([jnp.zeros(H_M), f_bias, jnp.zeros(H_M), f_bias]).reshape(4 * H_M)
    return {
        'x_prompt': nrm(ks[0], (BATCH, SEQ, D_MODEL), jnp.float32),
        'x_sample': nrm(ks[1], (DEC_BATCH, DEC_SEQ, D_MODEL), jnp.float32),
        'meta_tokens': nrm(ks[2], (N_META, D_MODEL), jnp.float32),
        'norm_g': 1.0 + 0.02 * nrm(ks[3], (DEPTH, D_MODEL), jnp.float32),
        'w_in': nrm(ks[4], (DEPTH, D_MODEL, IN_COLS), jnp.float32) * D_MODEL ** -0.5,
        'b_gate': gate_offset + 0.1 * nrm(ks[5], (DEPTH, 4 * H_M), jnp.float32),
        'conv_w': nrm(ks[6], (DEPTH, CONV_K, 2 * W_M), jnp.float32) * CONV_K ** -0.5,
        'mlstm_norm_g': 1.0 + 0.02 * nrm(ks[7], (DEPTH, W_M), jnp.float32),
        'q_norm_g': 1.0 + 0.02 * nrm(ks[8], (DEPTH, DH_A), jnp.float32),
        'k_norm_g': 1.0 + 0.02 * nrm(ks[9], (DEPTH, DH_A), jnp.float32),
        'rpb': 0.1 * nrm(ks[10], (DEPTH, H_A, 2 * WIN_H - 1, 2 * WIN_W - 1), jnp.float32),
        'w_out': nrm(ks[11], (DEPTH, MIX_W, D_MODEL), jnp.float32) * MIX_W ** -0.5,
    }


def reference(x_prompt, x_sample, meta_tokens, norm_g, w_in, b_gate, conv_w, mlstm_norm_g,
              q_norm_g, k_norm_g, rpb, w_out):
    y_prompt = run_trunk(x_prompt, meta_tokens, norm_g, w_in, b_gate, conv_w, mlstm_norm_g,
                         q_norm_g, k_norm_g, rpb, w_out)
    y_sample = run_trunk(x_sample, meta_tokens, norm_g, w_in, b_gate, conv_w, mlstm_norm_g,
                         q_norm_g, k_norm_g, rpb, w_out)
    return (y_prompt, y_sample)
```

```python
import numpy as np
from contextlib import ExitStack
import concourse.bass as bass
import concourse.mybir as mybir
from concourse.bass_utils import run_bass_kernel_spmd

F32 = mybir.dt.float32
BF16 = mybir.dt.bfloat16
ALU = mybir.AluOpType
AF = mybir.ActivationFunctionType

NCORES = 8
NT = 80
TPU = 16
NU = 5
DM = 1024
CIN = 4624
QM, KM, VM, OM, ZM, GT, QA, KA, VA, ZA = 0, 512, 1024, 1536, 2048, 2560, 2576, 3088, 3600, 4112
EPS = 1e-6
NPRE = 4
LAG = 3
OPT_WCONV_ACT = True
OPT_TAPS = True
OPT_SILU_HOIST = True
KR = 7
QR = 4
SZR = 5

C_BG, C_CW, C_NG, C_MG, C_QG, C_KG, C_KEEP, C_NKEEP, C_KEEPB, C_HLO, C_HHI, C_CM = 0, 16, 56, 64, 68, 69, 70, 75, 80, 84, 85, 86


def _win_unl(R):
    u, r = divmod(R, 32)
    rs = 32 * u + min(max(r - 4, 0), 24)
    return set(range(rs, rs + 8))


def _win_lnk(R):
    if R // 32 == 4:
        return _win_unl(R)
    rs = min(max(R - 4, 0), 120)
    return set(range(rs, rs + 8))


def attn_plan():
    plan = []
    cms = []
    for tq in range(NT):
        tiles = {}
        for hr in (0, 1):
            R = 2 * tq + hr
            wu, wl = _win_unl(R), _win_lnk(R)
            for a in sorted({r // 2 for r in (wu | wl)}):
                rows = {2 * a, 2 * a + 1}
                ru, rl = wu & rows, wl & rows
                ent = tiles.setdefault(a, [None, None])
                if ru == rl:
                    if len(ru) == 2:
                        ent[hr] = ("s", 0, 128)
                    elif ru == {2 * a}:
                        ent[hr] = ("s", 0, 64)
                    elif ru == {2 * a + 1}:
                        ent[hr] = ("s", 64, 128)
                else:
                    ent[hr] = ("d", len(cms))
                    cms.append((R, a))
        plan.append([(a, tiles[a]) for a in sorted(tiles) if tiles[a][0] or tiles[a][1]])
    return plan, cms


PLAN, CMS = attn_plan()
for _tq, _tl in enumerate(PLAN):
    for _a, _ in _tl:
        assert _tq - 3 <= _a <= _tq + 3 and 0 <= _a < NT, (_tq, _a)
NCST = C_CM + len(CMS)


class _Op:
    __slots__ = ("eng", "fn", "deps", "dma_key", "dma_cnt", "signal", "semval", "idx", "seq", "know")


class Sched:
    def __init__(self, nc):
        self.nc = nc
        self.streams = {e: [] for e in ("pe", "act", "dve", "pool", "sp")}
        self.buf = {}
        self.psr = {}
        self.dma_counts = {}
        self.dma_ops = {}
        self.nseq = 0

    def _deps_for(self, reads, writes):
        deps = []
        for k in reads:
            st = self.buf.get(k)
            if st and st[0] is not None:
                deps.append(st[0])
        for k in writes:
            st = self.buf.get(k)
            if st:
                if st[0] is not None:
                    deps.append(st[0])
                deps.extend(st[1])
        return deps

    def _commit(self, opid, reads, writes):
        for k in reads:
            self.buf.setdefault(k, [None, []])[1].append(opid)
        for k in writes:
            self.buf[k] = [opid, []]

    def op(self, eng, fn, reads=(), writes=()):
        o = _Op()
        o.eng, o.fn, o.dma_key, o.signal = eng, fn, None, False
        o.deps = self._deps_for(reads, writes)
        o.idx = len(self.streams[eng])
        oid = ("e", eng, o.idx)
        o.seq = self.nseq
        self.nseq += 1
        for k in reads:
            if k.startswith("ps") and k not in writes:
                rd = self.psr.setdefault(k, {})
                for e2, rid in rd.items():
                    if e2 != eng:
                        o.deps.append(rid)
                rd[eng] = oid
        for k in writes:
            if k.startswith("ps"):
                self.psr[k] = {}
        self.streams[eng].append(o)
        self._commit(oid, reads, writes)
        return o

    def dma(self, key, fn, reads=(), writes=(), eng="sp"):
        o = _Op()
        o.eng, o.fn, o.dma_key, o.signal = eng, fn, key, True
        c = self.dma_counts.get(key, 0) + 1
        self.dma_counts[key] = c
        o.dma_cnt = c
        o.deps = self._deps_for(reads, writes)
        o.idx = len(self.streams[eng])
        o.seq = self.nseq
        self.nseq += 1
        self.dma_ops[(key, c)] = o
        self.streams[eng].append(o)
        self._commit(("d", key, c), reads, writes)
        return o

    def finalize(self):
        allops = sorted((o for st in self.streams.values() for o in st), key=lambda o: o.seq)
        W = {e: {} for e in self.streams}

        def op_of(src, val):
            return self.streams[src[1]][val] if src[0] == "e" else self.dma_ops[(src[1], val)]

        for o in allops:
            eng = o.eng
            w = W[eng]
            cand = {}
            for d in o.deps:
                if d[0] == "e":
                    _, de, di = d
                    if de == eng and eng == "pe":
                        continue
                    src, val = ("e", de), di
                else:
                    src, val = ("d", d[1]), d[2]
                    if val <= 0:
                        continue
                if cand.get(src, -1) < val:
                    cand[src] = val
            need = {}
            for src, val in sorted(cand.items(), key=lambda kv: -op_of(kv[0], kv[1]).seq):
                if w.get(src, -1) >= val:
                    continue
                need[src] = val
                if w.get(src, -1) < val:
                    w[src] = val
                for s2, v2 in op_of(src, val).know.items():
                    if w.get(s2, -1) < v2:
                        w[s2] = v2
            o.deps = need
            for src, val in need.items():
                if src[0] == "e":
                    self.streams[src[1]][val].signal = True
            o.know = dict(w)
            if o.dma_key is None:
                o.know[("e", eng)] = o.idx
            else:
                o.know[("d", o.dma_key)] = o.dma_cnt
        for eng, stream in self.streams.items():
            c = 0
            for o in stream:
                o.semval = None
                if o.dma_key is None and o.signal:
                    c += 1
                    o.semval = c

    def run(self, eng, e, sems, final_dma_keys=()):
        for o in self.streams[eng]:
            for src, val in o.deps.items():
                if src[0] == "e":
                    e.wait_ge(sems[src[1]], self.streams[src[1]][val].semval)
                else:
                    e.wait_ge(sems["d:" + src[1]], 16 * val)
            ins = o.fn(e)
            if o.dma_key is not None:
                ins.then_inc(sems["d:" + o.dma_key], 16)
            elif o.signal:
                ins.then_inc(sems[eng], 1)
        for k in final_dma_keys:
            e.wait_ge(sems["d:" + k], 16 * self.dma_counts[k])


def build_program(debug=False, phase=9, s1_stop=0, s2_tiles=NT, s2_start=0, cstop=9):
    nc = bass.Bass("TRN2", target_bir_lowering=False)
    xs = nc.dram_tensor("xs", [NT * 128, DM], F32, kind="ExternalInput").ap()
    xh = nc.dram_tensor("xh", [NPRE * 128, DM], F32, kind="ExternalInput").ap()
    wi = nc.dram_tensor("wi", [DM, CIN], F32, kind="ExternalInput").ap()
    wo = nc.dram_tensor("wo", [DM, DM], F32, kind="ExternalInput").ap()
    cst_d = nc.dram_tensor("cst", [128, NCST], F32, kind="ExternalInput").ap()
    fraw = nc.dram_tensor("fraw", [128, 8 * 16 * 64], F32, kind="ExternalInput").ap()
    fmsk = nc.dram_tensor("fmsk", [128, 8 * 16 * 64], F32, kind="ExternalInput").ap()
    cmat = nc.dram_tensor("cmat", [128, 6 * 128], F32, kind="ExternalInput").ap()
    y = nc.dram_tensor("y", [NT * 128, DM], F32, kind="ExternalOutput").ap()
    cbs = nc.dram_tensor("cbs", [NT, 128, 4 * 129], F32, kind="Internal").ap()

    es = ExitStack()
    with es:
        def sb(name, shape, dt):
            return es.enter_context(nc.sbuf_tensor("s_" + name, shape, dt))

        S = Sched(nc)
        w_bf = sb("w_bf", [128, 8, CIN], BF16)
        wo_bf = sb("wo_bf", [128, 8, DM], BF16)
        Ftab = sb("Ftab", [128, 8, 16, 64], BF16)
        xhT = sb("xhT", [128, 8, 320], BF16)
        cst = sb("cst", [128, NCST], F32)
        cm_f = sb("cm_f", [128, 6, 128], F32)
        ident = sb("ident", [128, 128], BF16)
        bones = sb("bones", [128, 128], BF16)
        qg8 = sb("qg8", [128, 2], F32)
        xt = sb("xt", [128, DM], F32)
        st1 = sb("st1", [128, 4], F32)
        xnb = sb("xnb", [128, DM], BF16)
        xnT = sb("xnT", [128, 2, 8, 132], BF16)
        acc = sb("acc", [128, 8, 128], F32)
        qkT = sb("qkT", [128, 2, 8, 128], BF16)
        ktok = sb("ktok", [128, 512], BF16)
        vm = sb("vm", [128, 2, 512], F32)
        so = sb("so", [128, 2, 512], BF16)
        szm = sb("szm", [128, 2, 4, 128], BF16)
        V4 = sb("V4", [128, 3, 4, 129], BF16)
        gt = sb("gt", [128, 2, 16], F32)
        g8 = sb("g8", [128, 8], F32)
        d8 = sb("d8", [128, 8], F32)
        sc = sb("sc", [128, 4, 4], F32)
        eb8 = sb("eb8", [128, 8], F32)
        dec8 = sb("dec8", [128, 8], F32)
        sqb = sb("sqb", [128, 4, 128], BF16)
        rsn = sb("rsn", [128, 512], F32)
        qnT = sb("qnT", [128, QR, 2, 4, 128], BF16)
        knT = sb("knT", [128, KR, 4, 128], BF16)
        va = sb("va", [128, KR, 8, 65], BF16)
        sza = sb("sza", [128, SZR, 4, 128], BF16)
        ymT = sb("ymT", [128, QR, 4, 128], BF16)
        Eb = sb("Eb", [128, 2, 8, 128], BF16)
        Pb = sb("Pb", [128, 1, 8, 128], BF16)
        Em = sb("Em", [16, 8, 128], BF16)
        rd8 = sb("rd8", [128, 8], F32)
        ya = sb("ya", [128, 8, 64], BF16)
        yaT = sb("yaT", [128, 4, 128], BF16)
        Sfb = sb("Sfb", [128, 2, 4, 128], BF16)
        hf = sb("hf", [128, 4, 128], F32)
        hn = sb("hn", [128, 4, 128], BF16)
        da8 = sb("da8", [128, 8], F32)
        ss4 = sb("ss4", [128, 4], F32)
        r4 = sb("r4", [128, 4], F32)
        Cf = sb("Cf", [128, 1, 4, 129], F32)
        C_bf = sb("C_bf", [128, 4, 129], BF16)
        cb = sb("cb", [128, 4, 129], F32)
        cb_bf = sb("cb_bf", [128, 4, 129], BF16)
        xr = sb("xr", [128, DM], F32)
        kmT = sb("kmT", [128, 4, 16], BF16)
        va_m = sb("va_m", [128, 8, 65], BF16)
        prem = sb("prem", [128, 4, 20], F32)
        accm = sb("accm", [128, 4, 16], F32)
        kmm = sb("kmm", [128, 4, 16], BF16)
        Vwm = sb("Vwm", [16, 4, 129], BF16)
        wsm = sb("wsm", [16, 4], F32)

        pbank = [es.enter_context(nc.psum_tensor(f"ps{i}", [128, 512], F32)) for i in range(8)]
        sems = {k: es.enter_context(nc.semaphore(k)) for k in ("pe", "act", "dve", "pool")}
        dma_keys = ["ld_x", "ld_xr", "ld_cb", "st_y", "st_cb0", "st_cb1", "ld_w0", "ld_w1", "ld_c0", "ld_c1"]
        for k in dma_keys:
            sems["d:" + k] = es.enter_context(nc.semaphore("d_" + k))

        pools = {"A": [0, 1, 2, 3, 4, 5], "L1": [0, 1], "L2": [2, 3, 4], "L3": [5], "S1g": [6, 7], "junk": [0],
                 "C2": [3, 4], "M2": [2, 5]}
        pctr = {k: 0 for k in pools}
        cur_pool = ["A"]
        rec = [None]

        def nb():
            p = cur_pool[0]
            b = pools[p][pctr[p] % len(pools[p])]
            pctr[p] += 1
            return b

        def setctx(lst, pool):
            rec[0] = lst
            cur_pool[0] = pool

        def emit_op(eng, fn, reads, writes):
            if rec[0] is None:
                S.op(eng, fn, reads, writes)
            else:
                rec[0].append(("op", eng, fn, tuple(reads), tuple(writes)))

        def emit_dma(key, fn, reads=(), writes=()):
            if rec[0] is None:
                S.dma(key, fn, reads=reads, writes=writes)
            else:
                rec[0].append(("dma", key, fn, tuple(reads), tuple(writes)))

        def zip_lists(a, b):
            out = []
            for i in range(max(len(a), len(b))):
                if i < len(a):
                    out.append(a[i])
                if i < len(b):
                    out.append(b[i])
            return out

        def merge_emit(lists):
            lists = [L for L in lists if L]
            idx = [0] * len(lists)
            while True:
                best, bf = None, 2.0
                for li, L in enumerate(lists):
                    if idx[li] < len(L):
                        f = idx[li] / len(L)
                        if f < bf:
                            best, bf = li, f
                if best is None:
                    break
                it = lists[best][idx[best]]
                idx[best] += 1
                if it[0] == "op":
                    S.op(it[1], it[2], it[3], it[4])
                else:
                    S.dma(it[1], it[2], reads=it[3], writes=it[4])

        def pk(b):
            return f"ps{b}"

        def ps_f(b):
            return pbank[b]

        def ps_bf(b):
            return pbank[b][:].bitcast(BF16)

        def mm(out, lhsT, rhs, start, stop, reads, writes, skip=False):
            if skip:
                emit_op("pe", lambda e: e.matmul(out, lhsT=lhsT, rhs=rhs, start=start, stop=stop,
                                                 skip_group_check=True), reads, writes)
            else:
                emit_op("pe", lambda e: e.matmul(out, lhsT=lhsT, rhs=rhs, start=start, stop=stop), reads, writes)

        def tr(out, in_, reads, writes):
            emit_op("pe", lambda e: e.transpose(out=out, in_=in_, identity=ident[:]), list(reads) + ["ident"], writes)

        def act(out, in_, func, reads, writes, bias=None, scale=None, accum=None):
            kw = {}
            if bias is not None:
                kw["bias"] = bias
            if scale is not None:
                kw["scale"] = scale
            if accum is not None:
                kw["accum_out"] = accum
            emit_op("act", lambda e: e.activation(out=out, in_=in_, func=func, **kw), reads, writes)

        def tt(eng, out, in0, in1, op, reads, writes):
            emit_op(eng, lambda e: e.tensor_tensor(out=out, in0=in0, in1=in1, op=op), reads, writes)

        def ts(eng, out, in0, s1, op0, reads, writes, s2=None, op1=None):
            if op1 is None:
                emit_op(eng, lambda e: e.tensor_scalar(out=out, in0=in0, scalar1=s1, scalar2=None, op0=op0), reads, writes)
            else:
                emit_op(eng, lambda e: e.tensor_scalar(out=out, in0=in0, scalar1=s1, scalar2=s2, op0=op0, op1=op1),
                        reads, writes)

        def stt(eng, out, in0, scalar, in1, op0, op1, reads, writes):
            emit_op(eng, lambda e: e.scalar_tensor_tensor(out=out, in0=in0, scalar=scalar, in1=in1, op0=op0, op1=op1),
                    reads, writes)

        def cp(eng, out, in_, reads, writes):
            emit_op(eng, lambda e: e.tensor_copy(out, in_), reads, writes)

        def ms(eng, ap, val, writes):
            emit_op(eng, lambda e: e.memset(ap, val), (), writes)

        def recip(out, in_, reads, writes):
            emit_op("dve", lambda e: e.reciprocal(out, in_), reads, writes)

        def cc(i):
            return cst[:, i:i + 1]

        emit_dma("ld_c0", lambda e: e.dma_start(out=cst[:], in_=cst_d), writes=["cst"])
        emit_dma("ld_c1", lambda e: e.dma_start(out=cm_f[:].rearrange("p a b -> p (a b)"), in_=cmat), writes=["cm_f"])
        triu, tril, Tf, Tb, negones, Tm = (cm_f[:, i, :] for i in range(6))
        tt("dve", ident[:], cm_f[:, 0, :], cm_f[:, 1, :], ALU.mult, ["cm_f"], ["ident"])
        ms("pool", bones[:], 1.0, ["bones"])
        ms("pool", bones[0:64, 64:128], 0.0, ["bones"])
        ms("pool", bones[64:128, 0:64], 0.0, ["bones"])
        ts("dve", qg8[:, 0:1], cc(C_QG), 0.125, ALU.mult, ["cst"], ["qg8"], s2=cc(C_HLO), op1=ALU.mult)
        ts("dve", qg8[:, 1:2], cc(C_QG), 0.125, ALU.mult, ["cst"], ["qg8"], s2=cc(C_HHI), op1=ALU.mult)
        ms("pool", va[:].rearrange("p a b c -> p (a b c)"), 1.0, [f"va{i}" for i in range(KR)])
        ms("pool", va_m[:].rearrange("p b c -> p (b c)"), 1.0, ["va_m"])
        ms("pool", prem[:].rearrange("p a b -> p (a b)"), 0.0, ["prem"])
        ms("pool", Cf[:].rearrange("p a b c -> p (a b c)"), 0.0, ["Cf0"])
        ms("pool", xnT[:].rearrange("p s a b -> p (s a b)"), 0.0, ["xnT0", "xnT1"])

        PIECES = [(i * 512, min(512, CIN - i * 512)) for i in range((CIN + 511) // 512)]
        k = 0
        for kc in range(8):
            for (c0, cn) in PIECES:
                h = k % 2
                k += 1
                emit_dma(f"ld_w{h}", lambda e, kc=kc, c0=c0, cn=cn, h=h: e.dma_start(
                    out=xr[:, h * 512:h * 512 + cn], in_=wi[kc * 128:(kc + 1) * 128, c0:c0 + cn]),
                    writes=[f"xr{h}"])
                if h == 0 or not OPT_WCONV_ACT:
                    ts("dve", w_bf[:, kc, c0:c0 + cn], xr[:, h * 512:h * 512 + cn], cc(C_NG + kc), ALU.mult,
                       [f"xr{h}", "cst"], ["w_bf"])
                else:
                    act(w_bf[:, kc, c0:c0 + cn], xr[:, h * 512:h * 512 + cn], AF.Copy, [f"xr{h}", "cst"], ["w_bf"],
                        scale=cc(C_NG + kc))
        for kc in range(8):
            for half in range(2):
                h = k % 2
                k += 1
                emit_dma(f"ld_w{h}", lambda e, kc=kc, half=half, h=h: e.dma_start(
                    out=xr[:, h * 512:(h + 1) * 512], in_=wo[kc * 128:(kc + 1) * 128, half * 512:(half + 1) * 512]),
                    writes=[f"xr{h}"])
                dstw, srcw = wo_bf[:, kc, half * 512:(half + 1) * 512], xr[:, h * 512:(h + 1) * 512]
                if h == 0 or not OPT_WCONV_ACT:
                    if kc < 4:
                        ts("dve", dstw, srcw, cc(C_MG + kc), ALU.mult, [f"xr{h}", "cst"], ["wo_bf"])
                    else:
                        cp("dve", dstw, srcw, [f"xr{h}"], ["wo_bf"])
                else:
                    if kc < 4:
                        act(dstw, srcw, AF.Copy, [f"xr{h}", "cst"], ["wo_bf"], scale=cc(C_MG + kc))
                    else:
                        act(dstw, srcw, AF.Copy, [f"xr{h}"], ["wo_bf"])
        Ff = Ftab[:].rearrange("p h i q -> p (h i q)")
        for j in range(16):
            emit_dma("ld_w0", lambda e, j=j: e.dma_start(out=xr[:, 0:512], in_=fraw[:, j * 512:(j + 1) * 512]), writes=["xr0"])
            emit_dma("ld_w1", lambda e, j=j: e.dma_start(out=xr[:, 512:1024], in_=fmsk[:, j * 512:(j + 1) * 512]), writes=["xr1"])
            act(xr[:, 0:512], xr[:, 0:512], AF.Exp, ["xr0"], ["xr0"])
            tt("dve", Ff[:, j * 512:(j + 1) * 512], xr[:, 0:512], xr[:, 512:1024], ALU.mult, ["xr0", "xr1"], ["Ftab"])

        def stage_X(src_ap, xs_=0, t=None, halo_dst=None):
            emit_dma("ld_x", lambda e: e.dma_start(out=xt[:], in_=src_ap), writes=["xt"])
            act(xnb[:], xt[:], AF.Square, ["xt"], ["st1a", "xnb"], scale=1.0 / 32, accum=st1[:, 0:1])
            act(st1[:, 1:2], st1[:, 0:1], AF.Ln, ["st1a"], ["st1b"], bias=EPS)
            act(st1[:, 2:3], st1[:, 1:2], AF.Exp, ["st1b"], ["st1c"], scale=-0.5)
            ts("dve", xnb[:], xt[:], st1[:, 2:3], ALU.mult, ["xt", "st1c"], ["xnb"])
            b = nb()
            for kc in range(8):
                tr(ps_bf(b)[:, kc * 128:(kc + 1) * 128], xnb[:, kc * 128:(kc + 1) * 128], ["xnb"], [pk(b)])
            if halo_dst is not None:
                cp("dve", halo_dst, ps_bf(b).rearrange("p (a b) -> p a b", a=8), [pk(b)], ["xhT"])
                return
            cp("dve", xnT[:, xs_, :, 2:130], ps_bf(b).rearrange("p (a b) -> p a b", a=8), [pk(b)], [f"xnT{xs_}"])
            if t is not None:
                cp("pool", xnT[:, xs_, :, 0:2], xhT[:, :, 4 * t:4 * t + 2], ["xhT"], [f"xnT{xs_}"])
                cp("pool", xnT[:, xs_, :, 130:132], xhT[:, :, 4 * t + 2:4 * t + 4], ["xhT"], [f"xnT{xs_}"])

        def stage_A(mode, xs_=0, t=None, slot_q=None, slot_k=None, va_dst=None, kn_dst=None, va_key="va_m",
                    ctx=None, ab=0, slot_z=0):
            def enter(name):
                if ctx is not None:
                    setctx(*ctx[name])
            full = mode == "full"
            xk = f"xnT{xs_}"
            vmv, gtv, vmk, gtk = vm[:, ab, :], gt[:, ab, :], f"vm{ab}", f"gt{ab}"
            qkTs, sos, szms = qkT[:, ab], so[:, ab, :], szm[:, ab]
            qk_key, so_key, szm_key = f"qkT{ab}", f"so{ab}", f"szm{ab}"

            def pg(g):
                return g if full else (g - 4 + 4 * ab)
            enter("main")

            def fm_group(bank, pos, col0, n0, n1):
                n = n1 - n0
                for kc in range(8):
                    mm(ps_f(bank)[:, pos * n:(pos + 1) * n], w_bf[:, kc, col0:col0 + 128], xnT[:, xs_, kc, n0:n1],
                       kc == 0, kc == 7, ["w_bf", xk], [pk(bank)])

            def tm_proj(bank, col0, ncol):
                for kc in range(8):
                    mm(ps_f(bank)[:, 0:ncol], xnT[:, xs_, kc, 2:130], w_bf[:, kc, col0:col0 + ncol],
                       kc == 0, kc == 7, ["w_bf", xk], [pk(bank)])

            groups = list(range(8)) if full else [4, 5, 6, 7]
            gi = 0
            while gi < len(groups):
                grp = groups[gi:gi + 3]
                gi += 3
                bk = nb()
                for pos, g in enumerate(grp):
                    fm_group(bk, pos, g * 128, 0, 132)
                for pos, g in enumerate(grp):
                    src = ps_f(bk)[:, pos * 132:(pos + 1) * 132]
                    act(acc[:, pg(g), :], src[:, 0:128], AF.Copy, [pk(bk), "cst"], [f"acc{pg(g)}"], scale=cc(C_CW + g * 5))
                    if full and g >= 4 and t is None:
                        act(prem[:, g - 4, 2:18], src[:, 2:18], AF.Copy, [pk(bk)], ["prem"])
                    if full and g >= 4 and t is not None and t % TPU == 0:
                        act(prem[:, g - 4, 18:20], src[:, 2:4], AF.Copy, [pk(bk)], ["prem"])
                order = ([(j, pos, g) for j in range(1, 5) for pos, g in enumerate(grp)] if OPT_TAPS else
                         [(j, pos, g) for pos, g in enumerate(grp) for j in range(1, 5)])
                for (j, pos, g) in order:
                    src = ps_f(bk)[:, pos * 132:(pos + 1) * 132]
                    stt("dve", acc[:, pg(g), :], src[:, j:j + 128], cc(C_CW + g * 5 + j), acc[:, pg(g), :],
                        ALU.mult, ALU.add, [pk(bk), "cst", f"acc{pg(g)}"], [f"acc{pg(g)}"])
            bv = nb()
            tm_proj(bv, VM, 512)
            act(vmv, ps_f(bv)[:, :], AF.Copy, [pk(bv)], [vmk])
            bg = nb()
            tm_proj(bg, GT, 16)
            tt("dve", gtv, ps_f(bg)[:, 0:16], cst[:, C_BG:C_BG + 16], ALU.add, [pk(bg), "cst"], [gtk])
            if full:
                bo = nb()
                tm_proj(bo, OM, 512)
                act(sos, ps_f(bo)[:, :], AF.Tanh, [pk(bo)], [so_key], scale=0.5)
                ts("pool", sos, sos, 0.5, ALU.mult, [so_key], [so_key], s2=0.5, op1=ALU.add)
                bz = nb()
                for g in range(4):
                    fm_group(bz, g, ZM + g * 128, 2, 130)
                act(szms.rearrange("p a b -> p (a b)"), ps_f(bz)[:, :], AF.Silu, [pk(bz)], [szm_key])
                bz = nb()
                for g in range(4):
                    fm_group(bz, g, ZA + g * 128, 2, 130)
                act(sza[:, slot_z].rearrange("p a b -> p (a b)"), ps_f(bz)[:, :], AF.Silu, [pk(bz)], [f"sza{slot_z}"])
                act(qkTs.rearrange("p a b -> p (a b)"), acc[:].rearrange("p a b -> p (a b)"), AF.Silu,
                    [f"acc{g}" for g in range(8)], [qk_key])
                enter("norm")
                bva = nb()
                tm_proj(bva, VA, 512)
                cp("dve", va_dst[:, :, 0:64], ps_f(bva)[:, :].rearrange("p (h d) -> p h d", h=8), [pk(bva)], [va_key])
                for (col, isq, dkey) in ((QA, True, f"qnT{slot_q}"), (KA, False, f"knT{slot_k}")):
                    if isq and kn_dst is not None:
                        continue
                    bq = nb()
                    for g in range(4):
                        fm_group(bq, g, col + g * 128, 2, 130)
                    act(sqb[:].rearrange("p a b -> p (a b)"), ps_f(bq)[:, :], AF.Square, [pk(bq)], ["sqb"])
                    bs = nb()
                    for g in range(4):
                        mm(ps_f(bs)[:, g * 128:(g + 1) * 128], bones[:], sqb[:, g, :], True, True,
                           ["bones", "sqb"], [pk(bs)])
                    act(rsn[:], ps_f(bs)[:, :], AF.Ln, [pk(bs)], ["rsn"], scale=1.0 / 64, bias=EPS)
                    act(rsn[:], rsn[:], AF.Exp, ["rsn"], ["rsn"], scale=-0.5)
                    if isq:
                        for par in range(2):
                            stt("dve", qnT[:, slot_q, par].rearrange("p a b -> p (a b)"), ps_f(bq)[:, :], qg8[:, par:par + 1],
                                rsn[:], ALU.mult, ALU.mult, [pk(bq), "rsn", "qg8"], [dkey])
                    else:
                        dst = kn_dst if kn_dst is not None else knT[:, slot_k]
                        stt("dve", dst.rearrange("p a b -> p (a b)"), ps_f(bq)[:, :], cc(C_KG), rsn[:], ALU.mult, ALU.mult,
                            [pk(bq), "rsn", "cst"], [dkey])
            enter("tail_k")
            if not full:
                act(qkTs[:, 4:8, :].rearrange("p a b -> p (a b)"), acc[:, 4 * ab:4 * ab + 4, :].rearrange("p a b -> p (a b)"),
                    AF.Silu, [f"acc{4 * ab + g}" for g in range(4)], [qk_key])
            bt = nb()
            for g in range(4):
                tr(ps_bf(bt)[:, g * 128:(g + 1) * 128], qkTs[:, 4 + g, :], [qk_key], [pk(bt)])
            cp("dve", ktok[:], ps_bf(bt)[:, 0:512], [pk(bt)], ["ktok"])
            enter("tail_g")
            gv = gtv.rearrange("p (d k h) -> p d k h", d=2, k=2)
            f_view = gv[:, :, 1, :]
            i_view = gv[:, :, 0, :]
            g8v = g8[:].rearrange("p (a b) -> p a b", a=2)
            act(g8v, f_view, AF.Exp, [gtk], ["g8"], scale=-1.0)
            act(g8[:], g8[:], AF.Ln, ["g8"], ["g8"], bias=1.0)
            bc = nb()
            mm(ps_f(bc)[:, 0:4], Tf, g8[:, 0:4], True, True, ["cm_f", "g8"], [pk(bc)])
            mm(ps_f(bc)[:, 4:8], Tb, g8[:, 4:8], True, True, ["cm_f", "g8"], [pk(bc)])
            mm(ps_f(bc)[:, 8:16], negones, g8[:, 0:8], True, True, ["cm_f", "g8"], [pk(bc)])
            tt("dve", d8[:].rearrange("p (a b) -> p a b", a=2), i_view, ps_f(bc)[:, 0:8].rearrange("p (a b) -> p a b", a=2),
               ALU.subtract, [gtk, pk(bc)], ["d8"])
            act(sc[:, 0:2, :].rearrange("p a b -> p (a b)"), d8[:], AF.Exp, ["d8"], ["sc"])
            act(eb8[:], ps_f(bc)[:, 0:8], AF.Exp, [pk(bc)], ["eb8"], scale=-1.0, bias=float(0.5 * np.log(128.0)))
            act(dec8[:], ps_f(bc)[:, 8:16], AF.Exp, [pk(bc)], ["dec8"])
            tt("dve", sc[:, 2:4, :].rearrange("p a b -> p (a b)"), sc[:, 0:2, :].rearrange("p a b -> p (a b)"), dec8[:],
               ALU.mult, ["sc", "dec8"], ["sc"])
            vm3 = vmv.rearrange("p (h d) -> p h d", h=4)
            if full:
                vm_b = vm3.unsqueeze(1).to_broadcast([128, 2, 4, 128])
                sc_b = sc[:, 0:2, :].unsqueeze(3).to_broadcast([128, 2, 4, 128])
                tt("dve", V4[:, 0:2, :, 0:128], vm_b, sc_b, ALU.mult, [vmk, "sc"], ["V4u"])
                cp("dve", V4[:, 0:2, :, 128], sc[:, 0:2, :], ["sc"], ["V4u"])
                tt("pool", V4[:, 2, :, 0:128], vm3, sc[:, 2, :].unsqueeze(2).to_broadcast([128, 4, 128]), ALU.mult,
                   [vmk, "sc"], ["V4w"])
                cp("pool", V4[:, 2, :, 128], sc[:, 2, :], ["sc"], ["V4w"])
            else:
                tt("dve", V4[:, 2, :, 0:128], vm3, sc[:, 3, :].unsqueeze(2).to_broadcast([128, 4, 128]), ALU.mult,
                   [vmk, "sc"], ["V4w"])
                cp("dve", V4[:, 2, :, 128], sc[:, 3, :], ["sc"], ["V4w"])

        def state_update(cur, vk, deccol0):
            nxt = cur
            for j in range(2):
                bk = nb()
                for hh in range(2):
                    h = 2 * j + hh
                    mm(ps_f(bk)[:, hh * 129:(hh + 1) * 129], ktok[:, h * 128:(h + 1) * 128], V4[:, vk, h, :],
                       True, True, ["ktok", "V4w"], [pk(bk)])
                for hh in range(2):
                    h = 2 * j + hh
                    stt("dve", Cf[:, nxt, h, :], Cf[:, cur, h, :], dec8[:, deccol0 + h:deccol0 + h + 1],
                        ps_f(bk)[:, hh * 129:(hh + 1) * 129], ALU.mult, ALU.add,
                        [f"Cf{cur}", "dec8", pk(bk)], [f"Cf{nxt}"])
            return nxt

        if phase >= 1:
            for i in range(2):
                stage_X(xh[i * 128:(i + 1) * 128, :], halo_dst=xhT[:, :, i * 128:(i + 1) * 128])
            emit_dma("ld_x", lambda e: e.dma_start(out=xt[:], in_=xh[256:384, :]), writes=["xt"])
            act(xnb[:], xt[:], AF.Square, ["xt"], ["st1a", "xnb"], scale=1.0 / 32, accum=st1[:, 0:1])
            act(st1[:, 1:2], st1[:, 0:1], AF.Ln, ["st1a"], ["st1b"], bias=EPS)
            act(st1[:, 2:3], st1[:, 1:2], AF.Exp, ["st1b"], ["st1c"], scale=-0.5)
            ts("dve", xnb[:], xt[:], st1[:, 2:3], ALU.mult, ["xt", "st1c"], ["xnb"])
            b = nb()
            for kc in range(8):
                tr(ps_bf(b)[:, kc * 128:(kc + 1) * 128], xnb[:, kc * 128:(kc + 1) * 128], ["xnb"], [pk(b)])
            cp("dve", xhT[:, :, 256:320], ps_bf(b).rearrange("p (a b) -> p a b", a=8)[:, :, 0:64], [pk(b)], ["xhT"])
            stage_X(xh[384:512, :], xs_=0, t=None)
            stage_A("full", xs_=0, t=None, slot_q=0, slot_k=0, va_dst=va_m[:], kn_dst=knT[:, 0])
            cp("dve", kmT[:], knT[:, 0, :, 0:16], ["knT0"], ["kmT"])
            bm = nb()
            mm(ps_f(bm)[:, 0:4], Tm, g8[:, 0:4], True, True, ["cm_f", "g8"], [pk(bm)])
            tt("dve", d8[0:16, 0:4], gt[0:16, 0, 0:4], ps_f(bm)[0:16, 0:4], ALU.add, ["gt0", pk(bm)], ["d8"])
            act(wsm[:], d8[0:16, 0:4], AF.Exp, ["d8"], ["wsm"])
            tt("dve", Vwm[:, :, 0:128], vm[0:16, 0, :].rearrange("p (h d) -> p h d", h=4),
               wsm[:].unsqueeze(2).to_broadcast([16, 4, 128]), ALU.mult, ["vm0", "wsm"], ["Vwm"])
            cp("dve", Vwm[:, :, 128], wsm[:], ["wsm"], ["Vwm"])

        cur = 0
        s1_tiles = list(range(NT - 1, (s1_stop - 1) if phase >= 2 else NT - 1, -1))
        proc = [t for t in s1_tiles if t != s1_stop]
        junk = []
        if proc:
            setctx(None, "A")
            stage_X(xs[proc[0] * 128:(proc[0] + 1) * 128, :], xs_=proc[0] % 2, t=proc[0])
            stage_A("bwd", xs_=proc[0] % 2, t=proc[0], ab=proc[0] % 2,
                    ctx={"main": (None, "A"), "norm": (None, "A"), "tail_k": (junk, "junk"), "tail_g": (junk, "junk")})
            if len(proc) > 1:
                setctx(None, "A")
                stage_X(xs[proc[1] * 128:(proc[1] + 1) * 128, :], xs_=proc[1] % 2, t=proc[1])
        for t in s1_tiles:
            setctx(None, "A")
            emit_dma(f"st_cb{cur}", lambda e, t=t, cur=cur: e.dma_start(
                out=cbs[t], in_=Cf[:, cur].rearrange("p a b -> p (a b)")), reads=[f"Cf{cur}"])
            if t == s1_stop:
                break
            Lk, Lg, La, Lm, Lx = [], [], [], [], []
            stage_A("bwd", xs_=t % 2, t=t, ab=t % 2,
                    ctx={"main": (junk, "junk"), "norm": (junk, "junk"), "tail_k": (Lk, "L1"), "tail_g": (Lg, "S1g")})
            setctx(La, "L1")
            if t % TPU == TPU - 1 and t < NT - 1:
                u = t // TPU
                ts("dve", dec8[:, 4:8], dec8[:, 4:8], cc(C_KEEPB + u), ALU.mult, ["dec8", "cst"], ["dec8"])
            cur = state_update(cur, 2, 4)
            if t - 1 > s1_stop:
                stage_A("bwd", xs_=(t - 1) % 2, t=t - 1, ab=(t - 1) % 2,
                        ctx={"main": (Lm, "L2"), "norm": (Lm, "L2"), "tail_k": (junk, "junk"), "tail_g": (junk, "junk")})
            if t - 2 > s1_stop:
                setctx(Lx, "L3")
                stage_X(xs[(t - 2) * 128:(t - 1) * 128, :], xs_=(t - 2) % 2, t=t - 2)
            setctx(None, "A")
            del junk[:]
            merge_emit([zip_lists(Lk, Lg) + La, Lm, Lx])

        ms("pool", Cf[:, 0].rearrange("p a b -> p (a b)"), 0.0, ["Cf0"])
        ms("pool", C_bf[:].rearrange("p a b -> p (a b)"), 0.0, ["C_bf"])
        cur = 0

        kmtok = hn[0:16].rearrange("p a b -> p (a b)")

        Vwmu = Sfb[0:16].rearrange("p a b c -> p (a b c)")[:, 0:516].rearrange("p (a b) -> p a b", a=4)

        def stage_B(t):
            nonlocal cur
            sb_ = t % 2
            qkTs, sos, szms = qkT[:, sb_], so[:, sb_, :], szm[:, sb_]
            qk_key, so_key, szm_key = f"qkT{sb_}", f"so{sb_}", f"szm{sb_}"
            if t % TPU == 0:
                u = t // TPU
                for g in range(4):
                    ts("dve", accm[:, g, :], prem[:, g, 0:16], cc(C_CW + (4 + g) * 5), ALU.mult, ["prem", "cst"], ["accm"])
                    for j in range(1, 5):
                        stt("dve", accm[:, g, :], prem[:, g, j:j + 16], cc(C_CW + (4 + g) * 5 + j), accm[:, g, :],
                            ALU.mult, ALU.add, ["prem", "cst", "accm"], ["accm"])
                act(kmm[:].rearrange("p a b -> p (a b)"), accm[:].rearrange("p a b -> p (a b)"), AF.Silu, ["accm"], ["kmm"])
                bt = nb()
                for g in range(4):
                    tr(ps_bf(bt)[0:16, g * 128:(g + 1) * 128], kmm[:, g, :], ["kmm"], [pk(bt)])
                cp("dve", kmtok, ps_bf(bt)[0:16, 0:512], [pk(bt)], ["hn"])
                ts("dve", Vwmu, Vwm[:], cst[0:16, C_NKEEP + u:C_NKEEP + u + 1], ALU.mult, ["Vwm", "cst"], ["Sf", "Sb"])
                nxt = cur
                for j in range(2):
                    bk = nb()
                    for hh in range(2):
                        h = 2 * j + hh
                        mm(ps_f(bk)[:, hh * 129:(hh + 1) * 129], kmtok[:, h * 128:(h + 1) * 128], Vwmu[:, h, :],
                           True, True, ["hn", "Sf", "Sb"], [pk(bk)])
                    for hh in range(2):
                        h = 2 * j + hh
                        stt("dve", Cf[:, nxt, h, :], Cf[:, cur, h, :], cc(C_KEEP + u),
                            ps_f(bk)[:, hh * 129:(hh + 1) * 129], ALU.mult, ALU.add,
                            [f"Cf{cur}", "cst", pk(bk)], [f"Cf{nxt}"])
                cur = nxt
                act(C_bf[:].rearrange("p a b -> p (a b)"), Cf[:, cur].rearrange("p a b -> p (a b)"), AF.Copy,
                    [f"Cf{cur}"], ["C_bf"])
            emit_dma("ld_cb", lambda e: e.dma_start(out=cb[:].rearrange("p a b -> p (a b)"), in_=cbs[t]),
                     reads=["cbsA", "cbsB"], writes=["cb"])
            if t % TPU == TPU - 1 and t < NT - 1:
                act(cb_bf[:].rearrange("p a b -> p (a b)"), cb[:].rearrange("p a b -> p (a b)"), AF.Copy,
                    ["cb", "cst"], ["cb_bf"], scale=cc(C_KEEPB + t // TPU))
            else:
                act(cb_bf[:].rearrange("p a b -> p (a b)"), cb[:].rearrange("p a b -> p (a b)"), AF.Copy, ["cb"], ["cb_bf"])
            bs = nb()
            for h in range(4):
                mm(ps_f(bs)[:, h * 128:(h + 1) * 128], qkTs[:, 4 + h, :], qkTs[:, h, :], True, True, [qk_key], [pk(bs)])
            sview = ps_f(bs)[:, :].rearrange("p (h j) -> p h j", h=4)
            tt("dve", Sfb[:, 0], sview, triu.unsqueeze(1).to_broadcast([128, 4, 128]), ALU.mult, [pk(bs), "cm_f"], ["Sf"])
            tt("dve", Sfb[:, 1], sview, tril.unsqueeze(1).to_broadcast([128, 4, 128]), ALU.mult, [pk(bs), "cm_f"], ["Sb"])
            for d in range(2):
                banks = []
                for j in range(2):
                    bk = nb()
                    banks.append(bk)
                    for hh in range(2):
                        h = 2 * j + hh
                        o = ps_f(bk)[:, hh * 129:(hh + 1) * 129]
                        mm(o, Sfb[:, d, h, :], V4[:, d, h, :], True, False, ["Sf" if d == 0 else "Sb", "V4u"], [pk(bk)])
                        mm(o, qkTs[:, h, :], (C_bf if d == 0 else cb_bf)[:, h, :], False, True,
                           [qk_key, "C_bf" if d == 0 else "cb_bf"], [pk(bk)])
                    dv = ps_f(bk)[:, 0:258].rearrange("p (a b) -> p a b", a=2)[:, :, 128]
                    act(da8[:, d * 4 + 2 * j:d * 4 + 2 * j + 2], dv, AF.Abs, [pk(bk)], [f"da8{d}"])
                dsl = da8[:, d * 4:d * 4 + 4]
                tt("dve", dsl, dsl, eb8[:, d * 4:d * 4 + 4], ALU.max, [f"da8{d}", "eb8"], [f"da8{d}"])
                recip(dsl, dsl, [f"da8{d}"], [f"da8{d}"])
                for j in range(2):
                    bk = banks[j]
                    nv = ps_f(bk)[:, 0:258].rearrange("p (a b) -> p a b", a=2)[:, :, 0:128]
                    if d == 0:
                        tt("dve", hf[:, 2 * j:2 * j + 2, :], nv,
                           da8[:, 2 * j:2 * j + 2].unsqueeze(2).to_broadcast([128, 2, 128]),
                           ALU.mult, [pk(bk), "da80"], ["hf"])
                    else:
                        for hh in range(2):
                            h = 2 * j + hh
                            stt("dve", hf[:, h, :], nv[:, hh, :], da8[:, 4 + h:5 + h], hf[:, h, :], ALU.mult, ALU.add,
                                [pk(bk), "da81", "hf"], ["hf"])
            hfv = hf[:].rearrange("p a b -> p (a b)")
            tt("dve", hfv, hfv, sos, ALU.mult, ["hf", so_key], ["hf"])
            for h in range(4):
                act(hn[:, h, :], hf[:, h, :], AF.Square, ["hf"], ["hn", "ss4"], scale=float(128.0 ** -0.5),
                    accum=ss4[:, h:h + 1])
            act(r4[:], ss4[:], AF.Ln, ["ss4"], ["r4"], bias=EPS)
            act(r4[:], r4[:], AF.Exp, ["r4"], ["r4"], scale=-0.5)
            tt("dve", hn[:], hf[:], r4[:].unsqueeze(2).to_broadcast([128, 4, 128]), ALU.mult, ["hf", "r4"], ["hn"])
            bt = nb()
            for g in range(4):
                tr(ps_bf(bt)[:, g * 128:(g + 1) * 128], hn[:, g, :], ["hn"], [pk(bt)])
            tt("dve", ymT[:, t % QR].rearrange("p a b -> p (a b)"), ps_bf(bt)[:, 0:512],
               szms.rearrange("p a b -> p (a b)"), ALU.mult, [pk(bt), szm_key], [f"ymT{t % QR}"])
            cur = state_update(cur, 2, 0)
            act(C_bf[:].rearrange("p a b -> p (a b)"), Cf[:, cur].rearrange("p a b -> p (a b)"), AF.Copy,
                [f"Cf{cur}"], ["C_bf"])

        def stage_C(tq):
            sq = tq % QR
            bms = [nb(), nb()]
            for h in range(8):
                g, par = h // 2, h % 2
                mm(ps_f(bms[h // 4])[0:16, (h % 4) * 128:(h % 4 + 1) * 128], kmT[:, g, :],
                   qnT[:, sq, par, g, :], True, True, ["kmT", f"qnT{sq}"], [pk(bms[h // 4])])
            for j in range(2):
                act(Em[:, 4 * j:4 * j + 4, :].rearrange("p a b -> p (a b)"), ps_f(bms[j])[0:16, :], AF.Exp,
                    [pk(bms[j])], ["Em"])
            bpv = [6, 7]
            started = [False, False]

            def pv(h, lhsT, rhs, reads, last):
                j = h // 4
                st = not started[j]
                started[j] = True
                mm(ps_f(bpv[j])[:, (h % 4) * 65:(h % 4 + 1) * 65], lhsT, rhs, st, last,
                   reads, [pk(bpv[j])], skip=True)

            tiles = PLAN[tq]
            for h in range(8):
                pv(h, Em[0:16, h, :], va_m[0:16, h, :], ["Em", "va_m"], False)
            for idx, (a, halves) in enumerate(tiles):
                sk = a % KR
                es_ = idx % 2
                Ebs, Pbs = Eb[:, es_], Pb[:, 0]
                ek = [f"Eb{es_}0", f"Eb{es_}1"]
                pkey = "Pb"
                for j in range(2):
                    bq = nb()
                    for hh in range(4):
                        h = 4 * j + hh
                        g, par = h // 2, h % 2
                        mm(ps_f(bq)[:, hh * 128:(hh + 1) * 128], knT[:, sk, g, :],
                           qnT[:, sq, par, g, :], True, True, [f"knT{sk}", f"qnT{sq}"], [pk(bq)])
                    act(Ebs[:, 4 * j:4 * j + 4, :].rearrange("p a b -> p (a b)"), ps_f(bq)[:, :], AF.Exp,
                        [pk(bq)], [ek[j]])
                i0 = 7 - (2 * a - 2 * tq)
                both_full = all(hv is not None and hv[0] == "s" and hv[1] == 0 and hv[2] == 128 for hv in halves)
                if both_full:
                    tt("dve", Pbs.rearrange("p h (r q) -> p h r q", r=2),
                       Ebs.rearrange("p h (r q) -> p h r q", r=2), Ftab[:, :, i0:i0 + 2, :], ALU.mult,
                       ek + ["Ftab"], [pkey])
                else:
                    for hr in (0, 1):
                        hv = halves[hr]
                        dstp = Pbs[:, :, hr * 64:(hr + 1) * 64]
                        srcp = Ebs[:, :, hr * 64:(hr + 1) * 64]
                        if hv is None:
                            ts("dve", dstp, srcp, 0.0, ALU.mult, ek, [pkey])
                        elif hv[0] == "s" and hv[1] == 0 and hv[2] == 128:
                            tt("dve", dstp, srcp, Ftab[:, :, i0 + hr, :], ALU.mult, ek + ["Ftab"], [pkey])
                        else:
                            if hv[0] == "s":
                                mcol = C_HLO if hv[1] == 0 else C_HHI
                            else:
                                mcol = C_CM + hv[1]
                            stt("dve", dstp, srcp, cc(mcol), Ftab[:, :, i0 + hr, :], ALU.mult, ALU.mult,
                                ek + ["Ftab", "cst"], [pkey])
                for h in range(8):
                    pv(h, Pbs[:, h, :], va[:, sk, h, :], [pkey, f"va{sk}"], idx == len(tiles) - 1)
            for j in range(2):
                v3 = ps_f(bpv[j])[:, 0:260].rearrange("p (h d) -> p h d", h=4)
                recip(rd8[:, 4 * j:4 * j + 4], v3[:, :, 64], [pk(bpv[j])], [f"rd8{j}"])
                tt("dve", ya[:, 4 * j:4 * j + 4, :], v3[:, :, 0:64],
                   rd8[:, 4 * j:4 * j + 4].unsqueeze(2).to_broadcast([128, 4, 64]), ALU.mult,
                   [pk(bpv[j]), f"rd8{j}"], ["ya"])
            bt = nb()
            yav = ya[:].rearrange("p h d -> p (h d)")
            for g in range(4):
                tr(ps_bf(bt)[:, g * 128:(g + 1) * 128], yav[:, g * 128:(g + 1) * 128], ["ya"], [pk(bt)])
            tt("dve", yaT[:].rearrange("p a b -> p (a b)"), ps_bf(bt)[:, 0:512],
               sza[:, tq % SZR].rearrange("p a b -> p (a b)"), ALU.mult, [pk(bt), f"sza{tq % SZR}"], ["yaT"])
            emit_dma("ld_xr", lambda e: e.dma_start(out=xr[:], in_=xs[tq * 128:(tq + 1) * 128, :]), writes=["xr0", "xr1"])
            for n in range(2):
                bo = nb()
                for kt in range(8):
                    lhsT = ymT[:, sq, kt, :] if kt < 4 else yaT[:, kt - 4, :]
                    mm(ps_f(bo)[:, :], lhsT, wo_bf[:, kt, n * 512:(n + 1) * 512], kt == 0, kt == 7,
                       [f"ymT{sq}", "yaT", "wo_bf"], [pk(bo)])
                tt("dve", xr[:, n * 512:(n + 1) * 512], ps_f(bo)[:, :], xr[:, n * 512:(n + 1) * 512], ALU.add,
                   [pk(bo), f"xr{n}"], [f"xr{n}"])
            emit_dma("st_y", lambda e: e.dma_start(out=y[tq * 128:(tq + 1) * 128, :], in_=xr[:]), reads=["xr0", "xr1"])

        S.buf["cbsA"] = [("d", "st_cb0", S.dma_counts.get("st_cb0", 0)), []]
        S.buf["cbsB"] = [("d", "st_cb1", S.dma_counts.get("st_cb1", 0)), []]

        s2_end = s2_start + s2_tiles
        if phase >= 3:
            junk2 = []

            def a_call(i, keep):
                ctx = {}
                for name in ("main", "norm", "tail_k", "tail_g"):
                    ctx[name] = keep.get(name, (junk2, "junk"))
                stage_A("full", xs_=i % 2, t=i, slot_q=i % QR, slot_k=i % KR, va_dst=va[:, i % KR],
                        va_key=f"va{i % KR}", ctx=ctx, ab=i % 2, slot_z=i % SZR)
                del junk2[:]

            setctx(None, "A")
            stage_X(xs[s2_start * 128:(s2_start + 1) * 128, :], xs_=s2_start % 2, t=s2_start)
            for i in range(s2_start, s2_end + LAG):
                L1, L2, L3 = [], [], []
                if i < s2_end:
                    Lk, Lg = [], []
                    a_call(i, {"main": (None, "A")})
                    a_call(i, {"norm": (L2, "L2"), "tail_k": (Lk, "L1"), "tail_g": (Lg, "L1")})
                    L1.extend(zip_lists(Lk, Lg))
                    setctx(L1, "L1")
                    stage_B(i)
                if i - LAG >= s2_start and phase >= 4:
                    setctx(L2, "L2")
                    stage_C(i - LAG)
                if i + 1 < s2_end:
                    setctx(L3, "L3")
                    stage_X(xs[(i + 1) * 128:(i + 2) * 128, :], xs_=(i + 1) % 2, t=i + 1)
                setctx(None, "A")
                merge_emit([L1, L2, L3])

        S.finalize()
        with nc.Block() as block:
            @block.sync
            def _(e):
                S.run("sp", e, sems, final_dma_keys=[k for k in dma_keys if k.startswith("st_") and S.dma_counts.get(k)])

            @block.tensor
            def _(e):
                S.run("pe", e, sems)

            @block.scalar
            def _(e):
                S.run("act", e, sems)

            @block.vector
            def _(e):
                S.run("dve", e, sems)

            @block.gpsimd
            def _(e):
                S.run("pool", e, sems)
    return nc


def _core_streams(x_prompt, x_sample, c):
    if c < 2:
        xs = np.concatenate([x_sample[c], x_prompt[c]], axis=0)
        seq_starts = [0, 8192]
        seq_ends = [8192, 10240]
    else:
        i0 = 2 + 5 * (c - 2)
        xs = x_prompt[i0:i0 + 5].reshape(5 * 2048, DM)
        seq_starts = [2048 * u for u in range(5)]
        seq_ends = [2048 * (u + 1) for u in range(5)]
    return np.ascontiguousarray(xs), set(seq_starts), set(seq_ends)


def _host_layout(inputs):
    x_prompt = np.asarray(inputs["x_prompt"], np.float32)
    x_sample = np.asarray(inputs["x_sample"], np.float32)
    meta = np.asarray(inputs["meta_tokens"], np.float32)
    wi = np.ascontiguousarray(np.asarray(inputs["w_in"], np.float32)[0])
    wo = np.ascontiguousarray(np.asarray(inputs["w_out"], np.float32)[0])
    norm_g = np.asarray(inputs["norm_g"], np.float32)[0]
    b_gate = np.asarray(inputs["b_gate"], np.float32)[0]
    conv_w = np.asarray(inputs["conv_w"], np.float32)[0]
    mg = np.asarray(inputs["mlstm_norm_g"], np.float32)[0]
    qg = np.asarray(inputs["q_norm_g"], np.float32)[0]
    kg = np.asarray(inputs["k_norm_g"], np.float32)[0]
    rpb = np.asarray(inputs["rpb"], np.float32)[0]

    p = np.arange(128)
    rr, kc = p // 64, p % 64
    i = np.arange(16)
    qc = np.arange(64)
    dr = (7 - i)[None, :] + rr[:, None]
    colidx = kc[:, None] - qc[None, :] + 15
    cs = np.clip(qc - 8, 0, 48)
    allowed = (kc[:, None] >= cs[None, :]) & (kc[:, None] < cs[None, :] + 16)
    valid = (dr >= -7) & (dr <= 7)
    rowi = np.clip(dr + 7, 0, 14)
    coli = np.clip(colidx, 0, 30)
    fraw = rpb[:, rowi[:, :, None], coli[:, None, :]]
    fraw = np.ascontiguousarray(np.transpose(fraw, (1, 0, 2, 3))).reshape(128, 8 * 16 * 64).astype(np.float32)
    fm = (valid[:, :, None] & allowed[:, None, :]).astype(np.float32)
    fmsk = np.ascontiguousarray(np.broadcast_to(fm[:, None], (128, 8, 16, 64))).reshape(128, 8 * 16 * 64)

    s = np.arange(128)[:, None]
    j = np.arange(128)[None, :]
    cmat = np.stack([
        (s <= j), (s >= j), -(s <= j).astype(np.float32), -(s >= j).astype(np.float32),
        -np.ones((128, 128)), -((s > j) & (s <= 15)).astype(np.float32)], axis=1).astype(np.float32)
    cmat = np.ascontiguousarray(cmat).reshape(128, 6 * 128)

    in_maps = []
    for c in range(NCORES):
        xs, starts, ends = _core_streams(x_prompt, x_sample, c)
        xh = np.zeros((NPRE * 128, DM), np.float32)
        for t in range(NT):
            t0 = t * 128
            if t0 in starts:
                xh[4 * t:4 * t + 2] = meta[14:16]
            else:
                xh[4 * t:4 * t + 2] = xs[t0 - 2:t0]
            if t0 + 128 not in ends:
                xh[4 * t + 2:4 * t + 4] = xs[t0 + 128:t0 + 130]
        xh[384:400] = meta
        linked = c < 2
        cst = np.zeros((128, NCST), np.float32)
        cst[:, C_BG:C_BG + 16] = b_gate[None, :]
        for g in range(8):
            for jj in range(5):
                cst[:, C_CW + g * 5 + jj] = conv_w[jj, g * 128:(g + 1) * 128]
        for kc_ in range(8):
            cst[:, C_NG + kc_] = norm_g[kc_ * 128:(kc_ + 1) * 128]
        for g in range(4):
            cst[:, C_MG + g] = mg[g * 128:(g + 1) * 128]
        cst[:, C_QG] = qg[p % 64]
        cst[:, C_KG] = kg[p % 64]
        keep = [0, 1, 1, 1, 0] if linked else [0, 0, 0, 0, 0]
        for u in range(5):
            cst[:, C_KEEP + u] = keep[u]
            cst[:, C_NKEEP + u] = 1 - keep[u]
        for u in range(4):
            cst[:, C_KEEPB + u] = keep[u + 1]
        cst[0:64, C_HLO] = 1.0
        cst[64:128, C_HHI] = 1.0
        for ci, (R, a) in enumerate(CMS):
            w = _win_lnk(R) if linked else _win_unl(R)
            cst[:, C_CM + ci] = np.array([(2 * a + (pp // 64)) in w for pp in range(128)], np.float32)
        in_maps.append({"xs": xs, "xh": xh, "wi": wi, "wo": wo, "cst": cst, "fraw": fraw, "fmsk": fmsk, "cmat": cmat})
    return in_maps


_NC_CACHE = {}


def kernel(**inputs):
    in_maps = _host_layout(inputs)
    if "nc" not in _NC_CACHE:
        _NC_CACHE["nc"] = build_program()
    nc = _NC_CACHE["nc"]
    res = run_bass_kernel_spmd(nc, in_maps, core_ids=list(range(NCORES)))
    ys = [np.asarray(r["y"], np.float32) for r in res.results]
    y_prompt = np.empty((32, 2048, DM), np.float32)
    y_sample = np.empty((2, 8192, DM), np.float32)
    for c in range(NCORES):
        if c < 2:
            y_sample[c] = ys[c][:8192]
            y_prompt[c] = ys[c][8192:]
        else:
            i0 = 2 + 5 * (c - 2)
            y_prompt[i0:i0 + 5] = ys[c].reshape(5, 2048, DM)
    return (y_prompt, y_sample)
```

```python
import numpy as np
from contextlib import ExitStack
import concourse.bass as bass
import concourse.mybir as mybir
from concourse.bass_utils import run_bass_kernel_spmd

F32 = mybir.dt.float32
BF16 = mybir.dt.bfloat16
ALU = mybir.AluOpType
AF = mybir.ActivationFunctionType

NCORES = 8
NT = 80
TPU = 16
NU = 5
DM = 1024
CIN = 4624
QM, KM, VM, OM, ZM, GT, QA, KA, VA, ZA = 0, 512, 1024, 1536, 2048, 2560, 2576, 3088, 3600, 4112
EPS = 1e-6
NPRE = 4
LAG = 3
OPT_WCONV_ACT = True
OPT_TAPS = True
OPT_SILU_HOIST = True
KR = 7
QR = 4
SZR = 5

C_BG, C_CW, C_NG, C_MG, C_QG, C_KG, C_KEEP, C_NKEEP, C_KEEPB, C_HLO, C_HHI, C_CM = 0, 16, 56, 64, 68, 69, 70, 75, 80, 84, 85, 86


def _win_unl(R):
    u, r = divmod(R, 32)
    rs = 32 * u + min(max(r - 4, 0), 24)
    return set(range(rs, rs + 8))


def _win_lnk(R):
    if R // 32 == 4:
        return _win_unl(R)
    rs = min(max(R - 4, 0), 120)
    return set(range(rs, rs + 8))


def attn_plan():
    plan = []
    cms = []
    for tq in range(NT):
        tiles = {}
        for hr in (0, 1):
            R = 2 * tq + hr
            wu, wl = _win_unl(R), _win_lnk(R)
            for a in sorted({r // 2 for r in (wu | wl)}):
                rows = {2 * a, 2 * a + 1}
                ru, rl = wu & rows, wl & rows
                ent = tiles.setdefault(a, [None, None])
                if ru == rl:
                    if len(ru) == 2:
                        ent[hr] = ("s", 0, 128)
                    elif ru == {2 * a}:
                        ent[hr] = ("s", 0, 64)
                    elif ru == {2 * a + 1}:
                        ent[hr] = ("s", 64, 128)
                else:
                    ent[hr] = ("d", len(cms))
                    cms.append((R, a))
        plan.append([(a, tiles[a]) for a in sorted(tiles) if tiles[a][0] or tiles[a][1]])
    return plan, cms


PLAN, CMS = attn_plan()
for _tq, _tl in enumerate(PLAN):
    for _a, _ in _tl:
        assert _tq - 3 <= _a <= _tq + 3 and 0 <= _a < NT, (_tq, _a)
NCST = C_CM + len(CMS)


class _Op:
    __slots__ = ("eng", "fn", "deps", "dma_key", "dma_cnt", "signal", "semval", "idx", "seq", "know")


class Sched:
    def __init__(self, nc):
        self.nc = nc
        self.streams = {e: [] for e in ("pe", "act", "dve", "pool", "sp")}
        self.buf = {}
        self.psr = {}
        self.dma_counts = {}
        self.dma_ops = {}
        self.nseq = 0

    def _deps_for(self, reads, writes):
        deps = []
        for k in reads:
            st = self.buf.get(k)
            if st and st[0] is not None:
                deps.append(st[0])
        for k in writes:
            st = self.buf.get(k)
            if st:
                if st[0] is not None:
                    deps.append(st[0])
                deps.extend(st[1])
        return deps

    def _commit(self, opid, reads, writes):
        for k in reads:
            self.buf.setdefault(k, [None, []])[1].append(opid)
        for k in writes:
            self.buf[k] = [opid, []]

    def op(self, eng, fn, reads=(), writes=()):
        o = _Op()
        o.eng, o.fn, o.dma_key, o.signal = eng, fn, None, False
        o.deps = self._deps_for(reads, writes)
        o.idx = len(self.streams[eng])
        oid = ("e", eng, o.idx)
        o.seq = self.nseq
        self.nseq += 1
        for k in reads:
            if k.startswith("ps") and k not in writes:
                rd = self.psr.setdefault(k, {})
                for e2, rid in rd.items():
                    if e2 != eng:
                        o.deps.append(rid)
                rd[eng] = oid
        for k in writes:
            if k.startswith("ps"):
                self.psr[k] = {}
        self.streams[eng].append(o)
        self._commit(oid, reads, writes)
        return o

    def dma(self, key, fn, reads=(), writes=(), eng="sp"):
        o = _Op()
        o.eng, o.fn, o.dma_key, o.signal = eng, fn, key, True
        c = self.dma_counts.get(key, 0) + 1
        self.dma_counts[key] = c
        o.dma_cnt = c
        o.deps = self._deps_for(reads, writes)
        o.idx = len(self.streams[eng])
        o.seq = self.nseq
        self.nseq += 1
        self.dma_ops[(key, c)] = o
        self.streams[eng].append(o)
        self._commit(("d", key, c), reads, writes)
        return o

    def finalize(self):
        allops = sorted((o for st in self.streams.values() for o in st), key=lambda o: o.seq)
        W = {e: {} for e in self.streams}

        def op_of(src, val):
            return self.streams[src[1]][val] if src[0] == "e" else self.dma_ops[(src[1], val)]

        for o in allops:
            eng = o.eng
            w = W[eng]
            cand = {}
            for d in o.deps:
                if d[0] == "e":
                    _, de, di = d
                    if de == eng and eng == "pe":
                        continue
                    src, val = ("e", de), di
                else:
                    src, val = ("d", d[1]), d[2]
                    if val <= 0:
                        continue
                if cand.get(src, -1) < val:
                    cand[src] = val
            need = {}
            for src, val in sorted(cand.items(), key=lambda kv: -op_of(kv[0], kv[1]).seq):
                if w.get(src, -1) >= val:
                    continue
                need[src] = val
                if w.get(src, -1) < val:
                    w[src] = val
                for s2, v2 in op_of(src, val).know.items():
                    if w.get(s2, -1) < v2:
                        w[s2] = v2
            o.deps = need
            for src, val in need.items():
                if src[0] == "e":
                    self.streams[src[1]][val].signal = True
            o.know = dict(w)
            if o.dma_key is None:
                o.know[("e", eng)] = o.idx
            else:
                o.know[("d", o.dma_key)] = o.dma_cnt
        for eng, stream in self.streams.items():
            c = 0
            for o in stream:
                o.semval = None
                if o.dma_key is None and o.signal:
                    c += 1
                    o.semval = c

    def run(self, eng, e, sems, final_dma_keys=()):
        for o in self.streams[eng]:
            for src, val in o.deps.items():
                if src[0] == "e":
                    e.wait_ge(sems[src[1]], self.streams[src[1]][val].semval)
                else:
                    e.wait_ge(sems["d:" + src[1]], 16 * val)
            ins = o.fn(e)
            if o.dma_key is not None:
                ins.then_inc(sems["d:" + o.dma_key], 16)
            elif o.signal:
                ins.then_inc(sems[eng], 1)
        for k in final_dma_keys:
            e.wait_ge(sems["d:" + k], 16 * self.dma_counts[k])


def build_program(debug=False, phase=9, s1_stop=0, s2_tiles=NT, s2_start=0, cstop=9):
    nc = bass.Bass("TRN2", target_bir_lowering=False)
    xs = nc.dram_tensor("xs", [NT * 128, DM], F32, kind="ExternalInput").ap()
    xh = nc.dram_tensor("xh", [NPRE * 128, DM], F32, kind="ExternalInput").ap()
    wi = nc.dram_tensor("wi", [DM, CIN], F32, kind="ExternalInput").ap()
    wo = nc.dram_tensor("wo", [DM, DM], F32, kind="ExternalInput").ap()
    cst_d = nc.dram_tensor("cst", [128, NCST], F32, kind="ExternalInput").ap()
    fraw = nc.dram_tensor("fraw", [128, 8 * 16 * 64], F32, kind="ExternalInput").ap()
    fmsk = nc.dram_tensor("fmsk", [128, 8 * 16 * 64], F32, kind="ExternalInput").ap()
    cmat = nc.dram_tensor("cmat", [128, 6 * 128], F32, kind="ExternalInput").ap()
    y = nc.dram_tensor("y", [NT * 128, DM], F32, kind="ExternalOutput").ap()
    cbs = nc.dram_tensor("cbs", [NT, 128, 4 * 129], F32, kind="Internal").ap()

    es = ExitStack()
    with es:
        def sb(name, shape, dt):
            return es.enter_context(nc.sbuf_tensor("s_" + name, shape, dt))

        S = Sched(nc)
        w_bf = sb("w_bf", [128, 8, CIN], BF16)
        wo_bf = sb("wo_bf", [128, 8, DM], BF16)
        Ftab = sb("Ftab", [128, 8, 16, 64], BF16)
        xhT = sb("xhT", [128, 8, 320], BF16)
        cst = sb("cst", [128, NCST], F32)
        cm_f = sb("cm_f", [128, 6, 128], F32)
        ident = sb("ident", [128, 128], BF16)
        bones = sb("bones", [128, 128], BF16)
        qg8 = sb("qg8", [128, 2], F32)
        xt = sb("xt", [128, DM], F32)
        st1 = sb("st1", [128, 4], F32)
        xnb = sb("xnb", [128, DM], BF16)
        xnT = sb("xnT", [128, 2, 8, 132], BF16)
        acc = sb("acc", [128, 8, 128], F32)
        qkT = sb("qkT", [128, 2, 8, 128], BF16)
        ktok = sb("ktok", [128, 2, 512], BF16)
        vm = sb("vm", [128, 2, 512], F32)
        so = sb("so", [128, 2, 512], BF16)
        szm = sb("szm", [128, 2, 4, 128], BF16)
        V4 = sb("V4", [128, 3, 4, 129], BF16)
        gt = sb("gt", [128, 2, 16], F32)
        g8 = sb("g8", [128, 2, 8], F32)
        d8 = sb("d8", [128, 2, 8], F32)
        sc = sb("sc", [128, 2, 4, 4], F32)
        eb8 = sb("eb8", [128, 2, 8], F32)
        dec8 = sb("dec8", [128, 2, 8], F32)
        sqb = sb("sqb", [128, 4, 128], BF16)
        rsn = sb("rsn", [128, 512], F32)
        qnT = sb("qnT", [128, QR, 2, 4, 128], BF16)
        knT = sb("knT", [128, KR, 4, 128], BF16)
        va = sb("va", [128, KR, 8, 65], BF16)
        sza = sb("sza", [128, SZR, 4, 128], BF16)
        ymT = sb("ymT", [128, QR, 4, 128], BF16)
        Eb = sb("Eb", [128, 2, 8, 128], BF16)
        Pb = sb("Pb", [128, 1, 8, 128], BF16)
        Em = sb("Em", [16, 8, 128], BF16)
        rd8 = sb("rd8", [128, 8], F32)
        ya = sb("ya", [128, 8, 64], BF16)
        yaT = sb("yaT", [128, 4, 128], BF16)
        Sfb = sb("Sfb", [128, 2, 4, 128], BF16)
        hf = sb("hf", [128, 4, 128], F32)
        hn = sb("hn", [128, 4, 128], BF16)
        da8 = sb("da8", [128, 8], F32)
        ss4 = sb("ss4", [128, 4], F32)
        r4 = sb("r4", [128, 4], F32)
        Cf = sb("Cf", [128, 1, 4, 129], F32)
        C_bf = sb("C_bf", [128, 4, 129], BF16)
        cb = sb("cb", [128, 4, 129], F32)
        cb_bf = sb("cb_bf", [128, 4, 129], BF16)
        xr = sb("xr", [128, DM], F32)
        kmT = sb("kmT", [128, 4, 16], BF16)
        va_m = sb("va_m", [128, 8, 65], BF16)
        prem = sb("prem", [128, 4, 20], F32)
        accm = sb("accm", [128, 4, 16], F32)
        kmm = sb("kmm", [128, 4, 16], BF16)
        Vwm = sb("Vwm", [16, 4, 129], BF16)
        wsm = sb("wsm", [16, 4], F32)

        pbank = [es.enter_context(nc.psum_tensor(f"ps{i}", [128, 512], F32)) for i in range(8)]
        sems = {k: es.enter_context(nc.semaphore(k)) for k in ("pe", "act", "dve", "pool")}
        dma_keys = ["ld_x", "ld_xr", "ld_cb", "st_y", "st_cb0", "st_cb1", "ld_w0", "ld_w1", "ld_c0", "ld_c1"]
        for k in dma_keys:
            sems["d:" + k] = es.enter_context(nc.semaphore("d_" + k))

        pools = {"A": [0, 1, 2, 3, 4, 5], "L1": [0, 1], "L2": [2, 3, 4], "L3": [5], "S1g": [6, 7], "junk": [0],
                 "C2": [3, 4], "M2": [2, 5], "S1u": [2]}
        pctr = {k: 0 for k in pools}
        cur_pool = ["A"]
        rec = [None]

        def nb():
            p = cur_pool[0]
            b = pools[p][pctr[p] % len(pools[p])]
            pctr[p] += 1
            return b

        def setctx(lst, pool):
            rec[0] = lst
            cur_pool[0] = pool

        def emit_op(eng, fn, reads, writes):
            if rec[0] is None:
                S.op(eng, fn, reads, writes)
            else:
                rec[0].append(("op", eng, fn, tuple(reads), tuple(writes)))

        def emit_dma(key, fn, reads=(), writes=()):
            if rec[0] is None:
                S.dma(key, fn, reads=reads, writes=writes)
            else:
                rec[0].append(("dma", key, fn, tuple(reads), tuple(writes)))

        def zip_lists(a, b):
            out = []
            for i in range(max(len(a), len(b))):
                if i < len(a):
                    out.append(a[i])
                if i < len(b):
                    out.append(b[i])
            return out

        def merge_emit(lists):
            lists = [L for L in lists if L]
            idx = [0] * len(lists)
            while True:
                best, bf = None, 2.0
                for li, L in enumerate(lists):
                    if idx[li] < len(L):
                        f = idx[li] / len(L)
                        if f < bf:
                            best, bf = li, f
                if best is None:
                    break
                it = lists[best][idx[best]]
                idx[best] += 1
                if it[0] == "op":
                    S.op(it[1], it[2], it[3], it[4])
                else:
                    S.dma(it[1], it[2], reads=it[3], writes=it[4])

        def pk(b):
            return f"ps{b}"

        def ps_f(b):
            return pbank[b]

        def ps_bf(b):
            return pbank[b][:].bitcast(BF16)

        def mm(out, lhsT, rhs, start, stop, reads, writes, skip=False):
            if skip:
                emit_op("pe", lambda e: e.matmul(out, lhsT=lhsT, rhs=rhs, start=start, stop=stop,
                                                 skip_group_check=True), reads, writes)
            else:
                emit_op("pe", lambda e: e.matmul(out, lhsT=lhsT, rhs=rhs, start=start, stop=stop), reads, writes)

        def tr(out, in_, reads, writes):
            emit_op("pe", lambda e: e.transpose(out=out, in_=in_, identity=ident[:]), list(reads) + ["ident"], writes)

        def act(out, in_, func, reads, writes, bias=None, scale=None, accum=None):
            kw = {}
            if bias is not None:
                kw["bias"] = bias
            if scale is not None:
                kw["scale"] = scale
            if accum is not None:
                kw["accum_out"] = accum
            emit_op("act", lambda e: e.activation(out=out, in_=in_, func=func, **kw), reads, writes)

        def tt(eng, out, in0, in1, op, reads, writes):
            emit_op(eng, lambda e: e.tensor_tensor(out=out, in0=in0, in1=in1, op=op), reads, writes)

        def ts(eng, out, in0, s1, op0, reads, writes, s2=None, op1=None):
            if op1 is None:
                emit_op(eng, lambda e: e.tensor_scalar(out=out, in0=in0, scalar1=s1, scalar2=None, op0=op0), reads, writes)
            else:
                emit_op(eng, lambda e: e.tensor_scalar(out=out, in0=in0, scalar1=s1, scalar2=s2, op0=op0, op1=op1),
                        reads, writes)

        def stt(eng, out, in0, scalar, in1, op0, op1, reads, writes):
            emit_op(eng, lambda e: e.scalar_tensor_tensor(out=out, in0=in0, scalar=scalar, in1=in1, op0=op0, op1=op1),
                    reads, writes)

        def cp(eng, out, in_, reads, writes):
            emit_op(eng, lambda e: e.tensor_copy(out, in_), reads, writes)

        def ms(eng, ap, val, writes):
            emit_op(eng, lambda e: e.memset(ap, val), (), writes)

        def recip(out, in_, reads, writes):
            emit_op("dve", lambda e: e.reciprocal(out, in_), reads, writes)

        def cc(i):
            return cst[:, i:i + 1]

        emit_dma("ld_c0", lambda e: e.dma_start(out=cst[:], in_=cst_d), writes=["cst"])
        emit_dma("ld_c1", lambda e: e.dma_start(out=cm_f[:].rearrange("p a b -> p (a b)"), in_=cmat), writes=["cm_f"])
        triu, tril, Tf, Tb, negones, Tm = (cm_f[:, i, :] for i in range(6))
        tt("dve", ident[:], cm_f[:, 0, :], cm_f[:, 1, :], ALU.mult, ["cm_f"], ["ident"])
        ms("pool", bones[:], 1.0, ["bones"])
        ms("pool", bones[0:64, 64:128], 0.0, ["bones"])
        ms("pool", bones[64:128, 0:64], 0.0, ["bones"])
        ts("dve", qg8[:, 0:1], cc(C_QG), 0.125, ALU.mult, ["cst"], ["qg8"], s2=cc(C_HLO), op1=ALU.mult)
        ts("dve", qg8[:, 1:2], cc(C_QG), 0.125, ALU.mult, ["cst"], ["qg8"], s2=cc(C_HHI), op1=ALU.mult)
        ms("pool", va[:].rearrange("p a b c -> p (a b c)"), 1.0, [f"va{i}" for i in range(KR)])
        ms("pool", va_m[:].rearrange("p b c -> p (b c)"), 1.0, ["va_m"])
        ms("pool", prem[:].rearrange("p a b -> p (a b)"), 0.0, ["prem"])
        ms("pool", Cf[:].rearrange("p a b c -> p (a b c)"), 0.0, ["Cf0"])
        ms("pool", xnT[:].rearrange("p s a b -> p (s a b)"), 0.0, ["xnT0", "xnT1"])

        PIECES = [(i * 512, min(512, CIN - i * 512)) for i in range((CIN + 511) // 512)]
        k = 0
        for kc in range(8):
            for (c0, cn) in PIECES:
                h = k % 2
                k += 1
                emit_dma(f"ld_w{h}", lambda e, kc=kc, c0=c0, cn=cn, h=h: e.dma_start(
                    out=xr[:, h * 512:h * 512 + cn], in_=wi[kc * 128:(kc + 1) * 128, c0:c0 + cn]),
                    writes=[f"xr{h}"])
                if h == 0 or not OPT_WCONV_ACT:
                    ts("dve", w_bf[:, kc, c0:c0 + cn], xr[:, h * 512:h * 512 + cn], cc(C_NG + kc), ALU.mult,
                       [f"xr{h}", "cst"], ["w_bf"])
                else:
                    act(w_bf[:, kc, c0:c0 + cn], xr[:, h * 512:h * 512 + cn], AF.Copy, [f"xr{h}", "cst"], ["w_bf"],
                        scale=cc(C_NG + kc))
        for kc in range(8):
            for half in range(2):
                h = k % 2
                k += 1
                emit_dma(f"ld_w{h}", lambda e, kc=kc, half=half, h=h: e.dma_start(
                    out=xr[:, h * 512:(h + 1) * 512], in_=wo[kc * 128:(kc + 1) * 128, half * 512:(half + 1) * 512]),
                    writes=[f"xr{h}"])
                dstw, srcw = wo_bf[:, kc, half * 512:(half + 1) * 512], xr[:, h * 512:(h + 1) * 512]
                if h == 0 or not OPT_WCONV_ACT:
                    if kc < 4:
                        ts("dve", dstw, srcw, cc(C_MG + kc), ALU.mult, [f"xr{h}", "cst"], ["wo_bf"])
                    else:
                        cp("dve", dstw, srcw, [f"xr{h}"], ["wo_bf"])
                else:
                    if kc < 4:
                        act(dstw, srcw, AF.Copy, [f"xr{h}", "cst"], ["wo_bf"], scale=cc(C_MG + kc))
                    else:
                        act(dstw, srcw, AF.Copy, [f"xr{h}"], ["wo_bf"])
        Ff = Ftab[:].rearrange("p h i q -> p (h i q)")
        for j in range(16):
            emit_dma("ld_w0", lambda e, j=j: e.dma_start(out=xr[:, 0:512], in_=fraw[:, j * 512:(j + 1) * 512]), writes=["xr0"])
            emit_dma("ld_w1", lambda e, j=j: e.dma_start(out=xr[:, 512:1024], in_=fmsk[:, j * 512:(j + 1) * 512]), writes=["xr1"])
            act(xr[:, 0:512], xr[:, 0:512], AF.Exp, ["xr0"], ["xr0"])
            tt("dve", Ff[:, j * 512:(j + 1) * 512], xr[:, 0:512], xr[:, 512:1024], ALU.mult, ["xr0", "xr1"], ["Ftab"])

        def stage_X(src_ap, xs_=0, t=None, halo_dst=None):
            emit_dma("ld_x", lambda e: e.dma_start(out=xt[:], in_=src_ap), writes=["xt"])
            act(xnb[:], xt[:], AF.Square, ["xt"], ["st1a", "xnb"], scale=1.0 / 32, accum=st1[:, 0:1])
            act(st1[:, 1:2], st1[:, 0:1], AF.Ln, ["st1a"], ["st1b"], bias=EPS)
            act(st1[:, 2:3], st1[:, 1:2], AF.Exp, ["st1b"], ["st1c"], scale=-0.5)
            ts("dve", xnb[:], xt[:], st1[:, 2:3], ALU.mult, ["xt", "st1c"], ["xnb"])
            b = nb()
            for kc in range(8):
                tr(ps_bf(b)[:, kc * 128:(kc + 1) * 128], xnb[:, kc * 128:(kc + 1) * 128], ["xnb"], [pk(b)])
            if halo_dst is not None:
                cp("dve", halo_dst, ps_bf(b).rearrange("p (a b) -> p a b", a=8), [pk(b)], ["xhT"])
                return
            cp("dve", xnT[:, xs_, :, 2:130], ps_bf(b).rearrange("p (a b) -> p a b", a=8), [pk(b)], [f"xnT{xs_}"])
            if t is not None:
                cp("pool", xnT[:, xs_, :, 0:2], xhT[:, :, 4 * t:4 * t + 2], ["xhT"], [f"xnT{xs_}"])
                cp("pool", xnT[:, xs_, :, 130:132], xhT[:, :, 4 * t + 2:4 * t + 4], ["xhT"], [f"xnT{xs_}"])

        def stage_A(mode, xs_=0, t=None, slot_q=None, slot_k=None, va_dst=None, kn_dst=None, va_key="va_m",
                    ctx=None, ab=0, slot_z=0):
            def enter(name):
                if ctx is not None:
                    setctx(*ctx[name])
            full = mode == "full"
            xk = f"xnT{xs_}"
            vmv, gtv, vmk, gtk = vm[:, ab, :], gt[:, ab, :], f"vm{ab}", f"gt{ab}"
            qkTs, sos, szms = qkT[:, ab], so[:, ab, :], szm[:, ab]
            qk_key, so_key, szm_key = f"qkT{ab}", f"so{ab}", f"szm{ab}"
            ks = 0 if full else ab
            ktoks, g8s, d8s, scs, eb8s, dec8s = ktok[:, ks, :], g8[:, ks, :], d8[:, ks, :], sc[:, ks], eb8[:, ks, :], dec8[:, ks, :]
            kk, gk, dk, sk_, ek_, dck = f"ktok{ks}", f"g8{ks}", f"d8{ks}", f"sc{ks}", f"eb8{ks}", f"dec8{ks}"
            vslot = 2 if full else 1 + ab

            def pg(g):
                return g if full else (g - 4 + 4 * ab)
            enter("main")

            def fm_group(bank, pos, col0, n0, n1):
                n = n1 - n0
                for kc in range(8):
                    mm(ps_f(bank)[:, pos * n:(pos + 1) * n], w_bf[:, kc, col0:col0 + 128], xnT[:, xs_, kc, n0:n1],
                       kc == 0, kc == 7, ["w_bf", xk], [pk(bank)])

            def tm_proj(bank, col0, ncol):
                for kc in range(8):
                    mm(ps_f(bank)[:, 0:ncol], xnT[:, xs_, kc, 2:130], w_bf[:, kc, col0:col0 + ncol],
                       kc == 0, kc == 7, ["w_bf", xk], [pk(bank)])

            groups = list(range(8)) if full else [4, 5, 6, 7]
            gi = 0
            while gi < len(groups):
                grp = groups[gi:gi + 3]
                gi += 3
                bk = nb()
                for pos, g in enumerate(grp):
                    fm_group(bk, pos, g * 128, 0, 132)
                for pos, g in enumerate(grp):
                    src = ps_f(bk)[:, pos * 132:(pos + 1) * 132]
                    act(acc[:, pg(g), :], src[:, 0:128], AF.Copy, [pk(bk), "cst"], [f"acc{pg(g)}"], scale=cc(C_CW + g * 5))
                    if full and g >= 4 and t is None:
                        act(prem[:, g - 4, 2:18], src[:, 2:18], AF.Copy, [pk(bk)], ["prem"])
                    if full and g >= 4 and t is not None and t % TPU == 0:
                        act(prem[:, g - 4, 18:20], src[:, 2:4], AF.Copy, [pk(bk)], ["prem"])
                order = ([(j, pos, g) for j in range(1, 5) for pos, g in enumerate(grp)] if OPT_TAPS else
                         [(j, pos, g) for pos, g in enumerate(grp) for j in range(1, 5)])
                for (j, pos, g) in order:
                    src = ps_f(bk)[:, pos * 132:(pos + 1) * 132]
                    stt("dve", acc[:, pg(g), :], src[:, j:j + 128], cc(C_CW + g * 5 + j), acc[:, pg(g), :],
                        ALU.mult, ALU.add, [pk(bk), "cst", f"acc{pg(g)}"], [f"acc{pg(g)}"])
            bv = nb()
            tm_proj(bv, VM, 512)
            act(vmv, ps_f(bv)[:, :], AF.Copy, [pk(bv)], [vmk])
            bg = nb()
            tm_proj(bg, GT, 16)
            tt("dve", gtv, ps_f(bg)[:, 0:16], cst[:, C_BG:C_BG + 16], ALU.add, [pk(bg), "cst"], [gtk])
            if full:
                bo = nb()
                tm_proj(bo, OM, 512)
                act(sos, ps_f(bo)[:, :], AF.Tanh, [pk(bo)], [so_key], scale=0.5)
                ts("pool", sos, sos, 0.5, ALU.mult, [so_key], [so_key], s2=0.5, op1=ALU.add)
                bz = nb()
                for g in range(4):
                    fm_group(bz, g, ZM + g * 128, 2, 130)
                act(szms.rearrange("p a b -> p (a b)"), ps_f(bz)[:, :], AF.Silu, [pk(bz)], [szm_key])
                bz = nb()
                for g in range(4):
                    fm_group(bz, g, ZA + g * 128, 2, 130)
                act(sza[:, slot_z].rearrange("p a b -> p (a b)"), ps_f(bz)[:, :], AF.Silu, [pk(bz)], [f"sza{slot_z}"])
                act(qkTs.rearrange("p a b -> p (a b)"), acc[:].rearrange("p a b -> p (a b)"), AF.Silu,
                    [f"acc{g}" for g in range(8)], [qk_key])
                enter("norm")
                bva = nb()
                tm_proj(bva, VA, 512)
                cp("dve", va_dst[:, :, 0:64], ps_f(bva)[:, :].rearrange("p (h d) -> p h d", h=8), [pk(bva)], [va_key])
                for (col, isq, dkey) in ((QA, True, f"qnT{slot_q}"), (KA, False, f"knT{slot_k}")):
                    if isq and kn_dst is not None:
                        continue
                    bq = nb()
                    for g in range(4):
                        fm_group(bq, g, col + g * 128, 2, 130)
                    act(sqb[:].rearrange("p a b -> p (a b)"), ps_f(bq)[:, :], AF.Square, [pk(bq)], ["sqb"])
                    bs = nb()
                    for g in range(4):
                        mm(ps_f(bs)[:, g * 128:(g + 1) * 128], bones[:], sqb[:, g, :], True, True,
                           ["bones", "sqb"], [pk(bs)])
                    act(rsn[:], ps_f(bs)[:, :], AF.Ln, [pk(bs)], ["rsn"], scale=1.0 / 64, bias=EPS)
                    act(rsn[:], rsn[:], AF.Exp, ["rsn"], ["rsn"], scale=-0.5)
                    if isq:
                        for par in range(2):
                            stt("dve", qnT[:, slot_q, par].rearrange("p a b -> p (a b)"), ps_f(bq)[:, :], qg8[:, par:par + 1],
                                rsn[:], ALU.mult, ALU.mult, [pk(bq), "rsn", "qg8"], [dkey])
                    else:
                        dst = kn_dst if kn_dst is not None else knT[:, slot_k]
                        stt("dve", dst.rearrange("p a b -> p (a b)"), ps_f(bq)[:, :], cc(C_KG), rsn[:], ALU.mult, ALU.mult,
                            [pk(bq), "rsn", "cst"], [dkey])
            enter("tail_k")
            if not full:
                act(qkTs[:, 4:8, :].rearrange("p a b -> p (a b)"), acc[:, 4 * ab:4 * ab + 4, :].rearrange("p a b -> p (a b)"),
                    AF.Silu, [f"acc{4 * ab + g}" for g in range(4)], [qk_key])
            bt = nb()
            for g in range(4):
                tr(ps_bf(bt)[:, g * 128:(g + 1) * 128], qkTs[:, 4 + g, :], [qk_key], [pk(bt)])
            cp("dve", ktoks, ps_bf(bt)[:, 0:512], [pk(bt)], [kk])
            enter("tail_g")
            gv = gtv.rearrange("p (d k h) -> p d k h", d=2, k=2)
            f_view = gv[:, :, 1, :]
            i_view = gv[:, :, 0, :]
            g8v = g8s.rearrange("p (a b) -> p a b", a=2)
            act(g8v, f_view, AF.Exp, [gtk], [gk], scale=-1.0)
            act(g8s, g8s, AF.Ln, [gk], [gk], bias=1.0)
            bc = nb()
            mm(ps_f(bc)[:, 0:4], Tf, g8s[:, 0:4], True, True, ["cm_f", gk], [pk(bc)])
            mm(ps_f(bc)[:, 4:8], Tb, g8s[:, 4:8], True, True, ["cm_f", gk], [pk(bc)])
            mm(ps_f(bc)[:, 8:16], negones, g8s[:, 0:8], True, True, ["cm_f", gk], [pk(bc)])
            tt("dve", d8s.rearrange("p (a b) -> p a b", a=2), i_view, ps_f(bc)[:, 0:8].rearrange("p (a b) -> p a b", a=2),
               ALU.subtract, [gtk, pk(bc)], [dk])
            act(scs[:, 0:2, :].rearrange("p a b -> p (a b)"), d8s, AF.Exp, [dk], [sk_])
            act(eb8s, ps_f(bc)[:, 0:8], AF.Exp, [pk(bc)], [ek_], scale=-1.0, bias=float(0.5 * np.log(128.0)))
            act(dec8s, ps_f(bc)[:, 8:16], AF.Exp, [pk(bc)], [dck])
            tt("dve", scs[:, 2:4, :].rearrange("p a b -> p (a b)"), scs[:, 0:2, :].rearrange("p a b -> p (a b)"), dec8s,
               ALU.mult, [sk_, dck], [sk_])
            vm3 = vmv.rearrange("p (h d) -> p h d", h=4)
            if full:
                vm_b = vm3.unsqueeze(1).to_broadcast([128, 2, 4, 128])
                sc_b = scs[:, 0:2, :].unsqueeze(3).to_broadcast([128, 2, 4, 128])
                tt("dve", V4[:, 0:2, :, 0:128], vm_b, sc_b, ALU.mult, [vmk, sk_], ["V4u"])
                cp("dve", V4[:, 0:2, :, 128], scs[:, 0:2, :], [sk_], ["V4u"])
                tt("pool", V4[:, 2, :, 0:128], vm3, scs[:, 2, :].unsqueeze(2).to_broadcast([128, 4, 128]), ALU.mult,
                   [vmk, sk_], ["V4w2"])
                cp("pool", V4[:, 2, :, 128], scs[:, 2, :], [sk_], ["V4w2"])
            else:
                tt("dve", V4[:, vslot, :, 0:128], vm3, scs[:, 3, :].unsqueeze(2).to_broadcast([128, 4, 128]), ALU.mult,
                   [vmk, sk_], [f"V4w{vslot}"])
                cp("dve", V4[:, vslot, :, 128], scs[:, 3, :], [sk_], [f"V4w{vslot}"])

        def state_update(cur, vk, deccol0, ks=0):
            nxt = cur
            for j in range(2):
                bk = nb()
                for hh in range(2):
                    h = 2 * j + hh
                    mm(ps_f(bk)[:, hh * 129:(hh + 1) * 129], ktok[:, ks, h * 128:(h + 1) * 128], V4[:, vk, h, :],
                       True, True, [f"ktok{ks}", f"V4w{vk}"], [pk(bk)])
                for hh in range(2):
                    h = 2 * j + hh
                    stt("dve", Cf[:, nxt, h, :], Cf[:, cur, h, :], dec8[:, ks, deccol0 + h:deccol0 + h + 1],
                        ps_f(bk)[:, hh * 129:(hh + 1) * 129], ALU.mult, ALU.add,
                        [f"Cf{cur}", f"dec8{ks}", pk(bk)], [f"Cf{nxt}"])
            return nxt

        if phase >= 1:
            for i in range(2):
                stage_X(xh[i * 128:(i + 1) * 128, :], halo_dst=xhT[:, :, i * 128:(i + 1) * 128])
            emit_dma("ld_x", lambda e: e.dma_start(out=xt[:], in_=xh[256:384, :]), writes=["xt"])
            act(xnb[:], xt[:], AF.Square, ["xt"], ["st1a", "xnb"], scale=1.0 / 32, accum=st1[:, 0:1])
            act(st1[:, 1:2], st1[:, 0:1], AF.Ln, ["st1a"], ["st1b"], bias=EPS)
            act(st1[:, 2:3], st1[:, 1:2], AF.Exp, ["st1b"], ["st1c"], scale=-0.5)
            ts("dve", xnb[:], xt[:], st1[:, 2:3], ALU.mult, ["xt", "st1c"], ["xnb"])
            b = nb()
            for kc in range(8):
                tr(ps_bf(b)[:, kc * 128:(kc + 1) * 128], xnb[:, kc * 128:(kc + 1) * 128], ["xnb"], [pk(b)])
            cp("dve", xhT[:, :, 256:320], ps_bf(b).rearrange("p (a b) -> p a b", a=8)[:, :, 0:64], [pk(b)], ["xhT"])
            stage_X(xh[384:512, :], xs_=0, t=None)
            stage_A("full", xs_=0, t=None, slot_q=0, slot_k=0, va_dst=va_m[:], kn_dst=knT[:, 0])
            cp("dve", kmT[:], knT[:, 0, :, 0:16], ["knT0"], ["kmT"])
            bm = nb()
            mm(ps_f(bm)[:, 0:4], Tm, g8[:, 0, 0:4], True, True, ["cm_f", "g80"], [pk(bm)])
            tt("dve", d8[0:16, 0, 0:4], gt[0:16, 0, 0:4], ps_f(bm)[0:16, 0:4], ALU.add, ["gt0", pk(bm)], ["d80"])
            act(wsm[:], d8[0:16, 0, 0:4], AF.Exp, ["d80"], ["wsm"])
            tt("dve", Vwm[:, :, 0:128], vm[0:16, 0, :].rearrange("p (h d) -> p h d", h=4),
               wsm[:].unsqueeze(2).to_broadcast([16, 4, 128]), ALU.mult, ["vm0", "wsm"], ["Vwm"])
            cp("dve", Vwm[:, :, 128], wsm[:], ["wsm"], ["Vwm"])

        cur = 0
        s1_tiles = list(range(NT - 1, (s1_stop - 1) if phase >= 2 else NT - 1, -1))
        proc = [t for t in s1_tiles if t != s1_stop]
        junk = []
        pairs = [proc[k:k + 2] for k in range(0, len(proc), 2)]

        def a_bwd(t, keep):
            ctx = {}
            for name in ("main", "norm", "tail_k", "tail_g"):
                ctx[name] = keep.get(name, (junk, "junk"))
            stage_A("bwd", xs_=t % 2, t=t, ab=t % 2, ctx=ctx)
            del junk[:]

        if pairs:
            setctx(None, "A")
            for t in pairs[0]:
                stage_X(xs[t * 128:(t + 1) * 128, :], xs_=t % 2, t=t)
        for pi, pr in enumerate(pairs):
            setctx(None, "A")
            for t in pr:
                a_bwd(t, {"main": (None, "A")})
            chains, La, Lx = [], [], []
            for j, t in enumerate(pr):
                Lk, Lg = [], []
                a_bwd(t, {"tail_k": (Lk, "L1" if j == 0 else "C2"), "tail_g": (Lg, "S1g")})
                chains.append(zip_lists(Lk, Lg))
            setctx(La, "S1u")
            for t in pr:
                emit_dma("st_cb0", lambda e, t=t: e.dma_start(
                    out=cbs[t], in_=Cf[:, 0].rearrange("p a b -> p (a b)")), reads=["Cf0"])
                if t % TPU == TPU - 1 and t < NT - 1:
                    u = t // TPU
                    ts("dve", dec8[:, t % 2, 4:8], dec8[:, t % 2, 4:8], cc(C_KEEPB + u), ALU.mult,
                       [f"dec8{t % 2}", "cst"], [f"dec8{t % 2}"])
                cur = state_update(cur, 1 + t % 2, 4, ks=t % 2)
            if pi + 1 < len(pairs):
                setctx(Lx, "L3")
                for t in pairs[pi + 1]:
                    stage_X(xs[t * 128:(t + 1) * 128, :], xs_=t % 2, t=t)
            setctx(None, "A")
            merge_emit([(zip_lists(chains[0], chains[1]) if len(chains) > 1 else chains[0]) + La, Lx])
        if s1_tiles and s1_tiles[-1] == s1_stop:
            setctx(None, "A")
            emit_dma("st_cb0", lambda e: e.dma_start(
                out=cbs[s1_stop], in_=Cf[:, 0].rearrange("p a b -> p (a b)")), reads=["Cf0"])

        ms("pool", Cf[:, 0].rearrange("p a b -> p (a b)"), 0.0, ["Cf0"])
        ms("pool", C_bf[:].rearrange("p a b -> p (a b)"), 0.0, ["C_bf"])
        cur = 0

        kmtok = hn[0:16].rearrange("p a b -> p (a b)")

        Vwmu = Sfb[0:16].rearrange("p a b c -> p (a b c)")[:, 0:516].rearrange("p (a b) -> p a b", a=4)

        def stage_B(t):
            nonlocal cur
            sb_ = t % 2
            qkTs, sos, szms = qkT[:, sb_], so[:, sb_, :], szm[:, sb_]
            qk_key, so_key, szm_key = f"qkT{sb_}", f"so{sb_}", f"szm{sb_}"
            if t % TPU == 0:
                u = t // TPU
                for g in range(4):
                    ts("dve", accm[:, g, :], prem[:, g, 0:16], cc(C_CW + (4 + g) * 5), ALU.mult, ["prem", "cst"], ["accm"])
                    for j in range(1, 5):
                        stt("dve", accm[:, g, :], prem[:, g, j:j + 16], cc(C_CW + (4 + g) * 5 + j), accm[:, g, :],
                            ALU.mult, ALU.add, ["prem", "cst", "accm"], ["accm"])
                act(kmm[:].rearrange("p a b -> p (a b)"), accm[:].rearrange("p a b -> p (a b)"), AF.Silu, ["accm"], ["kmm"])
                bt = nb()
                for g in range(4):
                    tr(ps_bf(bt)[0:16, g * 128:(g + 1) * 128], kmm[:, g, :], ["kmm"], [pk(bt)])
                cp("dve", kmtok, ps_bf(bt)[0:16, 0:512], [pk(bt)], ["hn"])
                ts("dve", Vwmu, Vwm[:], cst[0:16, C_NKEEP + u:C_NKEEP + u + 1], ALU.mult, ["Vwm", "cst"], ["Sf", "Sb"])
                nxt = cur
                for j in range(2):
                    bk = nb()
                    for hh in range(2):
                        h = 2 * j + hh
                        mm(ps_f(bk)[:, hh * 129:(hh + 1) * 129], kmtok[:, h * 128:(h + 1) * 128], Vwmu[:, h, :],
                           True, True, ["hn", "Sf", "Sb"], [pk(bk)])
                    for hh in range(2):
                        h = 2 * j + hh
                        stt("dve", Cf[:, nxt, h, :], Cf[:, cur, h, :], cc(C_KEEP + u),
                            ps_f(bk)[:, hh * 129:(hh + 1) * 129], ALU.mult, ALU.add,
                            [f"Cf{cur}", "cst", pk(bk)], [f"Cf{nxt}"])
                cur = nxt
                act(C_bf[:].rearrange("p a b -> p (a b)"), Cf[:, cur].rearrange("p a b -> p (a b)"), AF.Copy,
                    [f"Cf{cur}"], ["C_bf"])
            emit_dma("ld_cb", lambda e: e.dma_start(out=cb[:].rearrange("p a b -> p (a b)"), in_=cbs[t]),
                     reads=["cbsA", "cbsB"], writes=["cb"])
            if t % TPU == TPU - 1 and t < NT - 1:
                act(cb_bf[:].rearrange("p a b -> p (a b)"), cb[:].rearrange("p a b -> p (a b)"), AF.Copy,
                    ["cb", "cst"], ["cb_bf"], scale=cc(C_KEEPB + t // TPU))
            else:
                act(cb_bf[:].rearrange("p a b -> p (a b)"), cb[:].rearrange("p a b -> p (a b)"), AF.Copy, ["cb"], ["cb_bf"])
            bs = nb()
            for h in range(4):
                mm(ps_f(bs)[:, h * 128:(h + 1) * 128], qkTs[:, 4 + h, :], qkTs[:, h, :], True, True, [qk_key], [pk(bs)])
            sview = ps_f(bs)[:, :].rearrange("p (h j) -> p h j", h=4)
            tt("dve", Sfb[:, 0], sview, triu.unsqueeze(1).to_broadcast([128, 4, 128]), ALU.mult, [pk(bs), "cm_f"], ["Sf"])
            tt("dve", Sfb[:, 1], sview, tril.unsqueeze(1).to_broadcast([128, 4, 128]), ALU.mult, [pk(bs), "cm_f"], ["Sb"])
            for d in range(2):
                banks = []
                for j in range(2):
                    bk = nb()
                    banks.append(bk)
                    for hh in range(2):
                        h = 2 * j + hh
                        o = ps_f(bk)[:, hh * 129:(hh + 1) * 129]
                        mm(o, Sfb[:, d, h, :], V4[:, d, h, :], True, False, ["Sf" if d == 0 else "Sb", "V4u"], [pk(bk)])
                        mm(o, qkTs[:, h, :], (C_bf if d == 0 else cb_bf)[:, h, :], False, True,
                           [qk_key, "C_bf" if d == 0 else "cb_bf"], [pk(bk)])
                    dv = ps_f(bk)[:, 0:258].rearrange("p (a b) -> p a b", a=2)[:, :, 128]
                    act(da8[:, d * 4 + 2 * j:d * 4 + 2 * j + 2], dv, AF.Abs, [pk(bk)], [f"da8{d}"])
                dsl = da8[:, d * 4:d * 4 + 4]
                tt("dve", dsl, dsl, eb8[:, 0, d * 4:d * 4 + 4], ALU.max, [f"da8{d}", "eb80"], [f"da8{d}"])
                recip(dsl, dsl, [f"da8{d}"], [f"da8{d}"])
                for j in range(2):
                    bk = banks[j]
                    nv = ps_f(bk)[:, 0:258].rearrange("p (a b) -> p a b", a=2)[:, :, 0:128]
                    if d == 0:
                        tt("dve", hf[:, 2 * j:2 * j + 2, :], nv,
                           da8[:, 2 * j:2 * j + 2].unsqueeze(2).to_broadcast([128, 2, 128]),
                           ALU.mult, [pk(bk), "da80"], ["hf"])
                    else:
                        for hh in range(2):
                            h = 2 * j + hh
                            stt("dve", hf[:, h, :], nv[:, hh, :], da8[:, 4 + h:5 + h], hf[:, h, :], ALU.mult, ALU.add,
                                [pk(bk), "da81", "hf"], ["hf"])
            hfv = hf[:].rearrange("p a b -> p (a b)")
            tt("dve", hfv, hfv, sos, ALU.mult, ["hf", so_key], ["hf"])
            for h in range(4):
                act(hn[:, h, :], hf[:, h, :], AF.Square, ["hf"], ["hn", "ss4"], scale=float(128.0 ** -0.5),
                    accum=ss4[:, h:h + 1])
            act(r4[:], ss4[:], AF.Ln, ["ss4"], ["r4"], bias=EPS)
            act(r4[:], r4[:], AF.Exp, ["r4"], ["r4"], scale=-0.5)
            tt("dve", hn[:], hf[:], r4[:].unsqueeze(2).to_broadcast([128, 4, 128]), ALU.mult, ["hf", "r4"], ["hn"])
            bt = nb()
            for g in range(4):
                tr(ps_bf(bt)[:, g * 128:(g + 1) * 128], hn[:, g, :], ["hn"], [pk(bt)])
            tt("dve", ymT[:, t % QR].rearrange("p a b -> p (a b)"), ps_bf(bt)[:, 0:512],
               szms.rearrange("p a b -> p (a b)"), ALU.mult, [pk(bt), szm_key], [f"ymT{t % QR}"])
            cur = state_update(cur, 2, 0)
            act(C_bf[:].rearrange("p a b -> p (a b)"), Cf[:, cur].rearrange("p a b -> p (a b)"), AF.Copy,
                [f"Cf{cur}"], ["C_bf"])

        def stage_C(tq):
            sq = tq % QR
            bms = [nb(), nb()]
            for h in range(8):
                g, par = h // 2, h % 2
                mm(ps_f(bms[h // 4])[0:16, (h % 4) * 128:(h % 4 + 1) * 128], kmT[:, g, :],
                   qnT[:, sq, par, g, :], True, True, ["kmT", f"qnT{sq}"], [pk(bms[h // 4])])
            for j in range(2):
                act(Em[:, 4 * j:4 * j + 4, :].rearrange("p a b -> p (a b)"), ps_f(bms[j])[0:16, :], AF.Exp,
                    [pk(bms[j])], ["Em"])
            bpv = [6, 7]
            started = [False, False]

            def pv(h, lhsT, rhs, reads, last):
                j = h // 4
                st = not started[j]
                started[j] = True
                mm(ps_f(bpv[j])[:, (h % 4) * 65:(h % 4 + 1) * 65], lhsT, rhs, st, last,
                   reads, [pk(bpv[j])], skip=True)

            tiles = PLAN[tq]
            for h in range(8):
                pv(h, Em[0:16, h, :], va_m[0:16, h, :], ["Em", "va_m"], False)
            for idx, (a, halves) in enumerate(tiles):
                sk = a % KR
                es_ = idx % 2
                Ebs, Pbs = Eb[:, es_], Pb[:, 0]
                ek = [f"Eb{es_}0", f"Eb{es_}1"]
                pkey = "Pb"
                for j in range(2):
                    bq = nb()
                    for hh in range(4):
                        h = 4 * j + hh
                        g, par = h // 2, h % 2
                        mm(ps_f(bq)[:, hh * 128:(hh + 1) * 128], knT[:, sk, g, :],
                           qnT[:, sq, par, g, :], True, True, [f"knT{sk}", f"qnT{sq}"], [pk(bq)])
                    act(Ebs[:, 4 * j:4 * j + 4, :].rearrange("p a b -> p (a b)"), ps_f(bq)[:, :], AF.Exp,
                        [pk(bq)], [ek[j]])
                i0 = 7 - (2 * a - 2 * tq)
                both_full = all(hv is not None and hv[0] == "s" and hv[1] == 0 and hv[2] == 128 for hv in halves)
                if both_full:
                    tt("dve", Pbs.rearrange("p h (r q) -> p h r q", r=2),
                       Ebs.rearrange("p h (r q) -> p h r q", r=2), Ftab[:, :, i0:i0 + 2, :], ALU.mult,
                       ek + ["Ftab"], [pkey])
                else:
                    for hr in (0, 1):
                        hv = halves[hr]
                        dstp = Pbs[:, :, hr * 64:(hr + 1) * 64]
                        srcp = Ebs[:, :, hr * 64:(hr + 1) * 64]
                        if hv is None:
                            ts("dve", dstp, srcp, 0.0, ALU.mult, ek, [pkey])
                        elif hv[0] == "s" and hv[1] == 0 and hv[2] == 128:
                            tt("dve", dstp, srcp, Ftab[:, :, i0 + hr, :], ALU.mult, ek + ["Ftab"], [pkey])
                        else:
                            if hv[0] == "s":
                                mcol = C_HLO if hv[1] == 0 else C_HHI
                            else:
                                mcol = C_CM + hv[1]
                            stt("dve", dstp, srcp, cc(mcol), Ftab[:, :, i0 + hr, :], ALU.mult, ALU.mult,
                                ek + ["Ftab", "cst"], [pkey])
                for h in range(8):
                    pv(h, Pbs[:, h, :], va[:, sk, h, :], [pkey, f"va{sk}"], idx == len(tiles) - 1)
            for j in range(2):
                v3 = ps_f(bpv[j])[:, 0:260].rearrange("p (h d) -> p h d", h=4)
                recip(rd8[:, 4 * j:4 * j + 4], v3[:, :, 64], [pk(bpv[j])], [f"rd8{j}"])
                tt("dve", ya[:, 4 * j:4 * j + 4, :], v3[:, :, 0:64],
                   rd8[:, 4 * j:4 * j + 4].unsqueeze(2).to_broadcast([128, 4, 64]), ALU.mult,
                   [pk(bpv[j]), f"rd8{j}"], ["ya"])
            bt = nb()
            yav = ya[:].rearrange("p h d -> p (h d)")
            for g in range(4):
                tr(ps_bf(bt)[:, g * 128:(g + 1) * 128], yav[:, g * 128:(g + 1) * 128], ["ya"], [pk(bt)])
            tt("dve", yaT[:].rearrange("p a b -> p (a b)"), ps_bf(bt)[:, 0:512],
               sza[:, tq % SZR].rearrange("p a b -> p (a b)"), ALU.mult, [pk(bt), f"sza{tq % SZR}"], ["yaT"])
            emit_dma("ld_xr", lambda e: e.dma_start(out=xr[:], in_=xs[tq * 128:(tq + 1) * 128, :]), writes=["xr0", "xr1"])
            for n in range(2):
                bo = nb()
                for kt in range(8):
                    lhsT = ymT[:, sq, kt, :] if kt < 4 else yaT[:, kt - 4, :]
                    mm(ps_f(bo)[:, :], lhsT, wo_bf[:, kt, n * 512:(n + 1) * 512], kt == 0, kt == 7,
                       [f"ymT{sq}", "yaT", "wo_bf"], [pk(bo)])
                tt("dve", xr[:, n * 512:(n + 1) * 512], ps_f(bo)[:, :], xr[:, n * 512:(n + 1) * 512], ALU.add,
                   [pk(bo), f"xr{n}"], [f"xr{n}"])
            emit_dma("st_y", lambda e: e.dma_start(out=y[tq * 128:(tq + 1) * 128, :], in_=xr[:]), reads=["xr0", "xr1"])

        S.buf["cbsA"] = [("d", "st_cb0", S.dma_counts.get("st_cb0", 0)), []]
        S.buf["cbsB"] = [("d", "st_cb1", S.dma_counts.get("st_cb1", 0)), []]

        s2_end = s2_start + s2_tiles
        if phase >= 3:
            junk2 = []

            def a_call(i, keep):
                ctx = {}
                for name in ("main", "norm", "tail_k", "tail_g"):
                    ctx[name] = keep.get(name, (junk2, "junk"))
                stage_A("full", xs_=i % 2, t=i, slot_q=i % QR, slot_k=i % KR, va_dst=va[:, i % KR],
                        va_key=f"va{i % KR}", ctx=ctx, ab=i % 2, slot_z=i % SZR)
                del junk2[:]

            setctx(None, "A")
            stage_X(xs[s2_start * 128:(s2_start + 1) * 128, :], xs_=s2_start % 2, t=s2_start)
            for i in range(s2_start, s2_end + LAG):
                L1, L2, L3 = [], [], []
                if i < s2_end:
                    Lk, Lg = [], []
                    a_call(i, {"main": (None, "A")})
                    a_call(i, {"norm": (L2, "L2"), "tail_k": (Lk, "L1"), "tail_g": (Lg, "L1")})
                    L1.extend(zip_lists(Lk, Lg))
                    setctx(L1, "L1")
                    stage_B(i)
                if i - LAG >= s2_start and phase >= 4:
                    setctx(L2, "L2")
                    stage_C(i - LAG)
                if i + 1 < s2_end:
                    setctx(L3, "L3")
                    stage_X(xs[(i + 1) * 128:(i + 2) * 128, :], xs_=(i + 1) % 2, t=i + 1)
                setctx(None, "A")
                merge_emit([L1, L2, L3])

        S.finalize()
        with nc.Block() as block:
            @block.sync
            def _(e):
                S.run("sp", e, sems, final_dma_keys=[k for k in dma_keys if k.startswith("st_") and S.dma_counts.get(k)])

            @block.tensor
            def _(e):
                S.run("pe", e, sems)

            @block.scalar
            def _(e):
                S.run("act", e, sems)

            @block.vector
            def _(e):
                S.run("dve", e, sems)

            @block.gpsimd
            def _(e):
                S.run("pool", e, sems)
    return nc


def _core_streams(x_prompt, x_sample, c):
    if c < 2:
        xs = np.concatenate([x_sample[c], x_prompt[c]], axis=0)
        seq_starts = [0, 8192]
        seq_ends = [8192, 10240]
    else:
        i0 = 2 + 5 * (c - 2)
        xs = x_prompt[i0:i0 + 5].reshape(5 * 2048, DM)
        seq_starts = [2048 * u for u in range(5)]
        seq_ends = [2048 * (u + 1) for u in range(5)]
    return np.ascontiguousarray(xs), set(seq_starts), set(seq_ends)


def _host_layout(inputs):
    x_prompt = np.asarray(inputs["x_prompt"], np.float32)
    x_sample = np.asarray(inputs["x_sample"], np.float32)
    meta = np.asarray(inputs["meta_tokens"], np.float32)
    wi = np.ascontiguousarray(np.asarray(inputs["w_in"], np.float32)[0])
    wo = np.ascontiguousarray(np.asarray(inputs["w_out"], np.float32)[0])
    norm_g = np.asarray(inputs["norm_g"], np.float32)[0]
    b_gate = np.asarray(inputs["b_gate"], np.float32)[0]
    conv_w = np.asarray(inputs["conv_w"], np.float32)[0]
    mg = np.asarray(inputs["mlstm_norm_g"], np.float32)[0]
    qg = np.asarray(inputs["q_norm_g"], np.float32)[0]
    kg = np.asarray(inputs["k_norm_g"], np.float32)[0]
    rpb = np.asarray(inputs["rpb"], np.float32)[0]

    p = np.arange(128)
    rr, kc = p // 64, p % 64
    i = np.arange(16)
    qc = np.arange(64)
    dr = (7 - i)[None, :] + rr[:, None]
    colidx = kc[:, None] - qc[None, :] + 15
    cs = np.clip(qc - 8, 0, 48)
    allowed = (kc[:, None] >= cs[None, :]) & (kc[:, None] < cs[None, :] + 16)
    valid = (dr >= -7) & (dr <= 7)
    rowi = np.clip(dr + 7, 0, 14)
    coli = np.clip(colidx, 0, 30)
    fraw = rpb[:, rowi[:, :, None], coli[:, None, :]]
    fraw = np.ascontiguousarray(np.transpose(fraw, (1, 0, 2, 3))).reshape(128, 8 * 16 * 64).astype(np.float32)
    fm = (valid[:, :, None] & allowed[:, None, :]).astype(np.float32)
    fmsk = np.ascontiguousarray(np.broadcast_to(fm[:, None], (128, 8, 16, 64))).reshape(128, 8 * 16 * 64)

    s = np.arange(128)[:, None]
    j = np.arange(128)[None, :]
    cmat = np.stack([
        (s <= j), (s >= j), -(s <= j).astype(np.float32), -(s >= j).astype(np.float32),
        -np.ones((128, 128)), -((s > j) & (s <= 15)).astype(np.float32)], axis=1).astype(np.float32)
    cmat = np.ascontiguousarray(cmat).reshape(128, 6 * 128)

    in_maps = []
    for c in range(NCORES):
        xs, starts, ends = _core_streams(x_prompt, x_sample, c)
        xh = np.zeros((NPRE * 128, DM), np.float32)
        for t in range(NT):
            t0 = t * 128
            if t0 in starts:
                xh[4 * t:4 * t + 2] = meta[14:16]
            else:
                xh[4 * t:4 * t + 2] = xs[t0 - 2:t0]
            if t0 + 128 not in ends:
                xh[4 * t + 2:4 * t + 4] = xs[t0 + 128:t0 + 130]
        xh[384:400] = meta
        linked = c < 2
        cst = np.zeros((128, NCST), np.float32)
        cst[:, C_BG:C_BG + 16] = b_gate[None, :]
        for g in range(8):
            for jj in range(5):
                cst[:, C_CW + g * 5 + jj] = conv_w[jj, g * 128:(g + 1) * 128]
        for kc_ in range(8):
            cst[:, C_NG + kc_] = norm_g[kc_ * 128:(kc_ + 1) * 128]
        for g in range(4):
            cst[:, C_MG + g] = mg[g * 128:(g + 1) * 128]
        cst[:, C_QG] = qg[p % 64]
        cst[:, C_KG] = kg[p % 64]
        keep = [0, 1, 1, 1, 0] if linked else [0, 0, 0, 0, 0]
        for u in range(5):
            cst[:, C_KEEP + u] = keep[u]
            cst[:, C_NKEEP + u] = 1 - keep[u]
        for u in range(4):
            cst[:, C_KEEPB + u] = keep[u + 1]
        cst[0:64, C_HLO] = 1.0
        cst[64:128, C_HHI] = 1.0
        for ci, (R, a) in enumerate(CMS):
            w = _win_lnk(R) if linked else _win_unl(R)
            cst[:, C_CM + ci] = np.array([(2 * a + (pp // 64)) in w for pp in range(128)], np.float32)
        in_maps.append({"xs": xs, "xh": xh, "wi": wi, "wo": wo, "cst": cst, "fraw": fraw, "fmsk": fmsk, "cmat": cmat})
    return in_maps


_NC_CACHE = {}


def kernel(**inputs):
    in_maps = _host_layout(inputs)
    if "nc" not in _NC_CACHE:
        _NC_CACHE["nc"] = build_program()
    nc = _NC_CACHE["nc"]
    res = run_bass_kernel_spmd(nc, in_maps, core_ids=list(range(NCORES)))
    ys = [np.asarray(r["y"], np.float32) for r in res.results]
    y_prompt = np.empty((32, 2048, DM), np.float32)
    y_sample = np.empty((2, 8192, DM), np.float32)
    for c in range(NCORES):
        if c < 2:
            y_sample[c] = ys[c][:8192]
            y_prompt[c] = ys[c][8192:]
        else:
            i0 = 2 + 5 * (c - 2)
            y_prompt[i0:i0 + 5] = ys[c].reshape(5, 2048, DM)
    return (y_prompt, y_sample)
```

```python
import numpy as np
from contextlib import ExitStack
import concourse.bass as bass
import concourse.mybir as mybir
from concourse.bass_utils import run_bass_kernel_spmd

F32 = mybir.dt.float32
BF16 = mybir.dt.bfloat16
ALU = mybir.AluOpType
AF = mybir.ActivationFunctionType

NCORES = 8
NT = 80
TPU = 16
NU = 5
DM = 1024
CIN = 4624
QM, KM, VM, OM, ZM, GT, QA, KA, VA, ZA = 0, 512, 1024, 1536, 2048, 2560, 2576, 3088, 3600, 4112
EPS = 1e-6
NPRE = 4
LAG = 3
OPT_WCONV_ACT = True
OPT_TAPS = True
OPT_SILU_HOIST = True
KR = 7
QR = 4
SZR = 5

C_BG, C_CW, C_NG, C_MG, C_QG, C_KG, C_KEEP, C_NKEEP, C_KEEPB, C_HLO, C_HHI, C_CM = 0, 16, 56, 64, 68, 69, 70, 75, 80, 84, 85, 86


def _win_unl(R):
    u, r = divmod(R, 32)
    rs = 32 * u + min(max(r - 4, 0), 24)
    return set(range(rs, rs + 8))


def _win_lnk(R):
    if R // 32 == 4:
        return _win_unl(R)
    rs = min(max(R - 4, 0), 120)
    return set(range(rs, rs + 8))


def attn_plan():
    plan = []
    cms = []
    for tq in range(NT):
        tiles = {}
        for hr in (0, 1):
            R = 2 * tq + hr
            wu, wl = _win_unl(R), _win_lnk(R)
            for a in sorted({r // 2 for r in (wu | wl)}):
                rows = {2 * a, 2 * a + 1}
                ru, rl = wu & rows, wl & rows
                ent = tiles.setdefault(a, [None, None])
                if ru == rl:
                    if len(ru) == 2:
                        ent[hr] = ("s", 0, 128)
                    elif ru == {2 * a}:
                        ent[hr] = ("s", 0, 64)
                    elif ru == {2 * a + 1}:
                        ent[hr] = ("s", 64, 128)
                else:
                    ent[hr] = ("d", len(cms))
                    cms.append((R, a))
        plan.append([(a, tiles[a]) for a in sorted(tiles) if tiles[a][0] or tiles[a][1]])
    return plan, cms


PLAN, CMS = attn_plan()
for _tq, _tl in enumerate(PLAN):
    for _a, _ in _tl:
        assert _tq - 3 <= _a <= _tq + 3 and 0 <= _a < NT, (_tq, _a)
NCST = C_CM + len(CMS)


class _Op:
    __slots__ = ("eng", "fn", "deps", "dma_key", "dma_cnt", "signal", "semval", "idx", "seq", "know")


class Sched:
    def __init__(self, nc):
        self.nc = nc
        self.streams = {e: [] for e in ("pe", "act", "dve", "pool", "sp")}
        self.buf = {}
        self.psr = {}
        self.dma_counts = {}
        self.dma_ops = {}
        self.nseq = 0

    def _deps_for(self, reads, writes):
        deps = []
        for k in reads:
            st = self.buf.get(k)
            if st and st[0] is not None:
                deps.append(st[0])
        for k in writes:
            st = self.buf.get(k)
            if st:
                if st[0] is not None:
                    deps.append(st[0])
                deps.extend(st[1])
        return deps

    def _commit(self, opid, reads, writes):
        for k in reads:
            self.buf.setdefault(k, [None, []])[1].append(opid)
        for k in writes:
            self.buf[k] = [opid, []]

    def op(self, eng, fn, reads=(), writes=()):
        o = _Op()
        o.eng, o.fn, o.dma_key, o.signal = eng, fn, None, False
        o.deps = self._deps_for(reads, writes)
        o.idx = len(self.streams[eng])
        oid = ("e", eng, o.idx)
        o.seq = self.nseq
        self.nseq += 1
        for k in reads:
            if k.startswith("ps") and k not in writes:
                rd = self.psr.setdefault(k, {})
                for e2, rid in rd.items():
                    if e2 != eng:
                        o.deps.append(rid)
                rd[eng] = oid
        for k in writes:
            if k.startswith("ps"):
                self.psr[k] = {}
        self.streams[eng].append(o)
        self._commit(oid, reads, writes)
        return o

    def dma(self, key, fn, reads=(), writes=(), eng="sp"):
        o = _Op()
        o.eng, o.fn, o.dma_key, o.signal = eng, fn, key, True
        c = self.dma_counts.get(key, 0) + 1
        self.dma_counts[key] = c
        o.dma_cnt = c
        o.deps = self._deps_for(reads, writes)
        o.idx = len(self.streams[eng])
        o.seq = self.nseq
        self.nseq += 1
        self.dma_ops[(key, c)] = o
        self.streams[eng].append(o)
        self._commit(("d", key, c), reads, writes)
        return o

    def finalize(self):
        allops = sorted((o for st in self.streams.values() for o in st), key=lambda o: o.seq)
        W = {e: {} for e in self.streams}

        def op_of(src, val):
            return self.streams[src[1]][val] if src[0] == "e" else self.dma_ops[(src[1], val)]

        for o in allops:
            eng = o.eng
            w = W[eng]
            cand = {}
            for d in o.deps:
                if d[0] == "e":
                    _, de, di = d
                    if de == eng and eng == "pe":
                        continue
                    src, val = ("e", de), di
                else:
                    src, val = ("d", d[1]), d[2]
                    if val <= 0:
                        continue
                if cand.get(src, -1) < val:
                    cand[src] = val
            need = {}
            for src, val in sorted(cand.items(), key=lambda kv: -op_of(kv[0], kv[1]).seq):
                if w.get(src, -1) >= val:
                    continue
                need[src] = val
                if w.get(src, -1) < val:
                    w[src] = val
                for s2, v2 in op_of(src, val).know.items():
                    if w.get(s2, -1) < v2:
                        w[s2] = v2
            o.deps = need
            for src, val in need.items():
                if src[0] == "e":
                    self.streams[src[1]][val].signal = True
            o.know = dict(w)
            if o.dma_key is None:
                o.know[("e", eng)] = o.idx
            else:
                o.know[("d", o.dma_key)] = o.dma_cnt
        for eng, stream in self.streams.items():
            c = 0
            for o in stream:
                o.semval = None
                if o.dma_key is None and o.signal:
                    c += 1
                    o.semval = c

    def run(self, eng, e, sems, final_dma_keys=()):
        for o in self.streams[eng]:
            for src, val in o.deps.items():
                if src[0] == "e":
                    e.wait_ge(sems[src[1]], self.streams[src[1]][val].semval)
                else:
                    e.wait_ge(sems["d:" + src[1]], 16 * val)
            ins = o.fn(e)
            if o.dma_key is not None:
                ins.then_inc(sems["d:" + o.dma_key], 16)
            elif o.signal:
                ins.then_inc(sems[eng], 1)
        for k in final_dma_keys:
            e.wait_ge(sems["d:" + k], 16 * self.dma_counts[k])


def build_program(debug=False, phase=9, s1_stop=0, s2_tiles=NT, s2_start=0, cstop=9):
    nc = bass.Bass("TRN2", target_bir_lowering=False)
    xs = nc.dram_tensor("xs", [NT * 128, DM], F32, kind="ExternalInput").ap()
    xh = nc.dram_tensor("xh", [NPRE * 128, DM], F32, kind="ExternalInput").ap()
    wi = nc.dram_tensor("wi", [DM, CIN], F32, kind="ExternalInput").ap()
    wo = nc.dram_tensor("wo", [DM, DM], F32, kind="ExternalInput").ap()
    cst_d = nc.dram_tensor("cst", [128, NCST], F32, kind="ExternalInput").ap()
    fraw = nc.dram_tensor("fraw", [128, 8 * 16 * 64], F32, kind="ExternalInput").ap()
    fmsk = nc.dram_tensor("fmsk", [128, 8 * 16 * 64], F32, kind="ExternalInput").ap()
    cmat = nc.dram_tensor("cmat", [128, 6 * 128], F32, kind="ExternalInput").ap()
    y = nc.dram_tensor("y", [NT * 128, DM], F32, kind="ExternalOutput").ap()
    cbs = nc.dram_tensor("cbs", [NT, 128, 4 * 129], F32, kind="Internal").ap()

    es = ExitStack()
    with es:
        def sb(name, shape, dt):
            return es.enter_context(nc.sbuf_tensor("s_" + name, shape, dt))

        S = Sched(nc)
        w_bf = sb("w_bf", [128, 8, CIN], BF16)
        wo_bf = sb("wo_bf", [128, 8, DM], BF16)
        Ftab = sb("Ftab", [128, 8, 16, 64], BF16)
        xhT = sb("xhT", [128, 8, 320], BF16)
        cst = sb("cst", [128, NCST], F32)
        cm_f = sb("cm_f", [128, 6, 128], F32)
        ident = sb("ident", [128, 128], BF16)
        bones = sb("bones", [128, 128], BF16)
        qg8 = sb("qg8", [128, 2], F32)
        xt = sb("xt", [128, DM], F32)
        st1 = sb("st1", [128, 4], F32)
        xnb = sb("xnb", [128, DM], BF16)
        xnT = sb("xnT", [128, 2, 8, 132], BF16)
        acc = sb("acc", [128, 8, 128], F32)
        qkT = sb("qkT", [128, 2, 8, 128], BF16)
        ktok = sb("ktok", [128, 512], BF16)
        vm = sb("vm", [128, 2, 512], F32)
        so = sb("so", [128, 2, 512], BF16)
        szm = sb("szm", [128, 2, 4, 128], BF16)
        V4 = sb("V4", [128, 3, 4, 129], BF16)
        gt = sb("gt", [128, 2, 16], F32)
        g8 = sb("g8", [128, 8], F32)
        d8 = sb("d8", [128, 8], F32)
        sc = sb("sc", [128, 4, 4], F32)
        eb8 = sb("eb8", [128, 8], F32)
        dec8 = sb("dec8", [128, 8], F32)
        sqb = sb("sqb", [128, 4, 128], BF16)
        rsn = sb("rsn", [128, 512], F32)
        qnT = sb("qnT", [128, QR, 2, 4, 128], BF16)
        knT = sb("knT", [128, KR, 4, 128], BF16)
        va = sb("va", [128, KR, 8, 65], BF16)
        sza = sb("sza", [128, SZR, 4, 128], BF16)
        ymT = sb("ymT", [128, QR, 4, 128], BF16)
        Eb = sb("Eb", [128, 2, 8, 128], BF16)
        Pb = sb("Pb", [128, 1, 8, 128], BF16)
        Em = sb("Em", [16, 8, 128], BF16)
        rd8 = sb("rd8", [128, 8], F32)
        ya = sb("ya", [128, 8, 64], BF16)
        yaT = sb("yaT", [128, 4, 128], BF16)
        Sfb = sb("Sfb", [128, 2, 4, 128], BF16)
        hf = sb("hf", [128, 4, 128], F32)
        hn = sb("hn", [128, 4, 128], BF16)
        da8 = sb("da8", [128, 8], F32)
        ss4 = sb("ss4", [128, 4], F32)
        r4 = sb("r4", [128, 4], F32)
        Cf = sb("Cf", [128, 1, 4, 129], F32)
        C_bf = sb("C_bf", [128, 4, 129], BF16)
        cb = sb("cb", [128, 4, 129], F32)
        cb_bf = sb("cb_bf", [128, 4, 129], BF16)
        xr = sb("xr", [128, DM], F32)
        kmT = sb("kmT", [128, 4, 16], BF16)
        va_m = sb("va_m", [128, 8, 65], BF16)
        prem = sb("prem", [128, 4, 20], F32)
        accm = sb("accm", [128, 4, 16], F32)
        kmm = sb("kmm", [128, 4, 16], BF16)
        Vwm = sb("Vwm", [16, 4, 129], BF16)
        wsm = sb("wsm", [16, 4], F32)

        pbank = [es.enter_context(nc.psum_tensor(f"ps{i}", [128, 512], F32)) for i in range(8)]
        sems = {k: es.enter_context(nc.semaphore(k)) for k in ("pe", "act", "dve", "pool")}
        dma_keys = ["ld_x", "ld_xr", "ld_cb", "st_y", "st_cb0", "st_cb1", "ld_w0", "ld_w1", "ld_w2", "ld_w3", "ld_c0", "ld_c1"]
        for k in dma_keys:
            sems["d:" + k] = es.enter_context(nc.semaphore("d_" + k))

        pools = {"A": [0, 1, 2, 3, 4, 5], "L1": [0, 1], "L2": [2, 3, 4], "L3": [5], "S1g": [6, 7], "junk": [0],
                 "C2": [3, 4], "M2": [2, 5]}
        pctr = {k: 0 for k in pools}
        cur_pool = ["A"]
        rec = [None]

        def nb():
            p = cur_pool[0]
            b = pools[p][pctr[p] % len(pools[p])]
            pctr[p] += 1
            return b

        def setctx(lst, pool):
            rec[0] = lst
            cur_pool[0] = pool

        def emit_op(eng, fn, reads, writes):
            if rec[0] is None:
                S.op(eng, fn, reads, writes)
            else:
                rec[0].append(("op", eng, fn, tuple(reads), tuple(writes)))

        def emit_dma(key, fn, reads=(), writes=()):
            if rec[0] is None:
                S.dma(key, fn, reads=reads, writes=writes)
            else:
                rec[0].append(("dma", key, fn, tuple(reads), tuple(writes)))

        def zip_lists(a, b):
            out = []
            for i in range(max(len(a), len(b))):
                if i < len(a):
                    out.append(a[i])
                if i < len(b):
                    out.append(b[i])
            return out

        def merge_emit(lists):
            lists = [L for L in lists if L]
            idx = [0] * len(lists)
            while True:
                best, bf = None, 2.0
                for li, L in enumerate(lists):
                    if idx[li] < len(L):
                        f = idx[li] / len(L)
                        if f < bf:
                            best, bf = li, f
                if best is None:
                    break
                it = lists[best][idx[best]]
                idx[best] += 1
                if it[0] == "op":
                    S.op(it[1], it[2], it[3], it[4])
                else:
                    S.dma(it[1], it[2], reads=it[3], writes=it[4])

        def pk(b):
            return f"ps{b}"

        def ps_f(b):
            return pbank[b]

        def ps_bf(b):
            return pbank[b][:].bitcast(BF16)

        def mm(out, lhsT, rhs, start, stop, reads, writes, skip=False):
            if skip:
                emit_op("pe", lambda e: e.matmul(out, lhsT=lhsT, rhs=rhs, start=start, stop=stop,
                                                 skip_group_check=True), reads, writes)
            else:
                emit_op("pe", lambda e: e.matmul(out, lhsT=lhsT, rhs=rhs, start=start, stop=stop), reads, writes)

        def tr(out, in_, reads, writes):
            emit_op("pe", lambda e: e.transpose(out=out, in_=in_, identity=ident[:]), list(reads) + ["ident"], writes)

        def act(out, in_, func, reads, writes, bias=None, scale=None, accum=None):
            kw = {}
            if bias is not None:
                kw["bias"] = bias
            if scale is not None:
                kw["scale"] = scale
            if accum is not None:
                kw["accum_out"] = accum
            emit_op("act", lambda e: e.activation(out=out, in_=in_, func=func, **kw), reads, writes)

        def tt(eng, out, in0, in1, op, reads, writes):
            emit_op(eng, lambda e: e.tensor_tensor(out=out, in0=in0, in1=in1, op=op), reads, writes)

        def ts(eng, out, in0, s1, op0, reads, writes, s2=None, op1=None):
            if op1 is None:
                emit_op(eng, lambda e: e.tensor_scalar(out=out, in0=in0, scalar1=s1, scalar2=None, op0=op0), reads, writes)
            else:
                emit_op(eng, lambda e: e.tensor_scalar(out=out, in0=in0, scalar1=s1, scalar2=s2, op0=op0, op1=op1),
                        reads, writes)

        def stt(eng, out, in0, scalar, in1, op0, op1, reads, writes):
            emit_op(eng, lambda e: e.scalar_tensor_tensor(out=out, in0=in0, scalar=scalar, in1=in1, op0=op0, op1=op1),
                    reads, writes)

        def cp(eng, out, in_, reads, writes):
            emit_op(eng, lambda e: e.tensor_copy(out, in_), reads, writes)

        def ms(eng, ap, val, writes):
            emit_op(eng, lambda e: e.memset(ap, val), (), writes)

        def recip(out, in_, reads, writes):
            emit_op("dve", lambda e: e.reciprocal(out, in_), reads, writes)

        def cc(i):
            return cst[:, i:i + 1]

        emit_dma("ld_c0", lambda e: e.dma_start(out=cst[:], in_=cst_d), writes=["cst"])
        emit_dma("ld_c1", lambda e: e.dma_start(out=cm_f[:].rearrange("p a b -> p (a b)"), in_=cmat), writes=["cm_f"])
        triu, tril, Tf, Tb, negones, Tm = (cm_f[:, i, :] for i in range(6))
        tt("dve", ident[:], cm_f[:, 0, :], cm_f[:, 1, :], ALU.mult, ["cm_f"], ["ident"])
        ms("pool", bones[:], 1.0, ["bones"])
        ms("pool", bones[0:64, 64:128], 0.0, ["bones"])
        ms("pool", bones[64:128, 0:64], 0.0, ["bones"])
        ts("dve", qg8[:, 0:1], cc(C_QG), 0.125, ALU.mult, ["cst"], ["qg8"], s2=cc(C_HLO), op1=ALU.mult)
        ts("dve", qg8[:, 1:2], cc(C_QG), 0.125, ALU.mult, ["cst"], ["qg8"], s2=cc(C_HHI), op1=ALU.mult)
        ms("pool", va[:].rearrange("p a b c -> p (a b c)"), 1.0, [f"va{i}" for i in range(KR)])
        ms("pool", va_m[:].rearrange("p b c -> p (b c)"), 1.0, ["va_m"])
        ms("pool", prem[:].rearrange("p a b -> p (a b)"), 0.0, ["prem"])
        ms("pool", Cf[:].rearrange("p a b c -> p (a b c)"), 0.0, ["Cf0"])
        ms("pool", xnT[:].rearrange("p s a b -> p (s a b)"), 0.0, ["xnT0", "xnT1"])

        PIECES = [(i * 512, min(512, CIN - i * 512)) for i in range((CIN + 511) // 512)]
        STG = [(xr, 0, "xr0", "ld_w0"), (xr, 512, "xr1", "ld_w1"), (xt, 0, "xt0", "ld_w2"), (xt, 512, "xt1", "ld_w3")]
        k = 0
        for kc in range(8):
            for (c0, cn) in PIECES:
                buf, off, bkey, dkey = STG[k % 4]
                on_dve = (k % 2 == 0)
                k += 1
                stg = buf[:, off:off + cn]
                emit_dma(dkey, lambda e, kc=kc, c0=c0, cn=cn, stg=stg: e.dma_start(
                    out=stg, in_=wi[kc * 128:(kc + 1) * 128, c0:c0 + cn]), writes=[bkey])
                if on_dve:
                    ts("dve", w_bf[:, kc, c0:c0 + cn], stg, cc(C_NG + kc), ALU.mult, [bkey, "cst"], ["w_bf"])
                else:
                    act(w_bf[:, kc, c0:c0 + cn], stg, AF.Copy, [bkey, "cst"], ["w_bf"], scale=cc(C_NG + kc))
        for kc in range(8):
            for half in range(2):
                buf, off, bkey, dkey = STG[k % 4]
                on_dve = (k % 2 == 0)
                k += 1
                srcw = buf[:, off:off + 512]
                emit_dma(dkey, lambda e, kc=kc, half=half, srcw=srcw: e.dma_start(
                    out=srcw, in_=wo[kc * 128:(kc + 1) * 128, half * 512:(half + 1) * 512]), writes=[bkey])
                dstw = wo_bf[:, kc, half * 512:(half + 1) * 512]
                if on_dve:
                    if kc < 4:
                        ts("dve", dstw, srcw, cc(C_MG + kc), ALU.mult, [bkey, "cst"], ["wo_bf"])
                    else:
                        cp("dve", dstw, srcw, [bkey], ["wo_bf"])
                else:
                    if kc < 4:
                        act(dstw, srcw, AF.Copy, [bkey, "cst"], ["wo_bf"], scale=cc(C_MG + kc))
                    else:
                        act(dstw, srcw, AF.Copy, [bkey], ["wo_bf"])
        Ff = Ftab[:].rearrange("p h i q -> p (h i q)")
        for j in range(16):
            (b0, o0, k0, d0), (b1, o1, k1, d1) = (STG[0], STG[1]) if j % 2 == 0 else (STG[2], STG[3])
            raw, msk = b0[:, o0:o0 + 512], b1[:, o1:o1 + 512]
            emit_dma(d0, lambda e, j=j, raw=raw: e.dma_start(out=raw, in_=fraw[:, j * 512:(j + 1) * 512]), writes=[k0])
            emit_dma(d1, lambda e, j=j, msk=msk: e.dma_start(out=msk, in_=fmsk[:, j * 512:(j + 1) * 512]), writes=[k1])
            act(raw, raw, AF.Exp, [k0], [k0])
            tt("dve", Ff[:, j * 512:(j + 1) * 512], raw, msk, ALU.mult, [k0, k1], ["Ftab"])

        def stage_X(src_ap, xs_=0, t=None, halo_dst=None):
            emit_dma("ld_x", lambda e: e.dma_start(out=xt[:], in_=src_ap), writes=["xt", "xt0", "xt1"])
            act(xnb[:], xt[:], AF.Square, ["xt"], ["st1a", "xnb"], scale=1.0 / 32, accum=st1[:, 0:1])
            act(st1[:, 1:2], st1[:, 0:1], AF.Ln, ["st1a"], ["st1b"], bias=EPS)
            act(st1[:, 2:3], st1[:, 1:2], AF.Exp, ["st1b"], ["st1c"], scale=-0.5)
            ts("dve", xnb[:], xt[:], st1[:, 2:3], ALU.mult, ["xt", "st1c"], ["xnb"])
            b = nb()
            for kc in range(8):
                tr(ps_bf(b)[:, kc * 128:(kc + 1) * 128], xnb[:, kc * 128:(kc + 1) * 128], ["xnb"], [pk(b)])
            if halo_dst is not None:
                cp("dve", halo_dst, ps_bf(b).rearrange("p (a b) -> p a b", a=8), [pk(b)], ["xhT"])
                return
            cp("dve", xnT[:, xs_, :, 2:130], ps_bf(b).rearrange("p (a b) -> p a b", a=8), [pk(b)], [f"xnT{xs_}"])
            if t is not None:
                cp("pool", xnT[:, xs_, :, 0:2], xhT[:, :, 4 * t:4 * t + 2], ["xhT"], [f"xnT{xs_}"])
                cp("pool", xnT[:, xs_, :, 130:132], xhT[:, :, 4 * t + 2:4 * t + 4], ["xhT"], [f"xnT{xs_}"])

        def stage_A(mode, xs_=0, t=None, slot_q=None, slot_k=None, va_dst=None, kn_dst=None, va_key="va_m",
                    ctx=None, ab=0, slot_z=0):
            def enter(name):
                if ctx is not None:
                    setctx(*ctx[name])
            full = mode == "full"
            xk = f"xnT{xs_}"
            vmv, gtv, vmk, gtk = vm[:, ab, :], gt[:, ab, :], f"vm{ab}", f"gt{ab}"
            qkTs, sos, szms = qkT[:, ab], so[:, ab, :], szm[:, ab]
            qk_key, so_key, szm_key = f"qkT{ab}", f"so{ab}", f"szm{ab}"

            def pg(g):
                return g if full else (g - 4 + 4 * ab)
            enter("main")

            def fm_group(bank, pos, col0, n0, n1):
                n = n1 - n0
                for kc in range(8):
                    mm(ps_f(bank)[:, pos * n:(pos + 1) * n], w_bf[:, kc, col0:col0 + 128], xnT[:, xs_, kc, n0:n1],
                       kc == 0, kc == 7, ["w_bf", xk], [pk(bank)])

            def tm_proj(bank, col0, ncol):
                for kc in range(8):
                    mm(ps_f(bank)[:, 0:ncol], xnT[:, xs_, kc, 2:130], w_bf[:, kc, col0:col0 + ncol],
                       kc == 0, kc == 7, ["w_bf", xk], [pk(bank)])

            groups = list(range(8)) if full else [4, 5, 6, 7]
            gi = 0
            while gi < len(groups):
                grp = groups[gi:gi + 3]
                gi += 3
                bk = nb()
                for pos, g in enumerate(grp):
                    fm_group(bk, pos, g * 128, 0, 132)
                for pos, g in enumerate(grp):
                    src = ps_f(bk)[:, pos * 132:(pos + 1) * 132]
                    act(acc[:, pg(g), :], src[:, 0:128], AF.Copy, [pk(bk), "cst"], [f"acc{pg(g)}"], scale=cc(C_CW + g * 5))
                    if full and g >= 4 and t is None:
                        act(prem[:, g - 4, 2:18], src[:, 2:18], AF.Copy, [pk(bk)], ["prem"])
                    if full and g >= 4 and t is not None and t % TPU == 0:
                        act(prem[:, g - 4, 18:20], src[:, 2:4], AF.Copy, [pk(bk)], ["prem"])
                order = ([(j, pos, g) for j in range(1, 5) for pos, g in enumerate(grp)] if OPT_TAPS else
                         [(j, pos, g) for pos, g in enumerate(grp) for j in range(1, 5)])
                for (j, pos, g) in order:
                    src = ps_f(bk)[:, pos * 132:(pos + 1) * 132]
                    stt("dve", acc[:, pg(g), :], src[:, j:j + 128], cc(C_CW + g * 5 + j), acc[:, pg(g), :],
                        ALU.mult, ALU.add, [pk(bk), "cst", f"acc{pg(g)}"], [f"acc{pg(g)}"])
            bv = nb()
            tm_proj(bv, VM, 512)
            act(vmv, ps_f(bv)[:, :], AF.Copy, [pk(bv)], [vmk])
            bg = nb()
            tm_proj(bg, GT, 16)
            tt("dve", gtv, ps_f(bg)[:, 0:16], cst[:, C_BG:C_BG + 16], ALU.add, [pk(bg), "cst"], [gtk])
            if full:
                bo = nb()
                tm_proj(bo, OM, 512)
                act(sos, ps_f(bo)[:, :], AF.Tanh, [pk(bo)], [so_key], scale=0.5)
                ts("pool", sos, sos, 0.5, ALU.mult, [so_key], [so_key], s2=0.5, op1=ALU.add)
                bz = nb()
                for g in range(4):
                    fm_group(bz, g, ZM + g * 128, 2, 130)
                act(szms.rearrange("p a b -> p (a b)"), ps_f(bz)[:, :], AF.Silu, [pk(bz)], [szm_key])
                bz = nb()
                for g in range(4):
                    fm_group(bz, g, ZA + g * 128, 2, 130)
                act(sza[:, slot_z].rearrange("p a b -> p (a b)"), ps_f(bz)[:, :], AF.Silu, [pk(bz)], [f"sza{slot_z}"])
                act(qkTs.rearrange("p a b -> p (a b)"), acc[:].rearrange("p a b -> p (a b)"), AF.Silu,
                    [f"acc{g}" for g in range(8)], [qk_key])
                enter("norm")
                bva = nb()
                tm_proj(bva, VA, 512)
                cp("dve", va_dst[:, :, 0:64], ps_f(bva)[:, :].rearrange("p (h d) -> p h d", h=8), [pk(bva)], [va_key])
                for (col, isq, dkey) in ((QA, True, f"qnT{slot_q}"), (KA, False, f"knT{slot_k}")):
                    if isq and kn_dst is not None:
                        continue
                    bq = nb()
                    for g in range(4):
                        fm_group(bq, g, col + g * 128, 2, 130)
                    act(sqb[:].rearrange("p a b -> p (a b)"), ps_f(bq)[:, :], AF.Square, [pk(bq)], ["sqb"])
                    bs = nb()
                    for g in range(4):
                        mm(ps_f(bs)[:, g * 128:(g + 1) * 128], bones[:], sqb[:, g, :], True, True,
                           ["bones", "sqb"], [pk(bs)])
                    act(rsn[:], ps_f(bs)[:, :], AF.Ln, [pk(bs)], ["rsn"], scale=1.0 / 64, bias=EPS)
                    act(rsn[:], rsn[:], AF.Exp, ["rsn"], ["rsn"], scale=-0.5)
                    if isq:
                        for par in range(2):
                            stt("dve", qnT[:, slot_q, par].rearrange("p a b -> p (a b)"), ps_f(bq)[:, :], qg8[:, par:par + 1],
                                rsn[:], ALU.mult, ALU.mult, [pk(bq), "rsn", "qg8"], [dkey])
                    else:
                        dst = kn_dst if kn_dst is not None else knT[:, slot_k]
                        stt("dve", dst.rearrange("p a b -> p (a b)"), ps_f(bq)[:, :], cc(C_KG), rsn[:], ALU.mult, ALU.mult,
                            [pk(bq), "rsn", "cst"], [dkey])
            enter("tail_k")
            if not full:
                act(qkTs[:, 4:8, :].rearrange("p a b -> p (a b)"), acc[:, 4 * ab:4 * ab + 4, :].rearrange("p a b -> p (a b)"),
                    AF.Silu, [f"acc{4 * ab + g}" for g in range(4)], [qk_key])
            bt = nb()
            for g in range(4):
                tr(ps_bf(bt)[:, g * 128:(g + 1) * 128], qkTs[:, 4 + g, :], [qk_key], [pk(bt)])
            cp("dve", ktok[:], ps_bf(bt)[:, 0:512], [pk(bt)], ["ktok"])
            enter("tail_g")
            gv = gtv.rearrange("p (d k h) -> p d k h", d=2, k=2)
            f_view = gv[:, :, 1, :]
            i_view = gv[:, :, 0, :]
            g8v = g8[:].rearrange("p (a b) -> p a b", a=2)
            act(g8v, f_view, AF.Exp, [gtk], ["g8"], scale=-1.0)
            act(g8[:], g8[:], AF.Ln, ["g8"], ["g8"], bias=1.0)
            bc = nb()
            mm(ps_f(bc)[:, 0:4], Tf, g8[:, 0:4], True, True, ["cm_f", "g8"], [pk(bc)])
            mm(ps_f(bc)[:, 4:8], Tb, g8[:, 4:8], True, True, ["cm_f", "g8"], [pk(bc)])
            mm(ps_f(bc)[:, 8:16], negones, g8[:, 0:8], True, True, ["cm_f", "g8"], [pk(bc)])
            tt("dve", d8[:].rearrange("p (a b) -> p a b", a=2), i_view, ps_f(bc)[:, 0:8].rearrange("p (a b) -> p a b", a=2),
               ALU.subtract, [gtk, pk(bc)], ["d8"])
            act(sc[:, 0:2, :].rearrange("p a b -> p (a b)"), d8[:], AF.Exp, ["d8"], ["sc"])
            act(eb8[:], ps_f(bc)[:, 0:8], AF.Exp, [pk(bc)], ["eb8"], scale=-1.0, bias=float(0.5 * np.log(128.0)))
            act(dec8[:], ps_f(bc)[:, 8:16], AF.Exp, [pk(bc)], ["dec8"])
            tt("dve", sc[:, 2:4, :].rearrange("p a b -> p (a b)"), sc[:, 0:2, :].rearrange("p a b -> p (a b)"), dec8[:],
               ALU.mult, ["sc", "dec8"], ["sc"])
            vm3 = vmv.rearrange("p (h d) -> p h d", h=4)
            if full:
                vm_b = vm3.unsqueeze(1).to_broadcast([128, 2, 4, 128])
                sc_b = sc[:, 0:2, :].unsqueeze(3).to_broadcast([128, 2, 4, 128])
                tt("dve", V4[:, 0:2, :, 0:128], vm_b, sc_b, ALU.mult, [vmk, "sc"], ["V4u"])
                cp("dve", V4[:, 0:2, :, 128], sc[:, 0:2, :], ["sc"], ["V4u"])
                tt("pool", V4[:, 2, :, 0:128], vm3, sc[:, 2, :].unsqueeze(2).to_broadcast([128, 4, 128]), ALU.mult,
                   [vmk, "sc"], ["V4w"])
                cp("pool", V4[:, 2, :, 128], sc[:, 2, :], ["sc"], ["V4w"])
            else:
                tt("dve", V4[:, 2, :, 0:128], vm3, sc[:, 3, :].unsqueeze(2).to_broadcast([128, 4, 128]), ALU.mult,
                   [vmk, "sc"], ["V4w"])
                cp("dve", V4[:, 2, :, 128], sc[:, 3, :], ["sc"], ["V4w"])

        def state_update(cur, vk, deccol0):
            nxt = cur
            for j in range(2):
                bk = nb()
                for hh in range(2):
                    h = 2 * j + hh
                    mm(ps_f(bk)[:, hh * 129:(hh + 1) * 129], ktok[:, h * 128:(h + 1) * 128], V4[:, vk, h, :],
                       True, True, ["ktok", "V4w"], [pk(bk)])
                for hh in range(2):
                    h = 2 * j + hh
                    stt("dve", Cf[:, nxt, h, :], Cf[:, cur, h, :], dec8[:, deccol0 + h:deccol0 + h + 1],
                        ps_f(bk)[:, hh * 129:(hh + 1) * 129], ALU.mult, ALU.add,
                        [f"Cf{cur}", "dec8", pk(bk)], [f"Cf{nxt}"])
            return nxt

        if phase >= 1:
            for i in range(2):
                stage_X(xh[i * 128:(i + 1) * 128, :], halo_dst=xhT[:, :, i * 128:(i + 1) * 128])
            emit_dma("ld_x", lambda e: e.dma_start(out=xt[:], in_=xh[256:384, :]), writes=["xt", "xt0", "xt1"])
            act(xnb[:], xt[:], AF.Square, ["xt"], ["st1a", "xnb"], scale=1.0 / 32, accum=st1[:, 0:1])
            act(st1[:, 1:2], st1[:, 0:1], AF.Ln, ["st1a"], ["st1b"], bias=EPS)
            act(st1[:, 2:3], st1[:, 1:2], AF.Exp, ["st1b"], ["st1c"], scale=-0.5)
            ts("dve", xnb[:], xt[:], st1[:, 2:3], ALU.mult, ["xt", "st1c"], ["xnb"])
            b = nb()
            for kc in range(8):
                tr(ps_bf(b)[:, kc * 128:(kc + 1) * 128], xnb[:, kc * 128:(kc + 1) * 128], ["xnb"], [pk(b)])
            cp("dve", xhT[:, :, 256:320], ps_bf(b).rearrange("p (a b) -> p a b", a=8)[:, :, 0:64], [pk(b)], ["xhT"])
            stage_X(xh[384:512, :], xs_=0, t=None)
            stage_A("full", xs_=0, t=None, slot_q=0, slot_k=0, va_dst=va_m[:], kn_dst=knT[:, 0])
            cp("dve", kmT[:], knT[:, 0, :, 0:16], ["knT0"], ["kmT"])
            bm = nb()
            mm(ps_f(bm)[:, 0:4], Tm, g8[:, 0:4], True, True, ["cm_f", "g8"], [pk(bm)])
            tt("dve", d8[0:16, 0:4], gt[0:16, 0, 0:4], ps_f(bm)[0:16, 0:4], ALU.add, ["gt0", pk(bm)], ["d8"])
            act(wsm[:], d8[0:16, 0:4], AF.Exp, ["d8"], ["wsm"])
            tt("dve", Vwm[:, :, 0:128], vm[0:16, 0, :].rearrange("p (h d) -> p h d", h=4),
               wsm[:].unsqueeze(2).to_broadcast([16, 4, 128]), ALU.mult, ["vm0", "wsm"], ["Vwm"])
            cp("dve", Vwm[:, :, 128], wsm[:], ["wsm"], ["Vwm"])

        cur = 0
        s1_tiles = list(range(NT - 1, (s1_stop - 1) if phase >= 2 else NT - 1, -1))
        proc = [t for t in s1_tiles if t != s1_stop]
        junk = []
        if proc:
            setctx(None, "A")
            stage_X(xs[proc[0] * 128:(proc[0] + 1) * 128, :], xs_=proc[0] % 2, t=proc[0])
            stage_A("bwd", xs_=proc[0] % 2, t=proc[0], ab=proc[0] % 2,
                    ctx={"main": (None, "A"), "norm": (None, "A"), "tail_k": (junk, "junk"), "tail_g": (junk, "junk")})
            if len(proc) > 1:
                setctx(None, "A")
                stage_X(xs[proc[1] * 128:(proc[1] + 1) * 128, :], xs_=proc[1] % 2, t=proc[1])
        for t in s1_tiles:
            setctx(None, "A")
            emit_dma(f"st_cb{cur}", lambda e, t=t, cur=cur: e.dma_start(
                out=cbs[t], in_=Cf[:, cur].rearrange("p a b -> p (a b)")), reads=[f"Cf{cur}"])
            if t == s1_stop:
                break
            Lk, Lg, La, Lm, Lx = [], [], [], [], []
            stage_A("bwd", xs_=t % 2, t=t, ab=t % 2,
                    ctx={"main": (junk, "junk"), "norm": (junk, "junk"), "tail_k": (Lk, "L1"), "tail_g": (Lg, "S1g")})
            setctx(La, "L1")
            if t % TPU == TPU - 1 and t < NT - 1:
                u = t // TPU
                ts("dve", dec8[:, 4:8], dec8[:, 4:8], cc(C_KEEPB + u), ALU.mult, ["dec8", "cst"], ["dec8"])
            cur = state_update(cur, 2, 4)
            if t - 1 > s1_stop:
                stage_A("bwd", xs_=(t - 1) % 2, t=t - 1, ab=(t - 1) % 2,
                        ctx={"main": (Lm, "L2"), "norm": (Lm, "L2"), "tail_k": (junk, "junk"), "tail_g": (junk, "junk")})
            if t - 2 > s1_stop:
                setctx(Lx, "L3")
                stage_X(xs[(t - 2) * 128:(t - 1) * 128, :], xs_=(t - 2) % 2, t=t - 2)
            setctx(None, "A")
            del junk[:]
            merge_emit([zip_lists(Lk, Lg) + La, Lm, Lx])

        ms("pool", Cf[:, 0].rearrange("p a b -> p (a b)"), 0.0, ["Cf0"])
        ms("pool", C_bf[:].rearrange("p a b -> p (a b)"), 0.0, ["C_bf"])
        cur = 0

        kmtok = hn[0:16].rearrange("p a b -> p (a b)")

        Vwmu = Sfb[0:16].rearrange("p a b c -> p (a b c)")[:, 0:516].rearrange("p (a b) -> p a b", a=4)

        def stage_B(t):
            nonlocal cur
            sb_ = t % 2
            qkTs, sos, szms = qkT[:, sb_], so[:, sb_, :], szm[:, sb_]
            qk_key, so_key, szm_key = f"qkT{sb_}", f"so{sb_}", f"szm{sb_}"
            if t % TPU == 0:
                u = t // TPU
                for g in range(4):
                    ts("dve", accm[:, g, :], prem[:, g, 0:16], cc(C_CW + (4 + g) * 5), ALU.mult, ["prem", "cst"], ["accm"])
                    for j in range(1, 5):
                        stt("dve", accm[:, g, :], prem[:, g, j:j + 16], cc(C_CW + (4 + g) * 5 + j), accm[:, g, :],
                            ALU.mult, ALU.add, ["prem", "cst", "accm"], ["accm"])
                act(kmm[:].rearrange("p a b -> p (a b)"), accm[:].rearrange("p a b -> p (a b)"), AF.Silu, ["accm"], ["kmm"])
                bt = nb()
                for g in range(4):
                    tr(ps_bf(bt)[0:16, g * 128:(g + 1) * 128], kmm[:, g, :], ["kmm"], [pk(bt)])
                cp("dve", kmtok, ps_bf(bt)[0:16, 0:512], [pk(bt)], ["hn"])
                ts("dve", Vwmu, Vwm[:], cst[0:16, C_NKEEP + u:C_NKEEP + u + 1], ALU.mult, ["Vwm", "cst"], ["Sf", "Sb"])
                nxt = cur
                for j in range(2):
                    bk = nb()
                    for hh in range(2):
                        h = 2 * j + hh
                        mm(ps_f(bk)[:, hh * 129:(hh + 1) * 129], kmtok[:, h * 128:(h + 1) * 128], Vwmu[:, h, :],
                           True, True, ["hn", "Sf", "Sb"], [pk(bk)])
                    for hh in range(2):
                        h = 2 * j + hh
                        stt("dve", Cf[:, nxt, h, :], Cf[:, cur, h, :], cc(C_KEEP + u),
                            ps_f(bk)[:, hh * 129:(hh + 1) * 129], ALU.mult, ALU.add,
                            [f"Cf{cur}", "cst", pk(bk)], [f"Cf{nxt}"])
                cur = nxt
                act(C_bf[:].rearrange("p a b -> p (a b)"), Cf[:, cur].rearrange("p a b -> p (a b)"), AF.Copy,
                    [f"Cf{cur}"], ["C_bf"])
            emit_dma("ld_cb", lambda e: e.dma_start(out=cb[:].rearrange("p a b -> p (a b)"), in_=cbs[t]),
                     reads=["cbsA", "cbsB"], writes=["cb"])
            if t % TPU == TPU - 1 and t < NT - 1:
                act(cb_bf[:].rearrange("p a b -> p (a b)"), cb[:].rearrange("p a b -> p (a b)"), AF.Copy,
                    ["cb", "cst"], ["cb_bf"], scale=cc(C_KEEPB + t // TPU))
            else:
                act(cb_bf[:].rearrange("p a b -> p (a b)"), cb[:].rearrange("p a b -> p (a b)"), AF.Copy, ["cb"], ["cb_bf"])
            bs = nb()
            for h in range(4):
                mm(ps_f(bs)[:, h * 128:(h + 1) * 128], qkTs[:, 4 + h, :], qkTs[:, h, :], True, True, [qk_key], [pk(bs)])
            sview = ps_f(bs)[:, :].rearrange("p (h j) -> p h j", h=4)
            tt("dve", Sfb[:, 0], sview, triu.unsqueeze(1).to_broadcast([128, 4, 128]), ALU.mult, [pk(bs), "cm_f"], ["Sf"])
            tt("dve", Sfb[:, 1], sview, tril.unsqueeze(1).to_broadcast([128, 4, 128]), ALU.mult, [pk(bs), "cm_f"], ["Sb"])
            for d in range(2):
                banks = []
                for j in range(2):
                    bk = nb()
                    banks.append(bk)
                    for hh in range(2):
                        h = 2 * j + hh
                        o = ps_f(bk)[:, hh * 129:(hh + 1) * 129]
                        mm(o, Sfb[:, d, h, :], V4[:, d, h, :], True, False, ["Sf" if d == 0 else "Sb", "V4u"], [pk(bk)])
                        mm(o, qkTs[:, h, :], (C_bf if d == 0 else cb_bf)[:, h, :], False, True,
                           [qk_key, "C_bf" if d == 0 else "cb_bf"], [pk(bk)])
                    dv = ps_f(bk)[:, 0:258].rearrange("p (a b) -> p a b", a=2)[:, :, 128]
                    act(da8[:, d * 4 + 2 * j:d * 4 + 2 * j + 2], dv, AF.Abs, [pk(bk)], [f"da8{d}"])
                dsl = da8[:, d * 4:d * 4 + 4]
                tt("dve", dsl, dsl, eb8[:, d * 4:d * 4 + 4], ALU.max, [f"da8{d}", "eb8"], [f"da8{d}"])
                recip(dsl, dsl, [f"da8{d}"], [f"da8{d}"])
                for j in range(2):
                    bk = banks[j]
                    nv = ps_f(bk)[:, 0:258].rearrange("p (a b) -> p a b", a=2)[:, :, 0:128]
                    if d == 0:
                        tt("dve", hf[:, 2 * j:2 * j + 2, :], nv,
                           da8[:, 2 * j:2 * j + 2].unsqueeze(2).to_broadcast([128, 2, 128]),
                           ALU.mult, [pk(bk), "da80"], ["hf"])
                    else:
                        for hh in range(2):
                            h = 2 * j + hh
                            stt("dve", hf[:, h, :], nv[:, hh, :], da8[:, 4 + h:5 + h], hf[:, h, :], ALU.mult, ALU.add,
                                [pk(bk), "da81", "hf"], ["hf"])
            hfv = hf[:].rearrange("p a b -> p (a b)")
            tt("dve", hfv, hfv, sos, ALU.mult, ["hf", so_key], ["hf"])
            for h in range(4):
                act(hn[:, h, :], hf[:, h, :], AF.Square, ["hf"], ["hn", "ss4"], scale=float(128.0 ** -0.5),
                    accum=ss4[:, h:h + 1])
            act(r4[:], ss4[:], AF.Ln, ["ss4"], ["r4"], bias=EPS)
            act(r4[:], r4[:], AF.Exp, ["r4"], ["r4"], scale=-0.5)
            tt("dve", hn[:], hf[:], r4[:].unsqueeze(2).to_broadcast([128, 4, 128]), ALU.mult, ["hf", "r4"], ["hn"])
            bt = nb()
            for g in range(4):
                tr(ps_bf(bt)[:, g * 128:(g + 1) * 128], hn[:, g, :], ["hn"], [pk(bt)])
            tt("dve", ymT[:, t % QR].rearrange("p a b -> p (a b)"), ps_bf(bt)[:, 0:512],
               szms.rearrange("p a b -> p (a b)"), ALU.mult, [pk(bt), szm_key], [f"ymT{t % QR}"])
            cur = state_update(cur, 2, 0)
            act(C_bf[:].rearrange("p a b -> p (a b)"), Cf[:, cur].rearrange("p a b -> p (a b)"), AF.Copy,
                [f"Cf{cur}"], ["C_bf"])

        def stage_C(tq):
            sq = tq % QR
            bms = [nb(), nb()]
            for h in range(8):
                g, par = h // 2, h % 2
                mm(ps_f(bms[h // 4])[0:16, (h % 4) * 128:(h % 4 + 1) * 128], kmT[:, g, :],
                   qnT[:, sq, par, g, :], True, True, ["kmT", f"qnT{sq}"], [pk(bms[h // 4])])
            for j in range(2):
                act(Em[:, 4 * j:4 * j + 4, :].rearrange("p a b -> p (a b)"), ps_f(bms[j])[0:16, :], AF.Exp,
                    [pk(bms[j])], ["Em"])
            bpv = [6, 7]
            started = [False, False]

            def pv(h, lhsT, rhs, reads, last):
                j = h // 4
                st = not started[j]
                started[j] = True
                mm(ps_f(bpv[j])[:, (h % 4) * 65:(h % 4 + 1) * 65], lhsT, rhs, st, last,
                   reads, [pk(bpv[j])], skip=True)

            tiles = PLAN[tq]
            for h in range(8):
                pv(h, Em[0:16, h, :], va_m[0:16, h, :], ["Em", "va_m"], False)
            for idx, (a, halves) in enumerate(tiles):
                sk = a % KR
                es_ = idx % 2
                Ebs, Pbs = Eb[:, es_], Pb[:, 0]
                ek = [f"Eb{es_}0", f"Eb{es_}1"]
                pkey = "Pb"
                for j in range(2):
                    bq = nb()
                    for hh in range(4):
                        h = 4 * j + hh
                        g, par = h // 2, h % 2
                        mm(ps_f(bq)[:, hh * 128:(hh + 1) * 128], knT[:, sk, g, :],
                           qnT[:, sq, par, g, :], True, True, [f"knT{sk}", f"qnT{sq}"], [pk(bq)])
                    act(Ebs[:, 4 * j:4 * j + 4, :].rearrange("p a b -> p (a b)"), ps_f(bq)[:, :], AF.Exp,
                        [pk(bq)], [ek[j]])
                i0 = 7 - (2 * a - 2 * tq)
                both_full = all(hv is not None and hv[0] == "s" and hv[1] == 0 and hv[2] == 128 for hv in halves)
                if both_full:
                    tt("dve", Pbs.rearrange("p h (r q) -> p h r q", r=2),
                       Ebs.rearrange("p h (r q) -> p h r q", r=2), Ftab[:, :, i0:i0 + 2, :], ALU.mult,
                       ek + ["Ftab"], [pkey])
                else:
                    for hr in (0, 1):
                        hv = halves[hr]
                        dstp = Pbs[:, :, hr * 64:(hr + 1) * 64]
                        srcp = Ebs[:, :, hr * 64:(hr + 1) * 64]
                        if hv is None:
                            ts("dve", dstp, srcp, 0.0, ALU.mult, ek, [pkey])
                        elif hv[0] == "s" and hv[1] == 0 and hv[2] == 128:
                            tt("dve", dstp, srcp, Ftab[:, :, i0 + hr, :], ALU.mult, ek + ["Ftab"], [pkey])
                        else:
                            if hv[0] == "s":
                                mcol = C_HLO if hv[1] == 0 else C_HHI
                            else:
                                mcol = C_CM + hv[1]
                            stt("dve", dstp, srcp, cc(mcol), Ftab[:, :, i0 + hr, :], ALU.mult, ALU.mult,
                                ek + ["Ftab", "cst"], [pkey])
                for h in range(8):
                    pv(h, Pbs[:, h, :], va[:, sk, h, :], [pkey, f"va{sk}"], idx == len(tiles) - 1)
            for j in range(2):
                v3 = ps_f(bpv[j])[:, 0:260].rearrange("p (h d) -> p h d", h=4)
                recip(rd8[:, 4 * j:4 * j + 4], v3[:, :, 64], [pk(bpv[j])], [f"rd8{j}"])
                tt("dve", ya[:, 4 * j:4 * j + 4, :], v3[:, :, 0:64],
                   rd8[:, 4 * j:4 * j + 4].unsqueeze(2).to_broadcast([128, 4, 64]), ALU.mult,
                   [pk(bpv[j]), f"rd8{j}"], ["ya"])
            bt = nb()
            yav = ya[:].rearrange("p h d -> p (h d)")
            for g in range(4):
                tr(ps_bf(bt)[:, g * 128:(g + 1) * 128], yav[:, g * 128:(g + 1) * 128], ["ya"], [pk(bt)])
            tt("dve", yaT[:].rearrange("p a b -> p (a b)"), ps_bf(bt)[:, 0:512],
               sza[:, tq % SZR].rearrange("p a b -> p (a b)"), ALU.mult, [pk(bt), f"sza{tq % SZR}"], ["yaT"])
            emit_dma("ld_xr", lambda e: e.dma_start(out=xr[:], in_=xs[tq * 128:(tq + 1) * 128, :]), writes=["xr0", "xr1"])
            for n in range(2):
                bo = nb()
                for kt in range(8):
                    lhsT = ymT[:, sq, kt, :] if kt < 4 else yaT[:, kt - 4, :]
                    mm(ps_f(bo)[:, :], lhsT, wo_bf[:, kt, n * 512:(n + 1) * 512], kt == 0, kt == 7,
                       [f"ymT{sq}", "yaT", "wo_bf"], [pk(bo)])
                tt("dve", xr[:, n * 512:(n + 1) * 512], ps_f(bo)[:, :], xr[:, n * 512:(n + 1) * 512], ALU.add,
                   [pk(bo), f"xr{n}"], [f"xr{n}"])
            emit_dma("st_y", lambda e: e.dma_start(out=y[tq * 128:(tq + 1) * 128, :], in_=xr[:]), reads=["xr0", "xr1"])

        S.buf["cbsA"] = [("d", "st_cb0", S.dma_counts.get("st_cb0", 0)), []]
        S.buf["cbsB"] = [("d", "st_cb1", S.dma_counts.get("st_cb1", 0)), []]

        s2_end = s2_start + s2_tiles
        if phase >= 3:
            junk2 = []

            def a_call(i, keep):
                ctx = {}
                for name in ("main", "norm", "tail_k", "tail_g"):
                    ctx[name] = keep.get(name, (junk2, "junk"))
                stage_A("full", xs_=i % 2, t=i, slot_q=i % QR, slot_k=i % KR, va_dst=va[:, i % KR],
                        va_key=f"va{i % KR}", ctx=ctx, ab=i % 2, slot_z=i % SZR)
                del junk2[:]

            setctx(None, "A")
            stage_X(xs[s2_start * 128:(s2_start + 1) * 128, :], xs_=s2_start % 2, t=s2_start)
            for i in range(s2_start, s2_end + LAG):
                L1, L2, L3 = [], [], []
                if i < s2_end:
                    Lk, Lg = [], []
                    a_call(i, {"main": (None, "A")})
                    a_call(i, {"norm": (L2, "L2"), "tail_k": (Lk, "L1"), "tail_g": (Lg, "L1")})
                    L1.extend(zip_lists(Lk, Lg))
                    setctx(L1, "L1")
                    stage_B(i)
                if i - LAG >= s2_start and phase >= 4:
                    setctx(L2, "L2")
                    stage_C(i - LAG)
                if i + 1 < s2_end:
                    setctx(L3, "L3")
                    stage_X(xs[(i + 1) * 128:(i + 2) * 128, :], xs_=(i + 1) % 2, t=i + 1)
                setctx(None, "A")
                merge_emit([L1, L2, L3])

        S.finalize()
        with nc.Block() as block:
            @block.sync
            def _(e):
                S.run("sp", e, sems, final_dma_keys=[k for k in dma_keys if k.startswith("st_") and S.dma_counts.get(k)])

            @block.tensor
            def _(e):
                S.run("pe", e, sems)

            @block.scalar
            def _(e):
                S.run("act", e, sems)

            @block.vector
            def _(e):
                S.run("dve", e, sems)

            @block.gpsimd
            def _(e):
                S.run("pool", e, sems)
    return nc


def _core_streams(x_prompt, x_sample, c):
    if c < 2:
        xs = np.concatenate([x_sample[c], x_prompt[c]], axis=0)
        seq_starts = [0, 8192]
        seq_ends = [8192, 10240]
    else:
        i0 = 2 + 5 * (c - 2)
        xs = x_prompt[i0:i0 + 5].reshape(5 * 2048, DM)
        seq_starts = [2048 * u for u in range(5)]
        seq_ends = [2048 * (u + 1) for u in range(5)]
    return np.ascontiguousarray(xs), set(seq_starts), set(seq_ends)


def _host_layout(inputs):
    x_prompt = np.asarray(inputs["x_prompt"], np.float32)
    x_sample = np.asarray(inputs["x_sample"], np.float32)
    meta = np.asarray(inputs["meta_tokens"], np.float32)
    wi = np.ascontiguousarray(np.asarray(inputs["w_in"], np.float32)[0])
    wo = np.ascontiguousarray(np.asarray(inputs["w_out"], np.float32)[0])
    norm_g = np.asarray(inputs["norm_g"], np.float32)[0]
    b_gate = np.asarray(inputs["b_gate"], np.float32)[0]
    conv_w = np.asarray(inputs["conv_w"], np.float32)[0]
    mg = np.asarray(inputs["mlstm_norm_g"], np.float32)[0]
    qg = np.asarray(inputs["q_norm_g"], np.float32)[0]
    kg = np.asarray(inputs["k_norm_g"], np.float32)[0]
    rpb = np.asarray(inputs["rpb"], np.float32)[0]

    p = np.arange(128)
    rr, kc = p // 64, p % 64
    i = np.arange(16)
    qc = np.arange(64)
    dr = (7 - i)[None, :] + rr[:, None]
    colidx = kc[:, None] - qc[None, :] + 15
    cs = np.clip(qc - 8, 0, 48)
    allowed = (kc[:, None] >= cs[None, :]) & (kc[:, None] < cs[None, :] + 16)
    valid = (dr >= -7) & (dr <= 7)
    rowi = np.clip(dr + 7, 0, 14)
    coli = np.clip(colidx, 0, 30)
    fraw = rpb[:, rowi[:, :, None], coli[:, None, :]]
    fraw = np.ascontiguousarray(np.transpose(fraw, (1, 0, 2, 3))).reshape(128, 8 * 16 * 64).astype(np.float32)
    fm = (valid[:, :, None] & allowed[:, None, :]).astype(np.float32)
    fmsk = np.ascontiguousarray(np.broadcast_to(fm[:, None], (128, 8, 16, 64))).reshape(128, 8 * 16 * 64)

    s = np.arange(128)[:, None]
    j = np.arange(128)[None, :]
    cmat = np.stack([
        (s <= j), (s >= j), -(s <= j).astype(np.float32), -(s >= j).astype(np.float32),
        -np.ones((128, 128)), -((s > j) & (s <= 15)).astype(np.float32)], axis=1).astype(np.float32)
    cmat = np.ascontiguousarray(cmat).reshape(128, 6 * 128)

    in_maps = []
    for c in range(NCORES):
        xs, starts, ends = _core_streams(x_prompt, x_sample, c)
        xh = np.zeros((NPRE * 128, DM), np.float32)
        for t in range(NT):
            t0 = t * 128
            if t0 in starts:
                xh[4 * t:4 * t + 2] = meta[14:16]
            else:
                xh[4 * t:4 * t + 2] = xs[t0 - 2:t0]
            if t0 + 128 not in ends:
                xh[4 * t + 2:4 * t + 4] = xs[t0 + 128:t0 + 130]
        xh[384:400] = meta
        linked = c < 2
        cst = np.zeros((128, NCST), np.float32)
        cst[:, C_BG:C_BG + 16] = b_gate[None, :]
        for g in range(8):
            for jj in range(5):
                cst[:, C_CW + g * 5 + jj] = conv_w[jj, g * 128:(g + 1) * 128]
        for kc_ in range(8):
            cst[:, C_NG + kc_] = norm_g[kc_ * 128:(kc_ + 1) * 128]
        for g in range(4):
            cst[:, C_MG + g] = mg[g * 128:(g + 1) * 128]
        cst[:, C_QG] = qg[p % 64]
        cst[:, C_KG] = kg[p % 64]
        keep = [0, 1, 1, 1, 0] if linked else [0, 0, 0, 0, 0]
        for u in range(5):
            cst[:, C_KEEP + u] = keep[u]
            cst[:, C_NKEEP + u] = 1 - keep[u]
        for u in range(4):
            cst[:, C_KEEPB + u] = keep[u + 1]
        cst[0:64, C_HLO] = 1.0
        cst[64:128, C_HHI] = 1.0
        for ci, (R, a) in enumerate(CMS):
            w = _win_lnk(R) if linked else _win_unl(R)
            cst[:, C_CM + ci] = np.array([(2 * a + (pp // 64)) in w for pp in range(128)], np.float32)
        in_maps.append({"xs": xs, "xh": xh, "wi": wi, "wo": wo, "cst": cst, "fraw": fraw, "fmsk": fmsk, "cmat": cmat})
    return in_maps


_NC_CACHE = {}


def kernel(**inputs):
    in_maps = _host_layout(inputs)
    if "nc" not in _NC_CACHE:
        _NC_CACHE["nc"] = build_program()
    nc = _NC_CACHE["nc"]
    res = run_bass_kernel_spmd(nc, in_maps, core_ids=list(range(NCORES)))
    ys = [np.asarray(r["y"], np.float32) for r in res.results]
    y_prompt = np.empty((32, 2048, DM), np.float32)
    y_sample = np.empty((2, 8192, DM), np.float32)
    for c in range(NCORES):
        if c < 2:
            y_sample[c] = ys[c][:8192]
            y_prompt[c] = ys[c][8192:]
        else:
            i0 = 2 + 5 * (c - 2)
            y_prompt[i0:i0 + 5] = ys[c].reshape(5, 2048, DM)
    return (y_prompt, y_sample)
```

```python
import numpy as np
from contextlib import ExitStack
import concourse.bass as bass
import concourse.mybir as mybir
from concourse.bass_utils import run_bass_kernel_spmd

F32 = mybir.dt.float32
BF16 = mybir.dt.bfloat16
ALU = mybir.AluOpType
AF = mybir.ActivationFunctionType

NCORES = 8
NT = 80
TPU = 16
NU = 5
DM = 1024
CIN = 4624
QM, KM, VM, OM, ZM, GT, QA, KA, VA, ZA = 0, 512, 1024, 1536, 2048, 2560, 2576, 3088, 3600, 4112
EPS = 1e-6
NPRE = 4
LAG = 3
OPT_WCONV_ACT = True
OPT_TAPS = True
OPT_SILU_HOIST = True
KR = 7
QR = 4
SZR = 5

C_BG, C_CW, C_NG, C_MG, C_QG, C_KG, C_KEEP, C_NKEEP, C_KEEPB, C_HLO, C_HHI, C_CM = 0, 16, 56, 64, 68, 69, 70, 75, 80, 84, 85, 86


def _win_unl(R):
    u, r = divmod(R, 32)
    rs = 32 * u + min(max(r - 4, 0), 24)
    return set(range(rs, rs + 8))


def _win_lnk(R):
    if R // 32 == 4:
        return _win_unl(R)
    rs = min(max(R - 4, 0), 120)
    return set(range(rs, rs + 8))


def attn_plan():
    plan = []
    cms = []
    for tq in range(NT):
        tiles = {}
        for hr in (0, 1):
            R = 2 * tq + hr
            wu, wl = _win_unl(R), _win_lnk(R)
            for a in sorted({r // 2 for r in (wu | wl)}):
                rows = {2 * a, 2 * a + 1}
                ru, rl = wu & rows, wl & rows
                ent = tiles.setdefault(a, [None, None])
                if ru == rl:
                    if len(ru) == 2:
                        ent[hr] = ("s", 0, 128)
                    elif ru == {2 * a}:
                        ent[hr] = ("s", 0, 64)
                    elif ru == {2 * a + 1}:
                        ent[hr] = ("s", 64, 128)
                else:
                    ent[hr] = ("d", len(cms))
                    cms.append((R, a))
        plan.append([(a, tiles[a]) for a in sorted(tiles) if tiles[a][0] or tiles[a][1]])
    return plan, cms


PLAN, CMS = attn_plan()
for _tq, _tl in enumerate(PLAN):
    for _a, _ in _tl:
        assert _tq - 3 <= _a <= _tq + 3 and 0 <= _a < NT, (_tq, _a)
NCST = C_CM + len(CMS)


class _Op:
    __slots__ = ("eng", "fn", "deps", "dma_key", "dma_cnt", "signal", "semval", "idx", "seq", "know")


class Sched:
    def __init__(self, nc):
        self.nc = nc
        self.streams = {e: [] for e in ("pe", "act", "dve", "pool", "sp")}
        self.buf = {}
        self.psr = {}
        self.dma_counts = {}
        self.dma_ops = {}
        self.nseq = 0

    def _deps_for(self, reads, writes):
        deps = []
        for k in reads:
            st = self.buf.get(k)
            if st and st[0] is not None:
                deps.append(st[0])
        for k in writes:
            st = self.buf.get(k)
            if st:
                if st[0] is not None:
                    deps.append(st[0])
                deps.extend(st[1])
        return deps

    def _commit(self, opid, reads, writes):
        for k in reads:
            self.buf.setdefault(k, [None, []])[1].append(opid)
        for k in writes:
            self.buf[k] = [opid, []]

    def op(self, eng, fn, reads=(), writes=()):
        o = _Op()
        o.eng, o.fn, o.dma_key, o.signal = eng, fn, None, False
        o.deps = self._deps_for(reads, writes)
        o.idx = len(self.streams[eng])
        oid = ("e", eng, o.idx)
        o.seq = self.nseq
        self.nseq += 1
        for k in reads:
            if k.startswith("ps") and k not in writes:
                rd = self.psr.setdefault(k, {})
                for e2, rid in rd.items():
                    if e2 != eng:
                        o.deps.append(rid)
                rd[eng] = oid
        for k in writes:
            if k.startswith("ps"):
                self.psr[k] = {}
        self.streams[eng].append(o)
        self._commit(oid, reads, writes)
        return o

    def dma(self, key, fn, reads=(), writes=(), eng="sp"):
        o = _Op()
        o.eng, o.fn, o.dma_key, o.signal = eng, fn, key, True
        c = self.dma_counts.get(key, 0) + 1
        self.dma_counts[key] = c
        o.dma_cnt = c
        o.deps = self._deps_for(reads, writes)
        o.idx = len(self.streams[eng])
        o.seq = self.nseq
        self.nseq += 1
        self.dma_ops[(key, c)] = o
        self.streams[eng].append(o)
        self._commit(("d", key, c), reads, writes)
        return o

    def finalize(self):
        allops = sorted((o for st in self.streams.values() for o in st), key=lambda o: o.seq)
        W = {e: {} for e in self.streams}

        def op_of(src, val):
            return self.streams[src[1]][val] if src[0] == "e" else self.dma_ops[(src[1], val)]

        for o in allops:
            eng = o.eng
            w = W[eng]
            cand = {}
            for d in o.deps:
                if d[0] == "e":
                    _, de, di = d
                    if de == eng and eng == "pe":
                        continue
                    src, val = ("e", de), di
                else:
                    src, val = ("d", d[1]), d[2]
                    if val <= 0:
                        continue
                if cand.get(src, -1) < val:
                    cand[src] = val
            need = {}
            for src, val in sorted(cand.items(), key=lambda kv: -op_of(kv[0], kv[1]).seq):
                if w.get(src, -1) >= val:
                    continue
                need[src] = val
                if w.get(src, -1) < val:
                    w[src] = val
                for s2, v2 in op_of(src, val).know.items():
                    if w.get(s2, -1) < v2:
                        w[s2] = v2
            o.deps = need
            for src, val in need.items():
                if src[0] == "e":
                    self.streams[src[1]][val].signal = True
            o.know = dict(w)
            if o.dma_key is None:
                o.know[("e", eng)] = o.idx
            else:
                o.know[("d", o.dma_key)] = o.dma_cnt
        for eng, stream in self.streams.items():
            c = 0
            for o in stream:
                o.semval = None
                if o.dma_key is None and o.signal:
                    c += 1
                    o.semval = c

    def run(self, eng, e, sems, final_dma_keys=()):
        for o in self.streams[eng]:
            for src, val in o.deps.items():
                if src[0] == "e":
                    e.wait_ge(sems[src[1]], self.streams[src[1]][val].semval)
                else:
                    e.wait_ge(sems["d:" + src[1]], 16 * val)
            ins = o.fn(e)
            if o.dma_key is not None:
                ins.then_inc(sems["d:" + o.dma_key], 16)
            elif o.signal:
                ins.then_inc(sems[eng], 1)
        for k in final_dma_keys:
            e.wait_ge(sems["d:" + k], 16 * self.dma_counts[k])


def build_program(debug=False, phase=9, s1_stop=0, s2_tiles=NT, s2_start=0, cstop=9):
    nc = bass.Bass("TRN2", target_bir_lowering=False)
    xs = nc.dram_tensor("xs", [NT * 128, DM], F32, kind="ExternalInput").ap()
    xh = nc.dram_tensor("xh", [NPRE * 128, DM], F32, kind="ExternalInput").ap()
    wi = nc.dram_tensor("wi", [DM, CIN], F32, kind="ExternalInput").ap()
    wo = nc.dram_tensor("wo", [DM, DM], F32, kind="ExternalInput").ap()
    cst_d = nc.dram_tensor("cst", [128, NCST], F32, kind="ExternalInput").ap()
    fraw = nc.dram_tensor("fraw", [128, 8 * 16 * 64], F32, kind="ExternalInput").ap()
    fmsk = nc.dram_tensor("fmsk", [128, 8 * 16 * 64], F32, kind="ExternalInput").ap()
    cmat = nc.dram_tensor("cmat", [128, 6 * 128], F32, kind="ExternalInput").ap()
    y = nc.dram_tensor("y", [NT * 128, DM], F32, kind="ExternalOutput").ap()
    cbs = nc.dram_tensor("cbs", [NT, 128, 4 * 129], F32, kind="Internal").ap()

    es = ExitStack()
    with es:
        def sb(name, shape, dt):
            return es.enter_context(nc.sbuf_tensor("s_" + name, shape, dt))

        S = Sched(nc)
        w_bf = sb("w_bf", [128, 8, CIN], BF16)
        wo_bf = sb("wo_bf", [128, 8, DM], BF16)
        Ftab = sb("Ftab", [128, 8, 16, 64], BF16)
        xhT = sb("xhT", [128, 8, 320], BF16)
        cst = sb("cst", [128, NCST], F32)
        cm_f = sb("cm_f", [128, 6, 128], F32)
        ident = sb("ident", [128, 128], BF16)
        bones = sb("bones", [128, 128], BF16)
        qg8 = sb("qg8", [128, 2], F32)
        xt = sb("xt", [128, DM], F32)
        st1 = sb("st1", [128, 4], F32)
        xnb = sb("xnb", [128, DM], BF16)
        xnT = sb("xnT", [128, 2, 8, 132], BF16)
        acc = sb("acc", [128, 8, 128], F32)
        qkT = sb("qkT", [128, 2, 8, 128], BF16)
        ktok = sb("ktok", [128, 2, 512], BF16)
        vm = sb("vm", [128, 2, 512], F32)
        so = sb("so", [128, 2, 512], BF16)
        szm = sb("szm", [128, 2, 4, 128], BF16)
        V4 = sb("V4", [128, 3, 4, 129], BF16)
        gt = sb("gt", [128, 2, 16], F32)
        g8 = sb("g8", [128, 2, 8], F32)
        d8 = sb("d8", [128, 2, 8], F32)
        sc = sb("sc", [128, 2, 4, 4], F32)
        eb8 = sb("eb8", [128, 2, 8], F32)
        dec8 = sb("dec8", [128, 2, 8], F32)
        sqb = sb("sqb", [128, 4, 128], BF16)
        rsn = sb("rsn", [128, 512], F32)
        qnT = sb("qnT", [128, QR, 2, 4, 128], BF16)
        knT = sb("knT", [128, KR, 4, 128], BF16)
        va = sb("va", [128, KR, 8, 65], BF16)
        sza = sb("sza", [128, SZR, 4, 128], BF16)
        ymT = sb("ymT", [128, QR, 4, 128], BF16)
        Eb = sb("Eb", [128, 2, 8, 128], BF16)
        Pb = sb("Pb", [128, 1, 8, 128], BF16)
        Em = sb("Em", [16, 8, 128], BF16)
        rd8 = sb("rd8", [128, 8], F32)
        ya = sb("ya", [128, 8, 64], BF16)
        yaT = sb("yaT", [128, 4, 128], BF16)
        Sfb = sb("Sfb", [128, 2, 4, 128], BF16)
        hf = sb("hf", [128, 4, 128], F32)
        hn = sb("hn", [128, 4, 128], BF16)
        da8 = sb("da8", [128, 8], F32)
        ss4 = sb("ss4", [128, 4], F32)
        r4 = sb("r4", [128, 4], F32)
        Cf = sb("Cf", [128, 1, 4, 129], F32)
        C_bf = sb("C_bf", [128, 4, 129], BF16)
        cb = sb("cb", [128, 4, 129], F32)
        cb_bf = sb("cb_bf", [128, 4, 129], BF16)
        xr = sb("xr", [128, DM], F32)
        kmT = sb("kmT", [128, 4, 16], BF16)
        va_m = sb("va_m", [128, 8, 65], BF16)
        prem = sb("prem", [128, 4, 20], F32)
        accm = sb("accm", [128, 4, 16], F32)
        kmm = sb("kmm", [128, 4, 16], BF16)
        Vwm = sb("Vwm", [16, 4, 129], BF16)
        wsm = sb("wsm", [16, 4], F32)

        pbank = [es.enter_context(nc.psum_tensor(f"ps{i}", [128, 512], F32)) for i in range(8)]
        sems = {k: es.enter_context(nc.semaphore(k)) for k in ("pe", "act", "dve", "pool")}
        dma_keys = ["ld_x", "ld_xr", "ld_cb", "st_y", "st_cb0", "st_cb1", "ld_w0", "ld_w1", "ld_w2", "ld_w3", "ld_c0", "ld_c1"]
        for k in dma_keys:
            sems["d:" + k] = es.enter_context(nc.semaphore("d_" + k))

        pools = {"A": [0, 1, 2, 3, 4, 5], "L1": [0, 1], "L2": [2, 3, 4], "L3": [5], "S1g": [6, 7], "junk": [0],
                 "C2": [3, 4], "M2": [2, 5], "S1u": [2]}
        pctr = {k: 0 for k in pools}
        cur_pool = ["A"]
        rec = [None]

        def nb():
            p = cur_pool[0]
            b = pools[p][pctr[p] % len(pools[p])]
            pctr[p] += 1
            return b

        def setctx(lst, pool):
            rec[0] = lst
            cur_pool[0] = pool

        def emit_op(eng, fn, reads, writes):
            if rec[0] is None:
                S.op(eng, fn, reads, writes)
            else:
                rec[0].append(("op", eng, fn, tuple(reads), tuple(writes)))

        def emit_dma(key, fn, reads=(), writes=()):
            if rec[0] is None:
                S.dma(key, fn, reads=reads, writes=writes)
            else:
                rec[0].append(("dma", key, fn, tuple(reads), tuple(writes)))

        def zip_lists(a, b):
            out = []
            for i in range(max(len(a), len(b))):
                if i < len(a):
                    out.append(a[i])
                if i < len(b):
                    out.append(b[i])
            return out

        def merge_emit(lists):
            lists = [L for L in lists if L]
            idx = [0] * len(lists)
            while True:
                best, bf = None, 2.0
                for li, L in enumerate(lists):
                    if idx[li] < len(L):
                        f = idx[li] / len(L)
                        if f < bf:
                            best, bf = li, f
                if best is None:
                    break
                it = lists[best][idx[best]]
                idx[best] += 1
                if it[0] == "op":
                    S.op(it[1], it[2], it[3], it[4])
                else:
                    S.dma(it[1], it[2], reads=it[3], writes=it[4])

        def pk(b):
            return f"ps{b}"

        def ps_f(b):
            return pbank[b]

        def ps_bf(b):
            return pbank[b][:].bitcast(BF16)

        def mm(out, lhsT, rhs, start, stop, reads, writes, skip=False):
            if skip:
                emit_op("pe", lambda e: e.matmul(out, lhsT=lhsT, rhs=rhs, start=start, stop=stop,
                                                 skip_group_check=True), reads, writes)
            else:
                emit_op("pe", lambda e: e.matmul(out, lhsT=lhsT, rhs=rhs, start=start, stop=stop), reads, writes)

        def tr(out, in_, reads, writes):
            emit_op("pe", lambda e: e.transpose(out=out, in_=in_, identity=ident[:]), list(reads) + ["ident"], writes)

        def act(out, in_, func, reads, writes, bias=None, scale=None, accum=None):
            kw = {}
            if bias is not None:
                kw["bias"] = bias
            if scale is not None:
                kw["scale"] = scale
            if accum is not None:
                kw["accum_out"] = accum
            emit_op("act", lambda e: e.activation(out=out, in_=in_, func=func, **kw), reads, writes)

        def tt(eng, out, in0, in1, op, reads, writes):
            emit_op(eng, lambda e: e.tensor_tensor(out=out, in0=in0, in1=in1, op=op), reads, writes)

        def ts(eng, out, in0, s1, op0, reads, writes, s2=None, op1=None):
            if op1 is None:
                emit_op(eng, lambda e: e.tensor_scalar(out=out, in0=in0, scalar1=s1, scalar2=None, op0=op0), reads, writes)
            else:
                emit_op(eng, lambda e: e.tensor_scalar(out=out, in0=in0, scalar1=s1, scalar2=s2, op0=op0, op1=op1),
                        reads, writes)

        def stt(eng, out, in0, scalar, in1, op0, op1, reads, writes):
            emit_op(eng, lambda e: e.scalar_tensor_tensor(out=out, in0=in0, scalar=scalar, in1=in1, op0=op0, op1=op1),
                    reads, writes)

        def cp(eng, out, in_, reads, writes):
            emit_op(eng, lambda e: e.tensor_copy(out, in_), reads, writes)

        def ms(eng, ap, val, writes):
            emit_op(eng, lambda e: e.memset(ap, val), (), writes)

        def recip(out, in_, reads, writes):
            emit_op("dve", lambda e: e.reciprocal(out, in_), reads, writes)

        def cc(i):
            return cst[:, i:i + 1]

        emit_dma("ld_c0", lambda e: e.dma_start(out=cst[:], in_=cst_d), writes=["cst"])
        emit_dma("ld_c1", lambda e: e.dma_start(out=cm_f[:].rearrange("p a b -> p (a b)"), in_=cmat), writes=["cm_f"])
        triu, tril, Tf, Tb, negones, Tm = (cm_f[:, i, :] for i in range(6))
        tt("dve", ident[:], cm_f[:, 0, :], cm_f[:, 1, :], ALU.mult, ["cm_f"], ["ident"])
        ms("pool", bones[:], 1.0, ["bones"])
        ms("pool", bones[0:64, 64:128], 0.0, ["bones"])
        ms("pool", bones[64:128, 0:64], 0.0, ["bones"])
        ts("dve", qg8[:, 0:1], cc(C_QG), 0.125, ALU.mult, ["cst"], ["qg8"], s2=cc(C_HLO), op1=ALU.mult)
        ts("dve", qg8[:, 1:2], cc(C_QG), 0.125, ALU.mult, ["cst"], ["qg8"], s2=cc(C_HHI), op1=ALU.mult)
        ms("pool", va[:].rearrange("p a b c -> p (a b c)"), 1.0, [f"va{i}" for i in range(KR)])
        ms("pool", va_m[:].rearrange("p b c -> p (b c)"), 1.0, ["va_m"])
        ms("pool", prem[:].rearrange("p a b -> p (a b)"), 0.0, ["prem"])
        ms("pool", Cf[:].rearrange("p a b c -> p (a b c)"), 0.0, ["Cf0"])
        ms("pool", xnT[:].rearrange("p s a b -> p (s a b)"), 0.0, ["xnT0", "xnT1"])

        PIECES = [(i * 512, min(512, CIN - i * 512)) for i in range((CIN + 511) // 512)]
        STG = [(xr, 0, "xr0", "ld_w0"), (xr, 512, "xr1", "ld_w1"), (xt, 0, "xt0", "ld_w2"), (xt, 512, "xt1", "ld_w3")]
        k = 0
        for kc in range(8):
            for (c0, cn) in PIECES:
                buf, off, bkey, dkey = STG[k % 4]
                on_dve = (k % 2 == 0)
                k += 1
                stg = buf[:, off:off + cn]
                emit_dma(dkey, lambda e, kc=kc, c0=c0, cn=cn, stg=stg: e.dma_start(
                    out=stg, in_=wi[kc * 128:(kc + 1) * 128, c0:c0 + cn]), writes=[bkey])
                if on_dve:
                    ts("dve", w_bf[:, kc, c0:c0 + cn], stg, cc(C_NG + kc), ALU.mult, [bkey, "cst"], ["w_bf"])
                else:
                    act(w_bf[:, kc, c0:c0 + cn], stg, AF.Copy, [bkey, "cst"], ["w_bf"], scale=cc(C_NG + kc))
        for kc in range(8):
            for half in range(2):
                buf, off, bkey, dkey = STG[k % 4]
                on_dve = (k % 2 == 0)
                k += 1
                srcw = buf[:, off:off + 512]
                emit_dma(dkey, lambda e, kc=kc, half=half, srcw=srcw: e.dma_start(
                    out=srcw, in_=wo[kc * 128:(kc + 1) * 128, half * 512:(half + 1) * 512]), writes=[bkey])
                dstw = wo_bf[:, kc, half * 512:(half + 1) * 512]
                if on_dve:
                    if kc < 4:
                        ts("dve", dstw, srcw, cc(C_MG + kc), ALU.mult, [bkey, "cst"], ["wo_bf"])
                    else:
                        cp("dve", dstw, srcw, [bkey], ["wo_bf"])
                else:
                    if kc < 4:
                        act(dstw, srcw, AF.Copy, [bkey, "cst"], ["wo_bf"], scale=cc(C_MG + kc))
                    else:
                        act(dstw, srcw, AF.Copy, [bkey], ["wo_bf"])
        Ff = Ftab[:].rearrange("p h i q -> p (h i q)")
        for j in range(16):
            (b0, o0, k0, d0), (b1, o1, k1, d1) = (STG[0], STG[1]) if j % 2 == 0 else (STG[2], STG[3])
            raw, msk = b0[:, o0:o0 + 512], b1[:, o1:o1 + 512]
            emit_dma(d0, lambda e, j=j, raw=raw: e.dma_start(out=raw, in_=fraw[:, j * 512:(j + 1) * 512]), writes=[k0])
            emit_dma(d1, lambda e, j=j, msk=msk: e.dma_start(out=msk, in_=fmsk[:, j * 512:(j + 1) * 512]), writes=[k1])
            act(raw, raw, AF.Exp, [k0], [k0])
            tt("dve", Ff[:, j * 512:(j + 1) * 512], raw, msk, ALU.mult, [k0, k1], ["Ftab"])

        def stage_X(src_ap, xs_=0, t=None, halo_dst=None):
            emit_dma("ld_x", lambda e: e.dma_start(out=xt[:], in_=src_ap), writes=["xt", "xt0", "xt1"])
            act(xnb[:], xt[:], AF.Square, ["xt"], ["st1a", "xnb"], scale=1.0 / 32, accum=st1[:, 0:1])
            act(st1[:, 1:2], st1[:, 0:1], AF.Ln, ["st1a"], ["st1b"], bias=EPS)
            act(st1[:, 2:3], st1[:, 1:2], AF.Exp, ["st1b"], ["st1c"], scale=-0.5)
            ts("dve", xnb[:], xt[:], st1[:, 2:3], ALU.mult, ["xt", "st1c"], ["xnb"])
            b = nb()
            for kc in range(8):
                tr(ps_bf(b)[:, kc * 128:(kc + 1) * 128], xnb[:, kc * 128:(kc + 1) * 128], ["xnb"], [pk(b)])
            if halo_dst is not None:
                cp("dve", halo_dst, ps_bf(b).rearrange("p (a b) -> p a b", a=8), [pk(b)], ["xhT"])
                return
            cp("dve", xnT[:, xs_, :, 2:130], ps_bf(b).rearrange("p (a b) -> p a b", a=8), [pk(b)], [f"xnT{xs_}"])
            if t is not None:
                cp("pool", xnT[:, xs_, :, 0:2], xhT[:, :, 4 * t:4 * t + 2], ["xhT"], [f"xnT{xs_}"])
                cp("pool", xnT[:, xs_, :, 130:132], xhT[:, :, 4 * t + 2:4 * t + 4], ["xhT"], [f"xnT{xs_}"])

        def stage_A(mode, xs_=0, t=None, slot_q=None, slot_k=None, va_dst=None, kn_dst=None, va_key="va_m",
                    ctx=None, ab=0, slot_z=0):
            def enter(name):
                if ctx is not None:
                    setctx(*ctx[name])
            full = mode == "full"
            xk = f"xnT{xs_}"
            vmv, gtv, vmk, gtk = vm[:, ab, :], gt[:, ab, :], f"vm{ab}", f"gt{ab}"
            qkTs, sos, szms = qkT[:, ab], so[:, ab, :], szm[:, ab]
            qk_key, so_key, szm_key = f"qkT{ab}", f"so{ab}", f"szm{ab}"
            ks = 0 if full else ab
            ktoks, g8s, d8s, scs, eb8s, dec8s = ktok[:, ks, :], g8[:, ks, :], d8[:, ks, :], sc[:, ks], eb8[:, ks, :], dec8[:, ks, :]
            kk, gk, dk, sk_, ek_, dck = f"ktok{ks}", f"g8{ks}", f"d8{ks}", f"sc{ks}", f"eb8{ks}", f"dec8{ks}"
            vslot = 2 if full else 1 + ab

            def pg(g):
                return g if full else (g - 4 + 4 * ab)
            enter("main")

            def fm_group(bank, pos, col0, n0, n1):
                n = n1 - n0
                for kc in range(8):
                    mm(ps_f(bank)[:, pos * n:(pos + 1) * n], w_bf[:, kc, col0:col0 + 128], xnT[:, xs_, kc, n0:n1],
                       kc == 0, kc == 7, ["w_bf", xk], [pk(bank)])

            def tm_proj(bank, col0, ncol):
                for kc in range(8):
                    mm(ps_f(bank)[:, 0:ncol], xnT[:, xs_, kc, 2:130], w_bf[:, kc, col0:col0 + ncol],
                       kc == 0, kc == 7, ["w_bf", xk], [pk(bank)])

            groups = list(range(8)) if full else [4, 5, 6, 7]
            gi = 0
            while gi < len(groups):
                grp = groups[gi:gi + 3]
                gi += 3
                bk = nb()
                for pos, g in enumerate(grp):
                    fm_group(bk, pos, g * 128, 0, 132)
                for pos, g in enumerate(grp):
                    src = ps_f(bk)[:, pos * 132:(pos + 1) * 132]
                    act(acc[:, pg(g), :], src[:, 0:128], AF.Copy, [pk(bk), "cst"], [f"acc{pg(g)}"], scale=cc(C_CW + g * 5))
                    if full and g >= 4 and t is None:
                        act(prem[:, g - 4, 2:18], src[:, 2:18], AF.Copy, [pk(bk)], ["prem"])
                    if full and g >= 4 and t is not None and t % TPU == 0:
                        act(prem[:, g - 4, 18:20], src[:, 2:4], AF.Copy, [pk(bk)], ["prem"])
                order = ([(j, pos, g) for j in range(1, 5) for pos, g in enumerate(grp)] if OPT_TAPS else
                         [(j, pos, g) for pos, g in enumerate(grp) for j in range(1, 5)])
                for (j, pos, g) in order:
                    src = ps_f(bk)[:, pos * 132:(pos + 1) * 132]
                    stt("dve", acc[:, pg(g), :], src[:, j:j + 128], cc(C_CW + g * 5 + j), acc[:, pg(g), :],
                        ALU.mult, ALU.add, [pk(bk), "cst", f"acc{pg(g)}"], [f"acc{pg(g)}"])
            bv = nb()
            tm_proj(bv, VM, 512)
            act(vmv, ps_f(bv)[:, :], AF.Copy, [pk(bv)], [vmk])
            bg = nb()
            tm_proj(bg, GT, 16)
            tt("dve", gtv, ps_f(bg)[:, 0:16], cst[:, C_BG:C_BG + 16], ALU.add, [pk(bg), "cst"], [gtk])
            if full:
                bo = nb()
                tm_proj(bo, OM, 512)
                act(sos, ps_f(bo)[:, :], AF.Tanh, [pk(bo)], [so_key], scale=0.5)
                ts("pool", sos, sos, 0.5, ALU.mult, [so_key], [so_key], s2=0.5, op1=ALU.add)
                bz = nb()
                for g in range(4):
                    fm_group(bz, g, ZM + g * 128, 2, 130)
                act(szms.rearrange("p a b -> p (a b)"), ps_f(bz)[:, :], AF.Silu, [pk(bz)], [szm_key])
                bz = nb()
                for g in range(4):
                    fm_group(bz, g, ZA + g * 128, 2, 130)
                act(sza[:, slot_z].rearrange("p a b -> p (a b)"), ps_f(bz)[:, :], AF.Silu, [pk(bz)], [f"sza{slot_z}"])
                act(qkTs.rearrange("p a b -> p (a b)"), acc[:].rearrange("p a b -> p (a b)"), AF.Silu,
                    [f"acc{g}" for g in range(8)], [qk_key])
                enter("norm")
                bva = nb()
                tm_proj(bva, VA, 512)
                cp("dve", va_dst[:, :, 0:64], ps_f(bva)[:, :].rearrange("p (h d) -> p h d", h=8), [pk(bva)], [va_key])
                for (col, isq, dkey) in ((QA, True, f"qnT{slot_q}"), (KA, False, f"knT{slot_k}")):
                    if isq and kn_dst is not None:
                        continue
                    bq = nb()
                    for g in range(4):
                        fm_group(bq, g, col + g * 128, 2, 130)
                    act(sqb[:].rearrange("p a b -> p (a b)"), ps_f(bq)[:, :], AF.Square, [pk(bq)], ["sqb"])
                    bs = nb()
                    for g in range(4):
                        mm(ps_f(bs)[:, g * 128:(g + 1) * 128], bones[:], sqb[:, g, :], True, True,
                           ["bones", "sqb"], [pk(bs)])
                    act(rsn[:], ps_f(bs)[:, :], AF.Ln, [pk(bs)], ["rsn"], scale=1.0 / 64, bias=EPS)
                    act(rsn[:], rsn[:], AF.Exp, ["rsn"], ["rsn"], scale=-0.5)
                    if isq:
                        for par in range(2):
                            stt("dve", qnT[:, slot_q, par].rearrange("p a b -> p (a b)"), ps_f(bq)[:, :], qg8[:, par:par + 1],
                                rsn[:], ALU.mult, ALU.mult, [pk(bq), "rsn", "qg8"], [dkey])
                    else:
                        dst = kn_dst if kn_dst is not None else knT[:, slot_k]
                        stt("dve", dst.rearrange("p a b -> p (a b)"), ps_f(bq)[:, :], cc(C_KG), rsn[:], ALU.mult, ALU.mult,
                            [pk(bq), "rsn", "cst"], [dkey])
            enter("tail_k")
            if not full:
                act(qkTs[:, 4:8, :].rearrange("p a b -> p (a b)"), acc[:, 4 * ab:4 * ab + 4, :].rearrange("p a b -> p (a b)"),
                    AF.Silu, [f"acc{4 * ab + g}" for g in range(4)], [qk_key])
            bt = nb()
            for g in range(4):
                tr(ps_bf(bt)[:, g * 128:(g + 1) * 128], qkTs[:, 4 + g, :], [qk_key], [pk(bt)])
            cp("dve", ktoks, ps_bf(bt)[:, 0:512], [pk(bt)], [kk])
            enter("tail_g")
            gv = gtv.rearrange("p (d k h) -> p d k h", d=2, k=2)
            f_view = gv[:, :, 1, :]
            i_view = gv[:, :, 0, :]
            g8v = g8s.rearrange("p (a b) -> p a b", a=2)
            act(g8v, f_view, AF.Exp, [gtk], [gk], scale=-1.0)
            act(g8s, g8s, AF.Ln, [gk], [gk], bias=1.0)
            bc = nb()
            mm(ps_f(bc)[:, 0:4], Tf, g8s[:, 0:4], True, True, ["cm_f", gk], [pk(bc)])
            mm(ps_f(bc)[:, 4:8], Tb, g8s[:, 4:8], True, True, ["cm_f", gk], [pk(bc)])
            mm(ps_f(bc)[:, 8:16], negones, g8s[:, 0:8], True, True, ["cm_f", gk], [pk(bc)])
            tt("dve", d8s.rearrange("p (a b) -> p a b", a=2), i_view, ps_f(bc)[:, 0:8].rearrange("p (a b) -> p a b", a=2),
               ALU.subtract, [gtk, pk(bc)], [dk])
            act(scs[:, 0:2, :].rearrange("p a b -> p (a b)"), d8s, AF.Exp, [dk], [sk_])
            act(eb8s, ps_f(bc)[:, 0:8], AF.Exp, [pk(bc)], [ek_], scale=-1.0, bias=float(0.5 * np.log(128.0)))
            act(dec8s, ps_f(bc)[:, 8:16], AF.Exp, [pk(bc)], [dck])
            tt("dve", scs[:, 2:4, :].rearrange("p a b -> p (a b)"), scs[:, 0:2, :].rearrange("p a b -> p (a b)"), dec8s,
               ALU.mult, [sk_, dck], [sk_])
            vm3 = vmv.rearrange("p (h d) -> p h d", h=4)
            if full:
                vm_b = vm3.unsqueeze(1).to_broadcast([128, 2, 4, 128])
                sc_b = scs[:, 0:2, :].unsqueeze(3).to_broadcast([128, 2, 4, 128])
                tt("dve", V4[:, 0:2, :, 0:128], vm_b, sc_b, ALU.mult, [vmk, sk_], ["V4u"])
                cp("dve", V4[:, 0:2, :, 128], scs[:, 0:2, :], [sk_], ["V4u"])
                tt("pool", V4[:, 2, :, 0:128], vm3, scs[:, 2, :].unsqueeze(2).to_broadcast([128, 4, 128]), ALU.mult,
                   [vmk, sk_], ["V4w2"])
                cp("pool", V4[:, 2, :, 128], scs[:, 2, :], [sk_], ["V4w2"])
            else:
                tt("dve", V4[:, vslot, :, 0:128], vm3, scs[:, 3, :].unsqueeze(2).to_broadcast([128, 4, 128]), ALU.mult,
                   [vmk, sk_], [f"V4w{vslot}"])
                cp("dve", V4[:, vslot, :, 128], scs[:, 3, :], [sk_], [f"V4w{vslot}"])

        def state_update(cur, vk, deccol0, ks=0):
            nxt = cur
            for j in range(2):
                bk = nb()
                for hh in range(2):
                    h = 2 * j + hh
                    mm(ps_f(bk)[:, hh * 129:(hh + 1) * 129], ktok[:, ks, h * 128:(h + 1) * 128], V4[:, vk, h, :],
                       True, True, [f"ktok{ks}", f"V4w{vk}"], [pk(bk)])
                for hh in range(2):
                    h = 2 * j + hh
                    stt("dve", Cf[:, nxt, h, :], Cf[:, cur, h, :], dec8[:, ks, deccol0 + h:deccol0 + h + 1],
                        ps_f(bk)[:, hh * 129:(hh + 1) * 129], ALU.mult, ALU.add,
                        [f"Cf{cur}", f"dec8{ks}", pk(bk)], [f"Cf{nxt}"])
            return nxt

        if phase >= 1:
            for i in range(2):
                stage_X(xh[i * 128:(i + 1) * 128, :], halo_dst=xhT[:, :, i * 128:(i + 1) * 128])
            emit_dma("ld_x", lambda e: e.dma_start(out=xt[:], in_=xh[256:384, :]), writes=["xt", "xt0", "xt1"])
            act(xnb[:], xt[:], AF.Square, ["xt"], ["st1a", "xnb"], scale=1.0 / 32, accum=st1[:, 0:1])
            act(st1[:, 1:2], st1[:, 0:1], AF.Ln, ["st1a"], ["st1b"], bias=EPS)
            act(st1[:, 2:3], st1[:, 1:2], AF.Exp, ["st1b"], ["st1c"], scale=-0.5)
            ts("dve", xnb[:], xt[:], st1[:, 2:3], ALU.mult, ["xt", "st1c"], ["xnb"])
            b = nb()
            for kc in range(8):
                tr(ps_bf(b)[:, kc * 128:(kc + 1) * 128], xnb[:, kc * 128:(kc + 1) * 128], ["xnb"], [pk(b)])
            cp("dve", xhT[:, :, 256:320], ps_bf(b).rearrange("p (a b) -> p a b", a=8)[:, :, 0:64], [pk(b)], ["xhT"])
            stage_X(xh[384:512, :], xs_=0, t=None)
            stage_A("full", xs_=0, t=None, slot_q=0, slot_k=0, va_dst=va_m[:], kn_dst=knT[:, 0])
            cp("dve", kmT[:], knT[:, 0, :, 0:16], ["knT0"], ["kmT"])
            bm = nb()
            mm(ps_f(bm)[:, 0:4], Tm, g8[:, 0, 0:4], True, True, ["cm_f", "g80"], [pk(bm)])
            tt("dve", d8[0:16, 0, 0:4], gt[0:16, 0, 0:4], ps_f(bm)[0:16, 0:4], ALU.add, ["gt0", pk(bm)], ["d80"])
            act(wsm[:], d8[0:16, 0, 0:4], AF.Exp, ["d80"], ["wsm"])
            tt("dve", Vwm[:, :, 0:128], vm[0:16, 0, :].rearrange("p (h d) -> p h d", h=4),
               wsm[:].unsqueeze(2).to_broadcast([16, 4, 128]), ALU.mult, ["vm0", "wsm"], ["Vwm"])
            cp("dve", Vwm[:, :, 128], wsm[:], ["wsm"], ["Vwm"])

        cur = 0
        s1_tiles = list(range(NT - 1, (s1_stop - 1) if phase >= 2 else NT - 1, -1))
        proc = [t for t in s1_tiles if t != s1_stop]
        junk = []
        pairs = [proc[k:k + 2] for k in range(0, len(proc), 2)]

        def a_bwd(t, keep):
            ctx = {}
            for name in ("main", "norm", "tail_k", "tail_g"):
                ctx[name] = keep.get(name, (junk, "junk"))
            stage_A("bwd", xs_=t % 2, t=t, ab=t % 2, ctx=ctx)
            del junk[:]

        if pairs:
            setctx(None, "A")
            for t in pairs[0]:
                stage_X(xs[t * 128:(t + 1) * 128, :], xs_=t % 2, t=t)
        for pi, pr in enumerate(pairs):
            setctx(None, "A")
            for t in pr:
                a_bwd(t, {"main": (None, "A")})
            chains, La, Lx = [], [], []
            for j, t in enumerate(pr):
                Lk, Lg = [], []
                a_bwd(t, {"tail_k": (Lk, "L1" if j == 0 else "C2"), "tail_g": (Lg, "S1g")})
                chains.append(zip_lists(Lk, Lg))
            setctx(La, "S1u")
            for t in pr:
                emit_dma("st_cb0", lambda e, t=t: e.dma_start(
                    out=cbs[t], in_=Cf[:, 0].rearrange("p a b -> p (a b)")), reads=["Cf0"])
                if t % TPU == TPU - 1 and t < NT - 1:
                    u = t // TPU
                    ts("dve", dec8[:, t % 2, 4:8], dec8[:, t % 2, 4:8], cc(C_KEEPB + u), ALU.mult,
                       [f"dec8{t % 2}", "cst"], [f"dec8{t % 2}"])
                cur = state_update(cur, 1 + t % 2, 4, ks=t % 2)
            if pi + 1 < len(pairs):
                setctx(Lx, "L3")
                for t in pairs[pi + 1]:
                    stage_X(xs[t * 128:(t + 1) * 128, :], xs_=t % 2, t=t)
            setctx(None, "A")
            merge_emit([(zip_lists(chains[0], chains[1]) if len(chains) > 1 else chains[0]) + La, Lx])
        if s1_tiles and s1_tiles[-1] == s1_stop:
            setctx(None, "A")
            emit_dma("st_cb0", lambda e: e.dma_start(
                out=cbs[s1_stop], in_=Cf[:, 0].rearrange("p a b -> p (a b)")), reads=["Cf0"])

        ms("pool", Cf[:, 0].rearrange("p a b -> p (a b)"), 0.0, ["Cf0"])
        ms("pool", C_bf[:].rearrange("p a b -> p (a b)"), 0.0, ["C_bf"])
        cur = 0

        kmtok = hn[0:16].rearrange("p a b -> p (a b)")

        Vwmu = Sfb[0:16].rearrange("p a b c -> p (a b c)")[:, 0:516].rearrange("p (a b) -> p a b", a=4)

        def stage_B(t):
            nonlocal cur
            sb_ = t % 2
            qkTs, sos, szms = qkT[:, sb_], so[:, sb_, :], szm[:, sb_]
            qk_key, so_key, szm_key = f"qkT{sb_}", f"so{sb_}", f"szm{sb_}"
            if t % TPU == 0:
                u = t // TPU
                for g in range(4):
                    ts("dve", accm[:, g, :], prem[:, g, 0:16], cc(C_CW + (4 + g) * 5), ALU.mult, ["prem", "cst"], ["accm"])
                    for j in range(1, 5):
                        stt("dve", accm[:, g, :], prem[:, g, j:j + 16], cc(C_CW + (4 + g) * 5 + j), accm[:, g, :],
                            ALU.mult, ALU.add, ["prem", "cst", "accm"], ["accm"])
                act(kmm[:].rearrange("p a b -> p (a b)"), accm[:].rearrange("p a b -> p (a b)"), AF.Silu, ["accm"], ["kmm"])
                bt = nb()
                for g in range(4):
                    tr(ps_bf(bt)[0:16, g * 128:(g + 1) * 128], kmm[:, g, :], ["kmm"], [pk(bt)])
                cp("dve", kmtok, ps_bf(bt)[0:16, 0:512], [pk(bt)], ["hn"])
                ts("dve", Vwmu, Vwm[:], cst[0:16, C_NKEEP + u:C_NKEEP + u + 1], ALU.mult, ["Vwm", "cst"], ["Sf", "Sb"])
                nxt = cur
                for j in range(2):
                    bk = nb()
                    for hh in range(2):
                        h = 2 * j + hh
                        mm(ps_f(bk)[:, hh * 129:(hh + 1) * 129], kmtok[:, h * 128:(h + 1) * 128], Vwmu[:, h, :],
                           True, True, ["hn", "Sf", "Sb"], [pk(bk)])
                    for hh in range(2):
                        h = 2 * j + hh
                        stt("dve", Cf[:, nxt, h, :], Cf[:, cur, h, :], cc(C_KEEP + u),
                            ps_f(bk)[:, hh * 129:(hh + 1) * 129], ALU.mult, ALU.add,
                            [f"Cf{cur}", "cst", pk(bk)], [f"Cf{nxt}"])
                cur = nxt
                act(C_bf[:].rearrange("p a b -> p (a b)"), Cf[:, cur].rearrange("p a b -> p (a b)"), AF.Copy,
                    [f"Cf{cur}"], ["C_bf"])
            emit_dma("ld_cb", lambda e: e.dma_start(out=cb[:].rearrange("p a b -> p (a b)"), in_=cbs[t]),
                     reads=["cbsA", "cbsB"], writes=["cb"])
            if t % TPU == TPU - 1 and t < NT - 1:
                act(cb_bf[:].rearrange("p a b -> p (a b)"), cb[:].rearrange("p a b -> p (a b)"), AF.Copy,
                    ["cb", "cst"], ["cb_bf"], scale=cc(C_KEEPB + t // TPU))
            else:
                act(cb_bf[:].rearrange("p a b -> p (a b)"), cb[:].rearrange("p a b -> p (a b)"), AF.Copy, ["cb"], ["cb_bf"])
            bs = nb()
            for h in range(4):
                mm(ps_f(bs)[:, h * 128:(h + 1) * 128], qkTs[:, 4 + h, :], qkTs[:, h, :], True, True, [qk_key], [pk(bs)])
            sview = ps_f(bs)[:, :].rearrange("p (h j) -> p h j", h=4)
            tt("dve", Sfb[:, 0], sview, triu.unsqueeze(1).to_broadcast([128, 4, 128]), ALU.mult, [pk(bs), "cm_f"], ["Sf"])
            tt("dve", Sfb[:, 1], sview, tril.unsqueeze(1).to_broadcast([128, 4, 128]), ALU.mult, [pk(bs), "cm_f"], ["Sb"])
            for d in range(2):
                banks = []
                for j in range(2):
                    bk = nb()
                    banks.append(bk)
                    for hh in range(2):
                        h = 2 * j + hh
                        o = ps_f(bk)[:, hh * 129:(hh + 1) * 129]
                        mm(o, Sfb[:, d, h, :], V4[:, d, h, :], True, False, ["Sf" if d == 0 else "Sb", "V4u"], [pk(bk)])
                        mm(o, qkTs[:, h, :], (C_bf if d == 0 else cb_bf)[:, h, :], False, True,
                           [qk_key, "C_bf" if d == 0 else "cb_bf"], [pk(bk)])
                    dv = ps_f(bk)[:, 0:258].rearrange("p (a b) -> p a b", a=2)[:, :, 128]
                    act(da8[:, d * 4 + 2 * j:d * 4 + 2 * j + 2], dv, AF.Abs, [pk(bk)], [f"da8{d}"])
                dsl = da8[:, d * 4:d * 4 + 4]
                tt("dve", dsl, dsl, eb8[:, 0, d * 4:d * 4 + 4], ALU.max, [f"da8{d}", "eb80"], [f"da8{d}"])
                recip(dsl, dsl, [f"da8{d}"], [f"da8{d}"])
                for j in range(2):
                    bk = banks[j]
                    nv = ps_f(bk)[:, 0:258].rearrange("p (a b) -> p a b", a=2)[:, :, 0:128]
                    if d == 0:
                        tt("dve", hf[:, 2 * j:2 * j + 2, :], nv,
                           da8[:, 2 * j:2 * j + 2].unsqueeze(2).to_broadcast([128, 2, 128]),
                           ALU.mult, [pk(bk), "da80"], ["hf"])
                    else:
                        for hh in range(2):
                            h = 2 * j + hh
                            stt("dve", hf[:, h, :], nv[:, hh, :], da8[:, 4 + h:5 + h], hf[:, h, :], ALU.mult, ALU.add,
                                [pk(bk), "da81", "hf"], ["hf"])
            hfv = hf[:].rearrange("p a b -> p (a b)")
            tt("dve", hfv, hfv, sos, ALU.mult, ["hf", so_key], ["hf"])
            for h in range(4):
                act(hn[:, h, :], hf[:, h, :], AF.Square, ["hf"], ["hn", "ss4"], scale=float(128.0 ** -0.5),
                    accum=ss4[:, h:h + 1])
            act(r4[:], ss4[:], AF.Ln, ["ss4"], ["r4"], bias=EPS)
            act(r4[:], r4[:], AF.Exp, ["r4"], ["r4"], scale=-0.5)
            tt("dve", hn[:], hf[:], r4[:].unsqueeze(2).to_broadcast([128, 4, 128]), ALU.mult, ["hf", "r4"], ["hn"])
            bt = nb()
            for g in range(4):
                tr(ps_bf(bt)[:, g * 128:(g + 1) * 128], hn[:, g, :], ["hn"], [pk(bt)])
            tt("dve", ymT[:, t % QR].rearrange("p a b -> p (a b)"), ps_bf(bt)[:, 0:512],
               szms.rearrange("p a b -> p (a b)"), ALU.mult, [pk(bt), szm_key], [f"ymT{t % QR}"])
            cur = state_update(cur, 2, 0)
            act(C_bf[:].rearrange("p a b -> p (a b)"), Cf[:, cur].rearrange("p a b -> p (a b)"), AF.Copy,
                [f"Cf{cur}"], ["C_bf"])

        def stage_C(tq):
            sq = tq % QR
            bms = [nb(), nb()]
            for h in range(8):
                g, par = h // 2, h % 2
                mm(ps_f(bms[h // 4])[0:16, (h % 4) * 128:(h % 4 + 1) * 128], kmT[:, g, :],
                   qnT[:, sq, par, g, :], True, True, ["kmT", f"qnT{sq}"], [pk(bms[h // 4])])
            for j in range(2):
                act(Em[:, 4 * j:4 * j + 4, :].rearrange("p a b -> p (a b)"), ps_f(bms[j])[0:16, :], AF.Exp,
                    [pk(bms[j])], ["Em"])
            bpv = [6, 7]
            started = [False, False]

            def pv(h, lhsT, rhs, reads, last):
                j = h // 4
                st = not started[j]
                started[j] = True
                mm(ps_f(bpv[j])[:, (h % 4) * 65:(h % 4 + 1) * 65], lhsT, rhs, st, last,
                   reads, [pk(bpv[j])], skip=True)

            tiles = PLAN[tq]
            for h in range(8):
                pv(h, Em[0:16, h, :], va_m[0:16, h, :], ["Em", "va_m"], False)
            for idx, (a, halves) in enumerate(tiles):
                sk = a % KR
                es_ = idx % 2
                Ebs, Pbs = Eb[:, es_], Pb[:, 0]
                ek = [f"Eb{es_}0", f"Eb{es_}1"]
                pkey = "Pb"
                for j in range(2):
                    bq = nb()
                    for hh in range(4):
                        h = 4 * j + hh
                        g, par = h // 2, h % 2
                        mm(ps_f(bq)[:, hh * 128:(hh + 1) * 128], knT[:, sk, g, :],
                           qnT[:, sq, par, g, :], True, True, [f"knT{sk}", f"qnT{sq}"], [pk(bq)])
                    act(Ebs[:, 4 * j:4 * j + 4, :].rearrange("p a b -> p (a b)"), ps_f(bq)[:, :], AF.Exp,
                        [pk(bq)], [ek[j]])
                i0 = 7 - (2 * a - 2 * tq)
                both_full = all(hv is not None and hv[0] == "s" and hv[1] == 0 and hv[2] == 128 for hv in halves)
                if both_full:
                    tt("dve", Pbs.rearrange("p h (r q) -> p h r q", r=2),
                       Ebs.rearrange("p h (r q) -> p h r q", r=2), Ftab[:, :, i0:i0 + 2, :], ALU.mult,
                       ek + ["Ftab"], [pkey])
                else:
                    for hr in (0, 1):
                        hv = halves[hr]
                        dstp = Pbs[:, :, hr * 64:(hr + 1) * 64]
                        srcp = Ebs[:, :, hr * 64:(hr + 1) * 64]
                        if hv is None:
                            ts("dve", dstp, srcp, 0.0, ALU.mult, ek, [pkey])
                        elif hv[0] == "s" and hv[1] == 0 and hv[2] == 128:
                            tt("dve", dstp, srcp, Ftab[:, :, i0 + hr, :], ALU.mult, ek + ["Ftab"], [pkey])
                        else:
                            if hv[0] == "s":
                                mcol = C_HLO if hv[1] == 0 else C_HHI
                            else:
                                mcol = C_CM + hv[1]
                            stt("dve", dstp, srcp, cc(mcol), Ftab[:, :, i0 + hr, :], ALU.mult, ALU.mult,
                                ek + ["Ftab", "cst"], [pkey])
                for h in range(8):
                    pv(h, Pbs[:, h, :], va[:, sk, h, :], [pkey, f"va{sk}"], idx == len(tiles) - 1)
            for j in range(2):
                v3 = ps_f(bpv[j])[:, 0:260].rearrange("p (h d) -> p h d", h=4)
                recip(rd8[:, 4 * j:4 * j + 4], v3[:, :, 64], [pk(bpv[j])], [f"rd8{j}"])
                tt("dve", ya[:, 4 * j:4 * j + 4, :], v3[:, :, 0:64],
                   rd8[:, 4 * j:4 * j + 4].unsqueeze(2).to_broadcast([128, 4, 64]), ALU.mult,
                   [pk(bpv[j]), f"rd8{j}"], ["ya"])
            bt = nb()
            yav = ya[:].rearrange("p h d -> p (h d)")
            for g in range(4):
                tr(ps_bf(bt)[:, g * 128:(g + 1) * 128], yav[:, g * 128:(g + 1) * 128], ["ya"], [pk(bt)])
            tt("dve", yaT[:].rearrange("p a b -> p (a b)"), ps_bf(bt)[:, 0:512],
               sza[:, tq % SZR].rearrange("p a b -> p (a b)"), ALU.mult, [pk(bt), f"sza{tq % SZR}"], ["yaT"])
            emit_dma("ld_xr", lambda e: e.dma_start(out=xr[:], in_=xs[tq * 128:(tq + 1) * 128, :]), writes=["xr0", "xr1"])
            for n in range(2):
                bo = nb()
                for kt in range(8):
                    lhsT = ymT[:, sq, kt, :] if kt < 4 else yaT[:, kt - 4, :]
                    mm(ps_f(bo)[:, :], lhsT, wo_bf[:, kt, n * 512:(n + 1) * 512], kt == 0, kt == 7,
                       [f"ymT{sq}", "yaT", "wo_bf"], [pk(bo)])
                tt("dve", xr[:, n * 512:(n + 1) * 512], ps_f(bo)[:, :], xr[:, n * 512:(n + 1) * 512], ALU.add,
                   [pk(bo), f"xr{n}"], [f"xr{n}"])
            emit_dma("st_y", lambda e: e.dma_start(out=y[tq * 128:(tq + 1) * 128, :], in_=xr[:]), reads=["xr0", "xr1"])

        S.buf["cbsA"] = [("d", "st_cb0", S.dma_counts.get("st_cb0", 0)), []]
        S.buf["cbsB"] = [("d", "st_cb1", S.dma_counts.get("st_cb1", 0)), []]

        s2_end = s2_start + s2_tiles
        if phase >= 3:
            junk2 = []

            def a_call(i, keep):
                ctx = {}
                for name in ("main", "norm", "tail_k", "tail_g"):
                    ctx[name] = keep.get(name, (junk2, "junk"))
                stage_A("full", xs_=i % 2, t=i, slot_q=i % QR, slot_k=i % KR, va_dst=va[:, i % KR],
                        va_key=f"va{i % KR}", ctx=ctx, ab=i % 2, slot_z=i % SZR)
                del junk2[:]

            setctx(None, "A")
            stage_X(xs[s2_start * 128:(s2_start + 1) * 128, :], xs_=s2_start % 2, t=s2_start)
            for i in range(s2_start, s2_end + LAG):
                L1, L2, L3 = [], [], []
                if i < s2_end:
                    Lk, Lg = [], []
                    a_call(i, {"main": (None, "A")})
                    a_call(i, {"norm": (L2, "L2"), "tail_k": (Lk, "L1"), "tail_g": (Lg, "L1")})
                    L1.extend(zip_lists(Lk, Lg))
                    setctx(L1, "L1")
                    stage_B(i)
                if i - LAG >= s2_start and phase >= 4:
                    setctx(L2, "L2")
                    stage_C(i - LAG)
                if i + 1 < s2_end:
                    setctx(L3, "L3")
                    stage_X(xs[(i + 1) * 128:(i + 2) * 128, :], xs_=(i + 1) % 2, t=i + 1)
                setctx(None, "A")
                merge_emit([L1, L2, L3])

        S.finalize()
        with nc.Block() as block:
            @block.sync
            def _(e):
                S.run("sp", e, sems, final_dma_keys=[k for k in dma_keys if k.startswith("st_") and S.dma_counts.get(k)])

            @block.tensor
            def _(e):
                S.run("pe", e, sems)

            @block.scalar
            def _(e):
                S.run("act", e, sems)

            @block.vector
            def _(e):
                S.run("dve", e, sems)

            @block.gpsimd
            def _(e):
                S.run("pool", e, sems)
    return nc


def _core_streams(x_prompt, x_sample, c):
    if c < 2:
        xs = np.concatenate([x_sample[c], x_prompt[c]], axis=0)
        seq_starts = [0, 8192]
        seq_ends = [8192, 10240]
    else:
        i0 = 2 + 5 * (c - 2)
        xs = x_prompt[i0:i0 + 5].reshape(5 * 2048, DM)
        seq_starts = [2048 * u for u in range(5)]
        seq_ends = [2048 * (u + 1) for u in range(5)]
    return np.ascontiguousarray(xs), set(seq_starts), set(seq_ends)


def _host_layout(inputs):
    x_prompt = np.asarray(inputs["x_prompt"], np.float32)
    x_sample = np.asarray(inputs["x_sample"], np.float32)
    meta = np.asarray(inputs["meta_tokens"], np.float32)
    wi = np.ascontiguousarray(np.asarray(inputs["w_in"], np.float32)[0])
    wo = np.ascontiguousarray(np.asarray(inputs["w_out"], np.float32)[0])
    norm_g = np.asarray(inputs["norm_g"], np.float32)[0]
    b_gate = np.asarray(inputs["b_gate"], np.float32)[0]
    conv_w = np.asarray(inputs["conv_w"], np.float32)[0]
    mg = np.asarray(inputs["mlstm_norm_g"], np.float32)[0]
    qg = np.asarray(inputs["q_norm_g"], np.float32)[0]
    kg = np.asarray(inputs["k_norm_g"], np.float32)[0]
    rpb = np.asarray(inputs["rpb"], np.float32)[0]

    p = np.arange(128)
    rr, kc = p // 64, p % 64
    i = np.arange(16)
    qc = np.arange(64)
    dr = (7 - i)[None, :] + rr[:, None]
    colidx = kc[:, None] - qc[None, :] + 15
    cs = np.clip(qc - 8, 0, 48)
    allowed = (kc[:, None] >= cs[None, :]) & (kc[:, None] < cs[None, :] + 16)
    valid = (dr >= -7) & (dr <= 7)
    rowi = np.clip(dr + 7, 0, 14)
    coli = np.clip(colidx, 0, 30)
    fraw = rpb[:, rowi[:, :, None], coli[:, None, :]]
    fraw = np.ascontiguousarray(np.transpose(fraw, (1, 0, 2, 3))).reshape(128, 8 * 16 * 64).astype(np.float32)
    fm = (valid[:, :, None] & allowed[:, None, :]).astype(np.float32)
    fmsk = np.ascontiguousarray(np.broadcast_to(fm[:, None], (128, 8, 16, 64))).reshape(128, 8 * 16 * 64)

    s = np.arange(128)[:, None]
    j = np.arange(128)[None, :]
    cmat = np.stack([
        (s <= j), (s >= j), -(s <= j).astype(np.float32), -(s >= j).astype(np.float32),
        -np.ones((128, 128)), -((s > j) & (s <= 15)).astype(np.float32)], axis=1).astype(np.float32)
    cmat = np.ascontiguousarray(cmat).reshape(128, 6 * 128)

    in_maps = []
    for c in range(NCORES):
        xs, starts, ends = _core_streams(x_prompt, x_sample, c)
        xh = np.zeros((NPRE * 128, DM), np.float32)
        for t in range(NT):
            t0 = t * 128
            if t0 in starts:
                xh[4 * t:4 * t + 2] = meta[14:16]
            else:
                xh[4 * t:4 * t + 2] = xs[t0 - 2:t0]
            if t0 + 128 not in ends:
                xh[4 * t + 2:4 * t + 4] = xs[t0 + 128:t0 + 130]
        xh[384:400] = meta
        linked = c < 2
        cst = np.zeros((128, NCST), np.float32)
        cst[:, C_BG:C_BG + 16] = b_gate[None, :]
        for g in range(8):
            for jj in range(5):
                cst[:, C_CW + g * 5 + jj] = conv_w[jj, g * 128:(g + 1) * 128]
        for kc_ in range(8):
            cst[:, C_NG + kc_] = norm_g[kc_ * 128:(kc_ + 1) * 128]
        for g in range(4):
            cst[:, C_MG + g] = mg[g * 128:(g + 1) * 128]
        cst[:, C_QG] = qg[p % 64]
        cst[:, C_KG] = kg[p % 64]
        keep = [0, 1, 1, 1, 0] if linked else [0, 0, 0, 0, 0]
        for u in range(5):
            cst[:, C_KEEP + u] = keep[u]
            cst[:, C_NKEEP + u] = 1 - keep[u]
        for u in range(4):
            cst[:, C_KEEPB + u] = keep[u + 1]
        cst[0:64, C_HLO] = 1.0
        cst[64:128, C_HHI] = 1.0
        for ci, (R, a) in enumerate(CMS):
            w = _win_lnk(R) if linked else _win_unl(R)
            cst[:, C_CM + ci] = np.array([(2 * a + (pp // 64)) in w for pp in range(128)], np.float32)
        in_maps.append({"xs": xs, "xh": xh, "wi": wi, "wo": wo, "cst": cst, "fraw": fraw, "fmsk": fmsk, "cmat": cmat})
    return in_maps


_NC_CACHE = {}


def kernel(**inputs):
    in_maps = _host_layout(inputs)
    if "nc" not in _NC_CACHE:
        _NC_CACHE["nc"] = build_program()
    nc = _NC_CACHE["nc"]
    res = run_bass_kernel_spmd(nc, in_maps, core_ids=list(range(NCORES)))
    ys = [np.asarray(r["y"], np.float32) for r in res.results]
    y_prompt = np.empty((32, 2048, DM), np.float32)
    y_sample = np.empty((2, 8192, DM), np.float32)
    for c in range(NCORES):
        if c < 2:
            y_sample[c] = ys[c][:8192]
            y_prompt[c] = ys[c][8192:]
        else:
            i0 = 2 + 5 * (c - 2)
            y_prompt[i0:i0 + 5] = ys[c].reshape(5, 2048, DM)
    return (y_prompt, y_sample)
```

```python
import numpy as np
from contextlib import ExitStack
import concourse.bass as bass
import concourse.mybir as mybir
from concourse.bass_utils import run_bass_kernel_spmd

F32 = mybir.dt.float32
BF16 = mybir.dt.bfloat16
ALU = mybir.AluOpType
AF = mybir.ActivationFunctionType

NCORES = 8
NT = 80
TPU = 16
NU = 5
DM = 1024
CIN = 4624
QM, KM, VM, OM, ZM, GT, QA, KA, VA, ZA = 0, 512, 1024, 1536, 2048, 2560, 2576, 3088, 3600, 4112
EPS = 1e-6
NPRE = 4
LAG = 3
OPT_WCONV_ACT = True
OPT_TAPS = True
OPT_SILU_HOIST = True
KR = 7
QR = 4
SZR = 5

C_BG, C_CW, C_NG, C_MG, C_QG, C_KG, C_KEEP, C_NKEEP, C_KEEPB, C_HLO, C_HHI, C_CM = 0, 16, 56, 64, 68, 69, 70, 75, 80, 84, 85, 86


def _win_unl(R):
    u, r = divmod(R, 32)
    rs = 32 * u + min(max(r - 4, 0), 24)
    return set(range(rs, rs + 8))


def _win_lnk(R):
    if R // 32 == 4:
        return _win_unl(R)
    rs = min(max(R - 4, 0), 120)
    return set(range(rs, rs + 8))


def attn_plan():
    plan = []
    cms = []
    for tq in range(NT):
        tiles = {}
        for hr in (0, 1):
            R = 2 * tq + hr
            wu, wl = _win_unl(R), _win_lnk(R)
            for a in sorted({r // 2 for r in (wu | wl)}):
                rows = {2 * a, 2 * a + 1}
                ru, rl = wu & rows, wl & rows
                ent = tiles.setdefault(a, [None, None])
                if ru == rl:
                    if len(ru) == 2:
                        ent[hr] = ("s", 0, 128)
                    elif ru == {2 * a}:
                        ent[hr] = ("s", 0, 64)
                    elif ru == {2 * a + 1}:
                        ent[hr] = ("s", 64, 128)
                else:
                    ent[hr] = ("d", len(cms))
                    cms.append((R, a))
        plan.append([(a, tiles[a]) for a in sorted(tiles) if tiles[a][0] or tiles[a][1]])
    return plan, cms


PLAN, CMS = attn_plan()
for _tq, _tl in enumerate(PLAN):
    for _a, _ in _tl:
        assert _tq - 3 <= _a <= _tq + 3 and 0 <= _a < NT, (_tq, _a)
NCST = C_CM + len(CMS)


class _Op:
    __slots__ = ("eng", "fn", "deps", "dma_key", "dma_cnt", "signal", "semval", "idx", "seq", "know")


class Sched:
    def __init__(self, nc):
        self.nc = nc
        self.streams = {e: [] for e in ("pe", "act", "dve", "pool", "sp")}
        self.buf = {}
        self.psr = {}
        self.dma_counts = {}
        self.dma_ops = {}
        self.nseq = 0

    def _deps_for(self, reads, writes):
        deps = []
        for k in reads:
            st = self.buf.get(k)
            if st and st[0] is not None:
                deps.append(st[0])
        for k in writes:
            st = self.buf.get(k)
            if st:
                if st[0] is not None:
                    deps.append(st[0])
                deps.extend(st[1])
        return deps

    def _commit(self, opid, reads, writes):
        for k in reads:
            self.buf.setdefault(k, [None, []])[1].append(opid)
        for k in writes:
            self.buf[k] = [opid, []]

    def op(self, eng, fn, reads=(), writes=()):
        o = _Op()
        o.eng, o.fn, o.dma_key, o.signal = eng, fn, None, False
        o.deps = self._deps_for(reads, writes)
        o.idx = len(self.streams[eng])
        oid = ("e", eng, o.idx)
        o.seq = self.nseq
        self.nseq += 1
        for k in reads:
            if k.startswith("ps") and k not in writes:
                rd = self.psr.setdefault(k, {})
                for e2, rid in rd.items():
                    if e2 != eng:
                        o.deps.append(rid)
                rd[eng] = oid
        for k in writes:
            if k.startswith("ps"):
                self.psr[k] = {}
        self.streams[eng].append(o)
        self._commit(oid, reads, writes)
        return o

    def dma(self, key, fn, reads=(), writes=(), eng="sp"):
        o = _Op()
        o.eng, o.fn, o.dma_key, o.signal = eng, fn, key, True
        c = self.dma_counts.get(key, 0) + 1
        self.dma_counts[key] = c
        o.dma_cnt = c
        o.deps = self._deps_for(reads, writes)
        o.idx = len(self.streams[eng])
        o.seq = self.nseq
        self.nseq += 1
        self.dma_ops[(key, c)] = o
        self.streams[eng].append(o)
        self._commit(("d", key, c), reads, writes)
        return o

    def finalize(self):
        allops = sorted((o for st in self.streams.values() for o in st), key=lambda o: o.seq)
        W = {e: {} for e in self.streams}

        def op_of(src, val):
            return self.streams[src[1]][val] if src[0] == "e" else self.dma_ops[(src[1], val)]

        for o in allops:
            eng = o.eng
            w = W[eng]
            cand = {}
            for d in o.deps:
                if d[0] == "e":
                    _, de, di = d
                    if de == eng and eng == "pe":
                        continue
                    src, val = ("e", de), di
                else:
                    src, val = ("d", d[1]), d[2]
                    if val <= 0:
                        continue
                if cand.get(src, -1) < val:
                    cand[src] = val
            need = {}
            for src, val in sorted(cand.items(), key=lambda kv: -op_of(kv[0], kv[1]).seq):
                if w.get(src, -1) >= val:
                    continue
                need[src] = val
                if w.get(src, -1) < val:
                    w[src] = val
                for s2, v2 in op_of(src, val).know.items():
                    if w.get(s2, -1) < v2:
                        w[s2] = v2
            o.deps = need
            for src, val in need.items():
                if src[0] == "e":
                    self.streams[src[1]][val].signal = True
            o.know = dict(w)
            if o.dma_key is None:
                o.know[("e", eng)] = o.idx
            else:
                o.know[("d", o.dma_key)] = o.dma_cnt
        for eng, stream in self.streams.items():
            c = 0
            for o in stream:
                o.semval = None
                if o.dma_key is None and o.signal:
                    c += 1
                    o.semval = c

    def run(self, eng, e, sems, final_dma_keys=()):
        for o in self.streams[eng]:
            for src, val in o.deps.items():
                if src[0] == "e":
                    e.wait_ge(sems[src[1]], self.streams[src[1]][val].semval)
                else:
                    e.wait_ge(sems["d:" + src[1]], 16 * val)
            ins = o.fn(e)
            if o.dma_key is not None:
                ins.then_inc(sems["d:" + o.dma_key], 16)
            elif o.signal:
                ins.then_inc(sems[eng], 1)
        for k in final_dma_keys:
            e.wait_ge(sems["d:" + k], 16 * self.dma_counts[k])


def build_program(debug=False, phase=9, s1_stop=0, s2_tiles=NT, s2_start=0, cstop=9):
    nc = bass.Bass("TRN2", target_bir_lowering=False)
    xs = nc.dram_tensor("xs", [NT * 128, DM], F32, kind="ExternalInput").ap()
    xh = nc.dram_tensor("xh", [NPRE * 128, DM], F32, kind="ExternalInput").ap()
    wi = nc.dram_tensor("wi", [DM, CIN], F32, kind="ExternalInput").ap()
    wo = nc.dram_tensor("wo", [DM, DM], F32, kind="ExternalInput").ap()
    cst_d = nc.dram_tensor("cst", [128, NCST], F32, kind="ExternalInput").ap()
    fraw = nc.dram_tensor("fraw", [128, 8 * 16 * 64], F32, kind="ExternalInput").ap()
    fmsk = nc.dram_tensor("fmsk", [128, 8 * 16 * 64], F32, kind="ExternalInput").ap()
    cmat = nc.dram_tensor("cmat", [128, 6 * 128], F32, kind="ExternalInput").ap()
    y = nc.dram_tensor("y", [NT * 128, DM], F32, kind="ExternalOutput").ap()
    cbs = nc.dram_tensor("cbs", [NT, 128, 4 * 129], F32, kind="Internal").ap()

    es = ExitStack()
    with es:
        def sb(name, shape, dt):
            return es.enter_context(nc.sbuf_tensor("s_" + name, shape, dt))

        S = Sched(nc)
        w_bf = sb("w_bf", [128, 8, CIN], BF16)
        wo_bf = sb("wo_bf", [128, 8, DM], BF16)
        Ftab = sb("Ftab", [128, 8, 16, 64], BF16)
        xhT = sb("xhT", [128, 8, 320], BF16)
        cst = sb("cst", [128, NCST], F32)
        cm_f = sb("cm_f", [128, 6, 128], F32)
        ident = sb("ident", [128, 128], BF16)
        bones = sb("bones", [128, 128], BF16)
        qg8 = sb("qg8", [128, 2], F32)
        xt = sb("xt", [128, DM], F32)
        st1 = sb("st1", [128, 4], F32)
        xnb = sb("xnb", [128, DM], BF16)
        xnT = sb("xnT", [128, 2, 8, 132], BF16)
        acc = sb("acc", [128, 8, 128], F32)
        qkT = sb("qkT", [128, 2, 8, 128], BF16)
        ktok = sb("ktok", [128, 2, 512], BF16)
        vm = sb("vm", [128, 2, 512], F32)
        so = sb("so", [128, 2, 512], BF16)
        szm = sb("szm", [128, 2, 4, 128], BF16)
        V4 = sb("V4", [128, 3, 4, 129], BF16)
        gt = sb("gt", [128, 2, 16], F32)
        g8 = sb("g8", [128, 2, 8], F32)
        d8 = sb("d8", [128, 2, 8], F32)
        sc = sb("sc", [128, 2, 4, 4], F32)
        eb8 = sb("eb8", [128, 2, 8], F32)
        dec8 = sb("dec8", [128, 2, 8], F32)
        sqb = sb("sqb", [128, 4, 128], BF16)
        rsn = sb("rsn", [128, 512], F32)
        qnT = sb("qnT", [128, QR, 2, 4, 128], BF16)
        knT = sb("knT", [128, KR, 4, 128], BF16)
        va = sb("va", [128, KR, 8, 65], BF16)
        sza = sb("sza", [128, SZR, 4, 128], BF16)
        ymT = sb("ymT", [128, QR, 4, 128], BF16)
        Eb = sb("Eb", [128, 2, 8, 128], BF16)
        Pb = sb("Pb", [128, 1, 8, 128], BF16)
        Em = sb("Em", [16, 8, 128], BF16)
        rd8 = sb("rd8", [128, 8], F32)
        ya = sb("ya", [128, 8, 64], BF16)
        yaT = sb("yaT", [128, 4, 128], BF16)
        Sfb = sb("Sfb", [128, 2, 4, 128], BF16)
        hf = sb("hf", [128, 4, 128], F32)
        hn = sb("hn", [128, 4, 128], BF16)
        da8 = sb("da8", [128, 8], F32)
        ss4 = sb("ss4", [128, 4], F32)
        r4 = sb("r4", [128, 4], F32)
        Cf = sb("Cf", [128, 1, 4, 129], F32)
        C_bf = sb("C_bf", [128, 4, 129], BF16)
        cb = sb("cb", [128, 4, 129], F32)
        cb_bf = sb("cb_bf", [128, 4, 129], BF16)
        xr = sb("xr", [128, DM], F32)
        kmT = sb("kmT", [128, 4, 16], BF16)
        va_m = sb("va_m", [128, 8, 65], BF16)
        prem = sb("prem", [128, 4, 20], F32)
        accm = sb("accm", [128, 4, 16], F32)
        kmm = sb("kmm", [128, 4, 16], BF16)
        Vwm = sb("Vwm", [16, 4, 129], BF16)
        wsm = sb("wsm", [16, 4], F32)

        pbank = [es.enter_context(nc.psum_tensor(f"ps{i}", [128, 512], F32)) for i in range(8)]
        sems = {k: es.enter_context(nc.semaphore(k)) for k in ("pe", "act", "dve", "pool")}
        dma_keys = ["ld_x", "ld_xr", "ld_cb", "st_y", "st_cb0", "st_cb1", "ld_w0", "ld_w1", "ld_w2", "ld_w3", "ld_c0", "ld_c1"]
        for k in dma_keys:
            sems["d:" + k] = es.enter_context(nc.semaphore("d_" + k))

        pools = {"A": [0, 1, 2, 3, 4, 5], "L1": [0, 1], "L2": [2, 3, 4], "L3": [5], "S1g": [6, 7], "junk": [0],
                 "C2": [3, 4], "M2": [2, 5], "S1u": [2]}
        pctr = {k: 0 for k in pools}
        cur_pool = ["A"]
        rec = [None]

        def nb():
            p = cur_pool[0]
            b = pools[p][pctr[p] % len(pools[p])]
            pctr[p] += 1
            return b

        def setctx(lst, pool):
            rec[0] = lst
            cur_pool[0] = pool

        def emit_op(eng, fn, reads, writes):
            if rec[0] is None:
                S.op(eng, fn, reads, writes)
            else:
                rec[0].append(("op", eng, fn, tuple(reads), tuple(writes)))

        def emit_dma(key, fn, reads=(), writes=()):
            if rec[0] is None:
                S.dma(key, fn, reads=reads, writes=writes)
            else:
                rec[0].append(("dma", key, fn, tuple(reads), tuple(writes)))

        def zip_lists(a, b):
            out = []
            for i in range(max(len(a), len(b))):
                if i < len(a):
                    out.append(a[i])
                if i < len(b):
                    out.append(b[i])
            return out

        def merge_emit(lists):
            lists = [L for L in lists if L]
            idx = [0] * len(lists)
            while True:
                best, bf = None, 2.0
                for li, L in enumerate(lists):
                    if idx[li] < len(L):
                        f = idx[li] / len(L)
                        if f < bf:
                            best, bf = li, f
                if best is None:
                    break
                it = lists[best][idx[best]]
                idx[best] += 1
                if it[0] == "op":
                    S.op(it[1], it[2], it[3], it[4])
                else:
                    S.dma(it[1], it[2], reads=it[3], writes=it[4])

        def pk(b):
            return f"ps{b}"

        def ps_f(b):
            return pbank[b]

        def ps_bf(b):
            return pbank[b][:].bitcast(BF16)

        def mm(out, lhsT, rhs, start, stop, reads, writes, skip=False):
            if skip:
                emit_op("pe", lambda e: e.matmul(out, lhsT=lhsT, rhs=rhs, start=start, stop=stop,
                                                 skip_group_check=True), reads, writes)
            else:
                emit_op("pe", lambda e: e.matmul(out, lhsT=lhsT, rhs=rhs, start=start, stop=stop), reads, writes)

        def tr(out, in_, reads, writes):
            emit_op("pe", lambda e: e.transpose(out=out, in_=in_, identity=ident[:]), list(reads) + ["ident"], writes)

        def act(out, in_, func, reads, writes, bias=None, scale=None, accum=None):
            kw = {}
            if bias is not None:
                kw["bias"] = bias
            if scale is not None:
                kw["scale"] = scale
            if accum is not None:
                kw["accum_out"] = accum
            emit_op("act", lambda e: e.activation(out=out, in_=in_, func=func, **kw), reads, writes)

        def tt(eng, out, in0, in1, op, reads, writes):
            emit_op(eng, lambda e: e.tensor_tensor(out=out, in0=in0, in1=in1, op=op), reads, writes)

        def ts(eng, out, in0, s1, op0, reads, writes, s2=None, op1=None):
            if op1 is None:
                emit_op(eng, lambda e: e.tensor_scalar(out=out, in0=in0, scalar1=s1, scalar2=None, op0=op0), reads, writes)
            else:
                emit_op(eng, lambda e: e.tensor_scalar(out=out, in0=in0, scalar1=s1, scalar2=s2, op0=op0, op1=op1),
                        reads, writes)

        def stt(eng, out, in0, scalar, in1, op0, op1, reads, writes):
            emit_op(eng, lambda e: e.scalar_tensor_tensor(out=out, in0=in0, scalar=scalar, in1=in1, op0=op0, op1=op1),
                    reads, writes)

        def cp(eng, out, in_, reads, writes):
            emit_op(eng, lambda e: e.tensor_copy(out, in_), reads, writes)

        def ms(eng, ap, val, writes):
            emit_op(eng, lambda e: e.memset(ap, val), (), writes)

        def recip(out, in_, reads, writes):
            emit_op("dve", lambda e: e.reciprocal(out, in_), reads, writes)

        def cc(i):
            return cst[:, i:i + 1]

        emit_dma("ld_c0", lambda e: e.dma_start(out=cst[:], in_=cst_d), writes=["cst"])
        emit_dma("ld_c1", lambda e: e.dma_start(out=cm_f[:].rearrange("p a b -> p (a b)"), in_=cmat), writes=["cm_f"])
        triu, tril, Tf, Tb, negones, Tm = (cm_f[:, i, :] for i in range(6))
        tt("dve", ident[:], cm_f[:, 0, :], cm_f[:, 1, :], ALU.mult, ["cm_f"], ["ident"])
        ms("pool", bones[:], 1.0, ["bones"])
        ms("pool", bones[0:64, 64:128], 0.0, ["bones"])
        ms("pool", bones[64:128, 0:64], 0.0, ["bones"])
        ts("dve", qg8[:, 0:1], cc(C_QG), 0.125, ALU.mult, ["cst"], ["qg8"], s2=cc(C_HLO), op1=ALU.mult)
        ts("dve", qg8[:, 1:2], cc(C_QG), 0.125, ALU.mult, ["cst"], ["qg8"], s2=cc(C_HHI), op1=ALU.mult)
        ms("pool", va[:].rearrange("p a b c -> p (a b c)"), 1.0, [f"va{i}" for i in range(KR)])
        ms("pool", va_m[:].rearrange("p b c -> p (b c)"), 1.0, ["va_m"])
        ms("pool", prem[:].rearrange("p a b -> p (a b)"), 0.0, ["prem"])
        ms("pool", Cf[:].rearrange("p a b c -> p (a b c)"), 0.0, ["Cf0"])
        ms("pool", xnT[:].rearrange("p s a b -> p (s a b)"), 0.0, ["xnT0", "xnT1"])

        PIECES = [(i * 512, min(512, CIN - i * 512)) for i in range((CIN + 511) // 512)]
        STG = [(xr, 0, "xr0", "ld_w0"), (xr, 512, "xr1", "ld_w1"), (xt, 0, "xt0", "ld_w2"), (xt, 512, "xt1", "ld_w3")]
        k = 0
        for kc in range(8):
            for (c0, cn) in PIECES:
                buf, off, bkey, dkey = STG[k % 4]
                on_dve = (k % 2 == 0)
                k += 1
                stg = buf[:, off:off + cn]
                emit_dma(dkey, lambda e, kc=kc, c0=c0, cn=cn, stg=stg: e.dma_start(
                    out=stg, in_=wi[kc * 128:(kc + 1) * 128, c0:c0 + cn]), writes=[bkey])
                if on_dve:
                    ts("dve", w_bf[:, kc, c0:c0 + cn], stg, cc(C_NG + kc), ALU.mult, [bkey, "cst"], ["w_bf"])
                else:
                    act(w_bf[:, kc, c0:c0 + cn], stg, AF.Copy, [bkey, "cst"], ["w_bf"], scale=cc(C_NG + kc))
        for kc in range(8):
            for half in range(2):
                buf, off, bkey, dkey = STG[k % 4]
                on_dve = (k % 2 == 0)
                k += 1
                srcw = buf[:, off:off + 512]
                emit_dma(dkey, lambda e, kc=kc, half=half, srcw=srcw: e.dma_start(
                    out=srcw, in_=wo[kc * 128:(kc + 1) * 128, half * 512:(half + 1) * 512]), writes=[bkey])
                dstw = wo_bf[:, kc, half * 512:(half + 1) * 512]
                if on_dve:
                    if kc < 4:
                        ts("dve", dstw, srcw, cc(C_MG + kc), ALU.mult, [bkey, "cst"], ["wo_bf"])
                    else:
                        cp("dve", dstw, srcw, [bkey], ["wo_bf"])
                else:
                    if kc < 4:
                        act(dstw, srcw, AF.Copy, [bkey, "cst"], ["wo_bf"], scale=cc(C_MG + kc))
                    else:
                        act(dstw, srcw, AF.Copy, [bkey], ["wo_bf"])
        Ff = Ftab[:].rearrange("p h i q -> p (h i q)")
        for j in range(16):
            (b0, o0, k0, d0), (b1, o1, k1, d1) = (STG[0], STG[1]) if j % 2 == 0 else (STG[2], STG[3])
            raw, msk = b0[:, o0:o0 + 512], b1[:, o1:o1 + 512]
            emit_dma(d0, lambda e, j=j, raw=raw: e.dma_start(out=raw, in_=fraw[:, j * 512:(j + 1) * 512]), writes=[k0])
            emit_dma(d1, lambda e, j=j, msk=msk: e.dma_start(out=msk, in_=fmsk[:, j * 512:(j + 1) * 512]), writes=[k1])
            act(raw, raw, AF.Exp, [k0], [k0])
            tt("dve", Ff[:, j * 512:(j + 1) * 512], raw, msk, ALU.mult, [k0, k1], ["Ftab"])

        def stage_X(src_ap, xs_=0, t=None, halo_dst=None):
            emit_dma("ld_x", lambda e: e.dma_start(out=xt[:], in_=src_ap), writes=["xt", "xt0", "xt1"])
            act(xnb[:], xt[:], AF.Square, ["xt"], ["st1a", "xnb"], scale=1.0 / 32, accum=st1[:, 0:1])
            act(st1[:, 1:2], st1[:, 0:1], AF.Ln, ["st1a"], ["st1b"], bias=EPS)
            act(st1[:, 2:3], st1[:, 1:2], AF.Exp, ["st1b"], ["st1c"], scale=-0.5)
            ts("dve", xnb[:], xt[:], st1[:, 2:3], ALU.mult, ["xt", "st1c"], ["xnb"])
            b = nb()
            for kc in range(8):
                tr(ps_bf(b)[:, kc * 128:(kc + 1) * 128], xnb[:, kc * 128:(kc + 1) * 128], ["xnb"], [pk(b)])
            if halo_dst is not None:
                cp("dve", halo_dst, ps_bf(b).rearrange("p (a b) -> p a b", a=8), [pk(b)], ["xhT"])
                return
            cp("dve", xnT[:, xs_, :, 2:130], ps_bf(b).rearrange("p (a b) -> p a b", a=8), [pk(b)], [f"xnT{xs_}"])
            if t is not None:
                act(xnT[:, xs_, :, 0:2], xhT[:, :, 4 * t:4 * t + 2], AF.Copy, ["xhT"], [f"xnT{xs_}"])
                act(xnT[:, xs_, :, 130:132], xhT[:, :, 4 * t + 2:4 * t + 4], AF.Copy, ["xhT"], [f"xnT{xs_}"])

        def stage_A(mode, xs_=0, t=None, slot_q=None, slot_k=None, va_dst=None, kn_dst=None, va_key="va_m",
                    ctx=None, ab=0, slot_z=0):
            def enter(name):
                if ctx is not None:
                    setctx(*ctx[name])
            full = mode == "full"
            xk = f"xnT{xs_}"
            vmv, gtv, vmk, gtk = vm[:, ab, :], gt[:, ab, :], f"vm{ab}", f"gt{ab}"
            qkTs, sos, szms = qkT[:, ab], so[:, ab, :], szm[:, ab]
            qk_key, so_key, szm_key = f"qkT{ab}", f"so{ab}", f"szm{ab}"
            ks = 0 if full else ab
            ktoks, g8s, d8s, scs, eb8s, dec8s = ktok[:, ks, :], g8[:, ks, :], d8[:, ks, :], sc[:, ks], eb8[:, ks, :], dec8[:, ks, :]
            kk, gk, dk, sk_, ek_, dck = f"ktok{ks}", f"g8{ks}", f"d8{ks}", f"sc{ks}", f"eb8{ks}", f"dec8{ks}"
            vslot = 2 if full else 1 + ab

            def pg(g):
                return g if full else (g - 4 + 4 * ab)
            enter("main")

            def fm_group(bank, pos, col0, n0, n1):
                n = n1 - n0
                for kc in range(8):
                    mm(ps_f(bank)[:, pos * n:(pos + 1) * n], w_bf[:, kc, col0:col0 + 128], xnT[:, xs_, kc, n0:n1],
                       kc == 0, kc == 7, ["w_bf", xk], [pk(bank)])

            def tm_proj(bank, col0, ncol):
                for kc in range(8):
                    mm(ps_f(bank)[:, 0:ncol], xnT[:, xs_, kc, 2:130], w_bf[:, kc, col0:col0 + ncol],
                       kc == 0, kc == 7, ["w_bf", xk], [pk(bank)])

            groups = list(range(8)) if full else [4, 5, 6, 7]
            gi = 0
            while gi < len(groups):
                grp = groups[gi:gi + 3]
                gi += 3
                bk = nb()
                for pos, g in enumerate(grp):
                    fm_group(bk, pos, g * 128, 0, 132)
                for pos, g in enumerate(grp):
                    src = ps_f(bk)[:, pos * 132:(pos + 1) * 132]
                    act(acc[:, pg(g), :], src[:, 0:128], AF.Copy, [pk(bk), "cst"], [f"acc{pg(g)}"], scale=cc(C_CW + g * 5))
                    if full and g >= 4 and t is None:
                        act(prem[:, g - 4, 2:18], src[:, 2:18], AF.Copy, [pk(bk)], ["prem"])
                    if full and g >= 4 and t is not None and t % TPU == 0:
                        act(prem[:, g - 4, 18:20], src[:, 2:4], AF.Copy, [pk(bk)], ["prem"])
                order = ([(j, pos, g) for j in range(1, 5) for pos, g in enumerate(grp)] if OPT_TAPS else
                         [(j, pos, g) for pos, g in enumerate(grp) for j in range(1, 5)])
                for (j, pos, g) in order:
                    src = ps_f(bk)[:, pos * 132:(pos + 1) * 132]
                    stt("dve", acc[:, pg(g), :], src[:, j:j + 128], cc(C_CW + g * 5 + j), acc[:, pg(g), :],
                        ALU.mult, ALU.add, [pk(bk), "cst", f"acc{pg(g)}"], [f"acc{pg(g)}"])
            bv = nb()
            tm_proj(bv, VM, 512)
            act(vmv, ps_f(bv)[:, :], AF.Copy, [pk(bv)], [vmk])
            bg = nb()
            tm_proj(bg, GT, 16)
            tt("dve", gtv, ps_f(bg)[:, 0:16], cst[:, C_BG:C_BG + 16], ALU.add, [pk(bg), "cst"], [gtk])
            if full:
                bo = nb()
                tm_proj(bo, OM, 512)
                act(sos, ps_f(bo)[:, :], AF.Tanh, [pk(bo)], [so_key], scale=0.5)
                ts("pool", sos, sos, 0.5, ALU.mult, [so_key], [so_key], s2=0.5, op1=ALU.add)
                bz = nb()
                for g in range(4):
                    fm_group(bz, g, ZM + g * 128, 2, 130)
                act(szms.rearrange("p a b -> p (a b)"), ps_f(bz)[:, :], AF.Silu, [pk(bz)], [szm_key])
                bz = nb()
                for g in range(4):
                    fm_group(bz, g, ZA + g * 128, 2, 130)
                act(sza[:, slot_z].rearrange("p a b -> p (a b)"), ps_f(bz)[:, :], AF.Silu, [pk(bz)], [f"sza{slot_z}"])
                act(qkTs.rearrange("p a b -> p (a b)"), acc[:].rearrange("p a b -> p (a b)"), AF.Silu,
                    [f"acc{g}" for g in range(8)], [qk_key])
                enter("norm")
                bva = nb()
                tm_proj(bva, VA, 512)
                cp("dve", va_dst[:, :, 0:64], ps_f(bva)[:, :].rearrange("p (h d) -> p h d", h=8), [pk(bva)], [va_key])
                for (col, isq, dkey) in ((QA, True, f"qnT{slot_q}"), (KA, False, f"knT{slot_k}")):
                    if isq and kn_dst is not None:
                        continue
                    bq = nb()
                    for g in range(4):
                        fm_group(bq, g, col + g * 128, 2, 130)
                    act(sqb[:].rearrange("p a b -> p (a b)"), ps_f(bq)[:, :], AF.Square, [pk(bq)], ["sqb"])
                    bs = nb()
                    for g in range(4):
                        mm(ps_f(bs)[:, g * 128:(g + 1) * 128], bones[:], sqb[:, g, :], True, True,
                           ["bones", "sqb"], [pk(bs)])
                    act(rsn[:], ps_f(bs)[:, :], AF.Ln, [pk(bs)], ["rsn"], scale=1.0 / 64, bias=EPS)
                    act(rsn[:], rsn[:], AF.Exp, ["rsn"], ["rsn"], scale=-0.5)
                    if isq:
                        for par in range(2):
                            stt("dve", qnT[:, slot_q, par].rearrange("p a b -> p (a b)"), ps_f(bq)[:, :], qg8[:, par:par + 1],
                                rsn[:], ALU.mult, ALU.mult, [pk(bq), "rsn", "qg8"], [dkey])
                    else:
                        dst = kn_dst if kn_dst is not None else knT[:, slot_k]
                        stt("dve", dst.rearrange("p a b -> p (a b)"), ps_f(bq)[:, :], cc(C_KG), rsn[:], ALU.mult, ALU.mult,
                            [pk(bq), "rsn", "cst"], [dkey])
            enter("tail_k")
            if not full:
                act(qkTs[:, 4:8, :].rearrange("p a b -> p (a b)"), acc[:, 4 * ab:4 * ab + 4, :].rearrange("p a b -> p (a b)"),
                    AF.Silu, [f"acc{4 * ab + g}" for g in range(4)], [qk_key])
            bt = nb()
            for g in range(4):
                tr(ps_bf(bt)[:, g * 128:(g + 1) * 128], qkTs[:, 4 + g, :], [qk_key], [pk(bt)])
            cp("dve", ktoks, ps_bf(bt)[:, 0:512], [pk(bt)], [kk])
            enter("tail_g")
            gv = gtv.rearrange("p (d k h) -> p d k h", d=2, k=2)
            f_view = gv[:, :, 1, :]
            i_view = gv[:, :, 0, :]
            g8v = g8s.rearrange("p (a b) -> p a b", a=2)
            act(g8v, f_view, AF.Exp, [gtk], [gk], scale=-1.0)
            act(g8s, g8s, AF.Ln, [gk], [gk], bias=1.0)
            bc = nb()
            mm(ps_f(bc)[:, 0:4], Tf, g8s[:, 0:4], True, True, ["cm_f", gk], [pk(bc)])
            mm(ps_f(bc)[:, 4:8], Tb, g8s[:, 4:8], True, True, ["cm_f", gk], [pk(bc)])
            mm(ps_f(bc)[:, 8:16], negones, g8s[:, 0:8], True, True, ["cm_f", gk], [pk(bc)])
            tt("dve", d8s.rearrange("p (a b) -> p a b", a=2), i_view, ps_f(bc)[:, 0:8].rearrange("p (a b) -> p a b", a=2),
               ALU.subtract, [gtk, pk(bc)], [dk])
            act(scs[:, 0:2, :].rearrange("p a b -> p (a b)"), d8s, AF.Exp, [dk], [sk_])
            act(eb8s, ps_f(bc)[:, 0:8], AF.Exp, [pk(bc)], [ek_], scale=-1.0, bias=float(0.5 * np.log(128.0)))
            act(dec8s, ps_f(bc)[:, 8:16], AF.Exp, [pk(bc)], [dck])
            tt("dve", scs[:, 2:4, :].rearrange("p a b -> p (a b)"), scs[:, 0:2, :].rearrange("p a b -> p (a b)"), dec8s,
               ALU.mult, [sk_, dck], [sk_])
            vm3 = vmv.rearrange("p (h d) -> p h d", h=4)
            if full:
                vm_b = vm3.unsqueeze(1).to_broadcast([128, 2, 4, 128])
                sc_b = scs[:, 0:2, :].unsqueeze(3).to_broadcast([128, 2, 4, 128])
                tt("dve", V4[:, 0:2, :, 0:128], vm_b, sc_b, ALU.mult, [vmk, sk_], ["V4u"])
                cp("dve", V4[:, 0:2, :, 128], scs[:, 0:2, :], [sk_], ["V4u"])
                tt("pool", V4[:, 2, :, 0:128], vm3, scs[:, 2, :].unsqueeze(2).to_broadcast([128, 4, 128]), ALU.mult,
                   [vmk, sk_], ["V4w2"])
                cp("pool", V4[:, 2, :, 128], scs[:, 2, :], [sk_], ["V4w2"])
            else:
                tt("dve", V4[:, vslot, :, 0:128], vm3, scs[:, 3, :].unsqueeze(2).to_broadcast([128, 4, 128]), ALU.mult,
                   [vmk, sk_], [f"V4w{vslot}"])
                cp("dve", V4[:, vslot, :, 128], scs[:, 3, :], [sk_], [f"V4w{vslot}"])

        def state_update(cur, vk, deccol0, ks=0):
            nxt = cur
            for j in range(2):
                bk = nb()
                for hh in range(2):
                    h = 2 * j + hh
                    mm(ps_f(bk)[:, hh * 129:(hh + 1) * 129], ktok[:, ks, h * 128:(h + 1) * 128], V4[:, vk, h, :],
                       True, True, [f"ktok{ks}", f"V4w{vk}"], [pk(bk)])
                for hh in range(2):
                    h = 2 * j + hh
                    stt("dve", Cf[:, nxt, h, :], Cf[:, cur, h, :], dec8[:, ks, deccol0 + h:deccol0 + h + 1],
                        ps_f(bk)[:, hh * 129:(hh + 1) * 129], ALU.mult, ALU.add,
                        [f"Cf{cur}", f"dec8{ks}", pk(bk)], [f"Cf{nxt}"])
            return nxt

        if phase >= 1:
            for i in range(2):
                stage_X(xh[i * 128:(i + 1) * 128, :], halo_dst=xhT[:, :, i * 128:(i + 1) * 128])
            emit_dma("ld_x", lambda e: e.dma_start(out=xt[:], in_=xh[256:384, :]), writes=["xt", "xt0", "xt1"])
            act(xnb[:], xt[:], AF.Square, ["xt"], ["st1a", "xnb"], scale=1.0 / 32, accum=st1[:, 0:1])
            act(st1[:, 1:2], st1[:, 0:1], AF.Ln, ["st1a"], ["st1b"], bias=EPS)
            act(st1[:, 2:3], st1[:, 1:2], AF.Exp, ["st1b"], ["st1c"], scale=-0.5)
            ts("dve", xnb[:], xt[:], st1[:, 2:3], ALU.mult, ["xt", "st1c"], ["xnb"])
            b = nb()
            for kc in range(8):
                tr(ps_bf(b)[:, kc * 128:(kc + 1) * 128], xnb[:, kc * 128:(kc + 1) * 128], ["xnb"], [pk(b)])
            cp("dve", xhT[:, :, 256:320], ps_bf(b).rearrange("p (a b) -> p a b", a=8)[:, :, 0:64], [pk(b)], ["xhT"])
            stage_X(xh[384:512, :], xs_=0, t=None)
            stage_A("full", xs_=0, t=None, slot_q=0, slot_k=0, va_dst=va_m[:], kn_dst=knT[:, 0])
            cp("dve", kmT[:], knT[:, 0, :, 0:16], ["knT0"], ["kmT"])
            bm = nb()
            mm(ps_f(bm)[:, 0:4], Tm, g8[:, 0, 0:4], True, True, ["cm_f", "g80"], [pk(bm)])
            tt("dve", d8[0:16, 0, 0:4], gt[0:16, 0, 0:4], ps_f(bm)[0:16, 0:4], ALU.add, ["gt0", pk(bm)], ["d80"])
            act(wsm[:], d8[0:16, 0, 0:4], AF.Exp, ["d80"], ["wsm"])
            tt("dve", Vwm[:, :, 0:128], vm[0:16, 0, :].rearrange("p (h d) -> p h d", h=4),
               wsm[:].unsqueeze(2).to_broadcast([16, 4, 128]), ALU.mult, ["vm0", "wsm"], ["Vwm"])
            cp("dve", Vwm[:, :, 128], wsm[:], ["wsm"], ["Vwm"])

        cur = 0
        s1_tiles = list(range(NT - 1, (s1_stop - 1) if phase >= 2 else NT - 1, -1))
        proc = [t for t in s1_tiles if t != s1_stop]
        junk = []
        pairs = [proc[k:k + 2] for k in range(0, len(proc), 2)]

        def a_bwd(t, keep):
            ctx = {}
            for name in ("main", "norm", "tail_k", "tail_g"):
                ctx[name] = keep.get(name, (junk, "junk"))
            stage_A("bwd", xs_=t % 2, t=t, ab=t % 2, ctx=ctx)
            del junk[:]

        if pairs:
            setctx(None, "A")
            for t in pairs[0]:
                stage_X(xs[t * 128:(t + 1) * 128, :], xs_=t % 2, t=t)
        for pi, pr in enumerate(pairs):
            setctx(None, "A")
            for t in pr:
                a_bwd(t, {"main": (None, "A")})
            chains, La, Lx = [], [], []
            for j, t in enumerate(pr):
                Lk, Lg = [], []
                a_bwd(t, {"tail_k": (Lk, "L1" if j == 0 else "C2"), "tail_g": (Lg, "S1g")})
                chains.append(zip_lists(Lk, Lg))
            setctx(La, "S1u")
            for t in pr:
                emit_dma("st_cb0", lambda e, t=t: e.dma_start(
                    out=cbs[t], in_=Cf[:, 0].rearrange("p a b -> p (a b)")), reads=["Cf0"])
                if t % TPU == TPU - 1 and t < NT - 1:
                    u = t // TPU
                    ts("dve", dec8[:, t % 2, 4:8], dec8[:, t % 2, 4:8], cc(C_KEEPB + u), ALU.mult,
                       [f"dec8{t % 2}", "cst"], [f"dec8{t % 2}"])
                cur = state_update(cur, 1 + t % 2, 4, ks=t % 2)
            if pi + 1 < len(pairs):
                setctx(Lx, "L3")
                for t in pairs[pi + 1]:
                    stage_X(xs[t * 128:(t + 1) * 128, :], xs_=t % 2, t=t)
            setctx(None, "A")
            merge_emit([(zip_lists(chains[0], chains[1]) if len(chains) > 1 else chains[0]) + La, Lx])
        if s1_tiles and s1_tiles[-1] == s1_stop:
            setctx(None, "A")
            emit_dma("st_cb0", lambda e: e.dma_start(
                out=cbs[s1_stop], in_=Cf[:, 0].rearrange("p a b -> p (a b)")), reads=["Cf0"])

        ms("pool", Cf[:, 0].rearrange("p a b -> p (a b)"), 0.0, ["Cf0"])
        ms("pool", C_bf[:].rearrange("p a b -> p (a b)"), 0.0, ["C_bf"])
        cur = 0

        kmtok = hn[0:16].rearrange("p a b -> p (a b)")

        Vwmu = Sfb[0:16].rearrange("p a b c -> p (a b c)")[:, 0:516].rearrange("p (a b) -> p a b", a=4)

        def stage_B(t):
            nonlocal cur
            sb_ = t % 2
            qkTs, sos, szms = qkT[:, sb_], so[:, sb_, :], szm[:, sb_]
            qk_key, so_key, szm_key = f"qkT{sb_}", f"so{sb_}", f"szm{sb_}"
            if t % TPU == 0:
                u = t // TPU
                for g in range(4):
                    ts("dve", accm[:, g, :], prem[:, g, 0:16], cc(C_CW + (4 + g) * 5), ALU.mult, ["prem", "cst"], ["accm"])
                    for j in range(1, 5):
                        stt("dve", accm[:, g, :], prem[:, g, j:j + 16], cc(C_CW + (4 + g) * 5 + j), accm[:, g, :],
                            ALU.mult, ALU.add, ["prem", "cst", "accm"], ["accm"])
                act(kmm[:].rearrange("p a b -> p (a b)"), accm[:].rearrange("p a b -> p (a b)"), AF.Silu, ["accm"], ["kmm"])
                bt = nb()
                for g in range(4):
                    tr(ps_bf(bt)[0:16, g * 128:(g + 1) * 128], kmm[:, g, :], ["kmm"], [pk(bt)])
                cp("dve", kmtok, ps_bf(bt)[0:16, 0:512], [pk(bt)], ["hn"])
                ts("dve", Vwmu, Vwm[:], cst[0:16, C_NKEEP + u:C_NKEEP + u + 1], ALU.mult, ["Vwm", "cst"], ["Sf", "Sb"])
                nxt = cur
                for j in range(2):
                    bk = nb()
                    for hh in range(2):
                        h = 2 * j + hh
                        mm(ps_f(bk)[:, hh * 129:(hh + 1) * 129], kmtok[:, h * 128:(h + 1) * 128], Vwmu[:, h, :],
                           True, True, ["hn", "Sf", "Sb"], [pk(bk)])
                    for hh in range(2):
                        h = 2 * j + hh
                        stt("dve", Cf[:, nxt, h, :], Cf[:, cur, h, :], cc(C_KEEP + u),
                            ps_f(bk)[:, hh * 129:(hh + 1) * 129], ALU.mult, ALU.add,
                            [f"Cf{cur}", "cst", pk(bk)], [f"Cf{nxt}"])
                cur = nxt
                act(C_bf[:].rearrange("p a b -> p (a b)"), Cf[:, cur].rearrange("p a b -> p (a b)"), AF.Copy,
                    [f"Cf{cur}"], ["C_bf"])
            emit_dma("ld_cb", lambda e: e.dma_start(out=cb[:].rearrange("p a b -> p (a b)"), in_=cbs[t]),
                     reads=["cbsA", "cbsB"], writes=["cb"])
            if t % TPU == TPU - 1 and t < NT - 1:
                act(cb_bf[:].rearrange("p a b -> p (a b)"), cb[:].rearrange("p a b -> p (a b)"), AF.Copy,
                    ["cb", "cst"], ["cb_bf"], scale=cc(C_KEEPB + t // TPU))
            else:
                act(cb_bf[:].rearrange("p a b -> p (a b)"), cb[:].rearrange("p a b -> p (a b)"), AF.Copy, ["cb"], ["cb_bf"])
            bs = nb()
            for h in range(4):
                mm(ps_f(bs)[:, h * 128:(h + 1) * 128], qkTs[:, 4 + h, :], qkTs[:, h, :], True, True, [qk_key], [pk(bs)])
            sview = ps_f(bs)[:, :].rearrange("p (h j) -> p h j", h=4)
            tt("dve", Sfb[:, 0], sview, triu.unsqueeze(1).to_broadcast([128, 4, 128]), ALU.mult, [pk(bs), "cm_f"], ["Sf"])
            tt("dve", Sfb[:, 1], sview, tril.unsqueeze(1).to_broadcast([128, 4, 128]), ALU.mult, [pk(bs), "cm_f"], ["Sb"])
            for d in range(2):
                banks = []
                for j in range(2):
                    bk = nb()
                    banks.append(bk)
                    for hh in range(2):
                        h = 2 * j + hh
                        o = ps_f(bk)[:, hh * 129:(hh + 1) * 129]
                        mm(o, Sfb[:, d, h, :], V4[:, d, h, :], True, False, ["Sf" if d == 0 else "Sb", "V4u"], [pk(bk)])
                        mm(o, qkTs[:, h, :], (C_bf if d == 0 else cb_bf)[:, h, :], False, True,
                           [qk_key, "C_bf" if d == 0 else "cb_bf"], [pk(bk)])
                    dv = ps_f(bk)[:, 0:258].rearrange("p (a b) -> p a b", a=2)[:, :, 128]
                    act(da8[:, d * 4 + 2 * j:d * 4 + 2 * j + 2], dv, AF.Abs, [pk(bk)], [f"da8{d}"])
                dsl = da8[:, d * 4:d * 4 + 4]
                tt("dve", dsl, dsl, eb8[:, 0, d * 4:d * 4 + 4], ALU.max, [f"da8{d}", "eb80"], [f"da8{d}"])
                recip(dsl, dsl, [f"da8{d}"], [f"da8{d}"])
                for j in range(2):
                    bk = banks[j]
                    nv = ps_f(bk)[:, 0:258].rearrange("p (a b) -> p a b", a=2)[:, :, 0:128]
                    if d == 0:
                        tt("dve", hf[:, 2 * j:2 * j + 2, :], nv,
                           da8[:, 2 * j:2 * j + 2].unsqueeze(2).to_broadcast([128, 2, 128]),
                           ALU.mult, [pk(bk), "da80"], ["hf"])
                    else:
                        for hh in range(2):
                            h = 2 * j + hh
                            stt("dve", hf[:, h, :], nv[:, hh, :], da8[:, 4 + h:5 + h], hf[:, h, :], ALU.mult, ALU.add,
                                [pk(bk), "da81", "hf"], ["hf"])
            hfv = hf[:].rearrange("p a b -> p (a b)")
            tt("dve", hfv, hfv, sos, ALU.mult, ["hf", so_key], ["hf"])
            for h in range(4):
                act(hn[:, h, :], hf[:, h, :], AF.Square, ["hf"], ["hn", "ss4"], scale=float(128.0 ** -0.5),
                    accum=ss4[:, h:h + 1])
            act(r4[:], ss4[:], AF.Ln, ["ss4"], ["r4"], bias=EPS)
            act(r4[:], r4[:], AF.Exp, ["r4"], ["r4"], scale=-0.5)
            tt("dve", hn[:], hf[:], r4[:].unsqueeze(2).to_broadcast([128, 4, 128]), ALU.mult, ["hf", "r4"], ["hn"])
            bt = nb()
            for g in range(4):
                tr(ps_bf(bt)[:, g * 128:(g + 1) * 128], hn[:, g, :], ["hn"], [pk(bt)])
            tt("dve", ymT[:, t % QR].rearrange("p a b -> p (a b)"), ps_bf(bt)[:, 0:512],
               szms.rearrange("p a b -> p (a b)"), ALU.mult, [pk(bt), szm_key], [f"ymT{t % QR}"])
            cur = state_update(cur, 2, 0)
            act(C_bf[:].rearrange("p a b -> p (a b)"), Cf[:, cur].rearrange("p a b -> p (a b)"), AF.Copy,
                [f"Cf{cur}"], ["C_bf"])

        def stage_C(tq):
            sq = tq % QR
            bms = [nb(), nb()]
            for h in range(8):
                g, par = h // 2, h % 2
                mm(ps_f(bms[h // 4])[0:16, (h % 4) * 128:(h % 4 + 1) * 128], kmT[:, g, :],
                   qnT[:, sq, par, g, :], True, True, ["kmT", f"qnT{sq}"], [pk(bms[h // 4])])
            for j in range(2):
                act(Em[:, 4 * j:4 * j + 4, :].rearrange("p a b -> p (a b)"), ps_f(bms[j])[0:16, :], AF.Exp,
                    [pk(bms[j])], ["Em"])
            bpv = [6, 7]
            started = [False, False]

            def pv(h, lhsT, rhs, reads, last):
                j = h // 4
                st = not started[j]
                started[j] = True
                mm(ps_f(bpv[j])[:, (h % 4) * 65:(h % 4 + 1) * 65], lhsT, rhs, st, last,
                   reads, [pk(bpv[j])], skip=True)

            tiles = PLAN[tq]
            for h in range(8):
                pv(h, Em[0:16, h, :], va_m[0:16, h, :], ["Em", "va_m"], False)
            for idx, (a, halves) in enumerate(tiles):
                sk = a % KR
                es_ = idx % 2
                Ebs, Pbs = Eb[:, es_], Pb[:, 0]
                ek = [f"Eb{es_}0", f"Eb{es_}1"]
                pkey = "Pb"
                for j in range(2):
                    bq = nb()
                    for hh in range(4):
                        h = 4 * j + hh
                        g, par = h // 2, h % 2
                        mm(ps_f(bq)[:, hh * 128:(hh + 1) * 128], knT[:, sk, g, :],
                           qnT[:, sq, par, g, :], True, True, [f"knT{sk}", f"qnT{sq}"], [pk(bq)])
                    act(Ebs[:, 4 * j:4 * j + 4, :].rearrange("p a b -> p (a b)"), ps_f(bq)[:, :], AF.Exp,
                        [pk(bq)], [ek[j]])
                i0 = 7 - (2 * a - 2 * tq)
                both_full = all(hv is not None and hv[0] == "s" and hv[1] == 0 and hv[2] == 128 for hv in halves)
                if both_full:
                    tt("dve", Pbs.rearrange("p h (r q) -> p h r q", r=2),
                       Ebs.rearrange("p h (r q) -> p h r q", r=2), Ftab[:, :, i0:i0 + 2, :], ALU.mult,
                       ek + ["Ftab"], [pkey])
                else:
                    for hr in (0, 1):
                        hv = halves[hr]
                        dstp = Pbs[:, :, hr * 64:(hr + 1) * 64]
                        srcp = Ebs[:, :, hr * 64:(hr + 1) * 64]
                        if hv is None:
                            ts("dve", dstp, srcp, 0.0, ALU.mult, ek, [pkey])
                        elif hv[0] == "s" and hv[1] == 0 and hv[2] == 128:
                            tt("dve", dstp, srcp, Ftab[:, :, i0 + hr, :], ALU.mult, ek + ["Ftab"], [pkey])
                        else:
                            if hv[0] == "s":
                                mcol = C_HLO if hv[1] == 0 else C_HHI
                            else:
                                mcol = C_CM + hv[1]
                            stt("dve", dstp, srcp, cc(mcol), Ftab[:, :, i0 + hr, :], ALU.mult, ALU.mult,
                                ek + ["Ftab", "cst"], [pkey])
                for h in range(8):
                    pv(h, Pbs[:, h, :], va[:, sk, h, :], [pkey, f"va{sk}"], idx == len(tiles) - 1)
            for j in range(2):
                v3 = ps_f(bpv[j])[:, 0:260].rearrange("p (h d) -> p h d", h=4)
                recip(rd8[:, 4 * j:4 * j + 4], v3[:, :, 64], [pk(bpv[j])], [f"rd8{j}"])
                tt("dve", ya[:, 4 * j:4 * j + 4, :], v3[:, :, 0:64],
                   rd8[:, 4 * j:4 * j + 4].unsqueeze(2).to_broadcast([128, 4, 64]), ALU.mult,
                   [pk(bpv[j]), f"rd8{j}"], ["ya"])
            bt = nb()
            yav = ya[:].rearrange("p h d -> p (h d)")
            for g in range(4):
                tr(ps_bf(bt)[:, g * 128:(g + 1) * 128], yav[:, g * 128:(g + 1) * 128], ["ya"], [pk(bt)])
            tt("dve", yaT[:].rearrange("p a b -> p (a b)"), ps_bf(bt)[:, 0:512],
               sza[:, tq % SZR].rearrange("p a b -> p (a b)"), ALU.mult, [pk(bt), f"sza{tq % SZR}"], ["yaT"])
            emit_dma("ld_xr", lambda e: e.dma_start(out=xr[:], in_=xs[tq * 128:(tq + 1) * 128, :]), writes=["xr0", "xr1"])
            for n in range(2):
                bo = nb()
                for kt in range(8):
                    lhsT = ymT[:, sq, kt, :] if kt < 4 else yaT[:, kt - 4, :]
                    mm(ps_f(bo)[:, :], lhsT, wo_bf[:, kt, n * 512:(n + 1) * 512], kt == 0, kt == 7,
                       [f"ymT{sq}", "yaT", "wo_bf"], [pk(bo)])
                tt("dve", xr[:, n * 512:(n + 1) * 512], ps_f(bo)[:, :], xr[:, n * 512:(n + 1) * 512], ALU.add,
                   [pk(bo), f"xr{n}"], [f"xr{n}"])
            emit_dma("st_y", lambda e: e.dma_start(out=y[tq * 128:(tq + 1) * 128, :], in_=xr[:]), reads=["xr0", "xr1"])

        S.buf["cbsA"] = [("d", "st_cb0", S.dma_counts.get("st_cb0", 0)), []]
        S.buf["cbsB"] = [("d", "st_cb1", S.dma_counts.get("st_cb1", 0)), []]

        s2_end = s2_start + s2_tiles
        if phase >= 3:
            junk2 = []

            def a_call(i, keep):
                ctx = {}
                for name in ("main", "norm", "tail_k", "tail_g"):
                    ctx[name] = keep.get(name, (junk2, "junk"))
                stage_A("full", xs_=i % 2, t=i, slot_q=i % QR, slot_k=i % KR, va_dst=va[:, i % KR],
                        va_key=f"va{i % KR}", ctx=ctx, ab=i % 2, slot_z=i % SZR)
                del junk2[:]

            setctx(None, "A")
            stage_X(xs[s2_start * 128:(s2_start + 1) * 128, :], xs_=s2_start % 2, t=s2_start)
            for i in range(s2_start, s2_end + LAG):
                L1, L2, L3 = [], [], []
                if i < s2_end:
                    Lk, Lg = [], []
                    a_call(i, {"main": (None, "A")})
                    a_call(i, {"norm": (L2, "L2"), "tail_k": (Lk, "L1"), "tail_g": (Lg, "L1")})
                    L1.extend(zip_lists(Lk, Lg))
                    setctx(L1, "L1")
                    stage_B(i)
                if i - LAG >= s2_start and phase >= 4:
                    setctx(L2, "L2")
                    stage_C(i - LAG)
                if i + 1 < s2_end:
                    setctx(L3, "L3")
                    stage_X(xs[(i + 1) * 128:(i + 2) * 128, :], xs_=(i + 1) % 2, t=i + 1)
                setctx(None, "A")
                merge_emit([L1, L2, L3])

        S.finalize()
        with nc.Block() as block:
            @block.sync
            def _(e):
                S.run("sp", e, sems, final_dma_keys=[k for k in dma_keys if k.startswith("st_") and S.dma_counts.get(k)])

            @block.tensor
            def _(e):
                S.run("pe", e, sems)

            @block.scalar
            def _(e):
                S.run("act", e, sems)

            @block.vector
            def _(e):
                S.run("dve", e, sems)

            @block.gpsimd
            def _(e):
                S.run("pool", e, sems)
    return nc


def _core_streams(x_prompt, x_sample, c):
    if c < 2:
        xs = np.concatenate([x_sample[c], x_prompt[c]], axis=0)
        seq_starts = [0, 8192]
        seq_ends = [8192, 10240]
    else:
        i0 = 2 + 5 * (c - 2)
        xs = x_prompt[i0:i0 + 5].reshape(5 * 2048, DM)
        seq_starts = [2048 * u for u in range(5)]
        seq_ends = [2048 * (u + 1) for u in range(5)]
    return np.ascontiguousarray(xs), set(seq_starts), set(seq_ends)


def _host_layout(inputs):
    x_prompt = np.asarray(inputs["x_prompt"], np.float32)
    x_sample = np.asarray(inputs["x_sample"], np.float32)
    meta = np.asarray(inputs["meta_tokens"], np.float32)
    wi = np.ascontiguousarray(np.asarray(inputs["w_in"], np.float32)[0])
    wo = np.ascontiguousarray(np.asarray(inputs["w_out"], np.float32)[0])
    norm_g = np.asarray(inputs["norm_g"], np.float32)[0]
    b_gate = np.asarray(inputs["b_gate"], np.float32)[0]
    conv_w = np.asarray(inputs["conv_w"], np.float32)[0]
    mg = np.asarray(inputs["mlstm_norm_g"], np.float32)[0]
    qg = np.asarray(inputs["q_norm_g"], np.float32)[0]
    kg = np.asarray(inputs["k_norm_g"], np.float32)[0]
    rpb = np.asarray(inputs["rpb"], np.float32)[0]

    p = np.arange(128)
    rr, kc = p // 64, p % 64
    i = np.arange(16)
    qc = np.arange(64)
    dr = (7 - i)[None, :] + rr[:, None]
    colidx = kc[:, None] - qc[None, :] + 15
    cs = np.clip(qc - 8, 0, 48)
    allowed = (kc[:, None] >= cs[None, :]) & (kc[:, None] < cs[None, :] + 16)
    valid = (dr >= -7) & (dr <= 7)
    rowi = np.clip(dr + 7, 0, 14)
    coli = np.clip(colidx, 0, 30)
    fraw = rpb[:, rowi[:, :, None], coli[:, None, :]]
    fraw = np.ascontiguousarray(np.transpose(fraw, (1, 0, 2, 3))).reshape(128, 8 * 16 * 64).astype(np.float32)
    fm = (valid[:, :, None] & allowed[:, None, :]).astype(np.float32)
    fmsk = np.ascontiguousarray(np.broadcast_to(fm[:, None], (128, 8, 16, 64))).reshape(128, 8 * 16 * 64)

    s = np.arange(128)[:, None]
    j = np.arange(128)[None, :]
    cmat = np.stack([
        (s <= j), (s >= j), -(s <= j).astype(np.float32), -(s >= j).astype(np.float32),
        -np.ones((128, 128)), -((s > j) & (s <= 15)).astype(np.float32)], axis=1).astype(np.float32)
    cmat = np.ascontiguousarray(cmat).reshape(128, 6 * 128)

    in_maps = []
    for c in range(NCORES):
        xs, starts, ends = _core_streams(x_prompt, x_sample, c)
        xh = np.zeros((NPRE * 128, DM), np.float32)
        for t in range(NT):
            t0 = t * 128
            if t0 in starts:
                xh[4 * t:4 * t + 2] = meta[14:16]
            else:
                xh[4 * t:4 * t + 2] = xs[t0 - 2:t0]
            if t0 + 128 not in ends:
                xh[4 * t + 2:4 * t + 4] = xs[t0 + 128:t0 + 130]
        xh[384:400] = meta
        linked = c < 2
        cst = np.zeros((128, NCST), np.float32)
        cst[:, C_BG:C_BG + 16] = b_gate[None, :]
        for g in range(8):
            for jj in range(5):
                cst[:, C_CW + g * 5 + jj] = conv_w[jj, g * 128:(g + 1) * 128]
        for kc_ in range(8):
            cst[:, C_NG + kc_] = norm_g[kc_ * 128:(kc_ + 1) * 128]
        for g in range(4):
            cst[:, C_MG + g] = mg[g * 128:(g + 1) * 128]
        cst[:, C_QG] = qg[p % 64]
        cst[:, C_KG] = kg[p % 64]
        keep = [0, 1, 1, 1, 0] if linked else [0, 0, 0, 0, 0]
        for u in range(5):
            cst[:, C_KEEP + u] = keep[u]
            cst[:, C_NKEEP + u] = 1 - keep[u]
        for u in range(4):
            cst[:, C_KEEPB + u] = keep[u + 1]
        cst[0:64, C_HLO] = 1.0
        cst[64:128, C_HHI] = 1.0
        for ci, (R, a) in enumerate(CMS):
            w = _win_lnk(R) if linked else _win_unl(R)
            cst[:, C_CM + ci] = np.array([(2 * a + (pp // 64)) in w for pp in range(128)], np.float32)
        in_maps.append({"xs": xs, "xh": xh, "wi": wi, "wo": wo, "cst": cst, "fraw": fraw, "fmsk": fmsk, "cmat": cmat})
    return in_maps


_NC_CACHE = {}


def kernel(**inputs):
    in_maps = _host_layout(inputs)
    if "nc" not in _NC_CACHE:
        _NC_CACHE["nc"] = build_program()
    nc = _NC_CACHE["nc"]
    res = run_bass_kernel_spmd(nc, in_maps, core_ids=list(range(NCORES)))
    ys = [np.asarray(r["y"], np.float32) for r in res.results]
    y_prompt = np.empty((32, 2048, DM), np.float32)
    y_sample = np.empty((2, 8192, DM), np.float32)
    for c in range(NCORES):
        if c < 2:
            y_sample[c] = ys[c][:8192]
            y_prompt[c] = ys[c][8192:]
        else:
            i0 = 2 + 5 * (c - 2)
            y_prompt[i0:i0 + 5] = ys[c].reshape(5, 2048, DM)
    return (y_prompt, y_sample)
```
